# Optimizing a Trainium2 kernel written in Bass

```python
import math
import jax, jax.numpy as jnp
from jax import lax
import numpy as np

D_MODEL = 1024
BATCH = 2
SEQ = 8192
DEPTH = 1

RWKV_HEAD = 64
D_RWKV = D_MODEL // 2
N_RWKV_HEADS = D_RWKV // RWKV_HEAD
DECAY_LORA = 64
AAA_LORA = 64
GATE_LORA = 160
LN_X_EPS = 64e-5
D_CONV = D_MODEL // 2
CONV_WIDTH = 3
N_MEM = 256
N_XHEADS = 4
XHEAD_DIM = D_MODEL // N_XHEADS
D_FF = 4 * D_MODEL
RMS_EPS = 1e-6

SPLITS = (D_RWKV, D_RWKV, D_RWKV, DECAY_LORA, AAA_LORA, GATE_LORA,
          D_CONV, D_CONV, D_CONV,
          D_MODEL, D_MODEL)
D_IN = sum(SPLITS)
RWKV_COLS = 3 * D_RWKV + DECAY_LORA + AAA_LORA + GATE_LORA

kernel_name = "hybrid_rwkv7_shortconv_xattn_block"


def rms_norm(x, g):
    xf = x.astype(jnp.float32)
    y = xf * lax.rsqrt(jnp.mean(xf * xf, axis=-1, keepdims=True) + RMS_EPS)
    return (y * g.astype(jnp.float32)).astype(x.dtype)


def token_shift(p):
    return jnp.pad(p[:, :-1], ((0, 0), (1, 0), (0, 0)))


def wkv7_scan(r, w, k, v, a, b):
    bsz, _, h, n = r.shape

    def step(state, inp):
        r_t, w_t, k_t, v_t, a_t, b_t = inp
        sa = jnp.einsum("bhvk,bhk->bhv", state, a_t)
        state = (state * w_t[:, :, None, :]
                 + sa[..., None] * b_t[:, :, None, :]
                 + v_t[..., None] * k_t[:, :, None, :])
        return state, jnp.einsum("bhvk,bhk->bhv", state, r_t)

    xs = tuple(jnp.moveaxis(t, 1, 0) for t in (r, w, k, v, a, b))
    s0 = jnp.zeros((bsz, h, n, n), jnp.float32)
    _, ys = lax.scan(step, s0, xs)
    return jnp.moveaxis(ys, 0, 1)


def rwkv7_branch(r, k, v, wd, ad, gd, w0, w_lora_w, a0, w_lora_a, w_lora_g,
                 k_k, k_a, r_k, ln_x_w, ln_x_b):
    bsz, s, _ = r.shape
    f32 = jnp.float32
    log_w = -jax.nn.softplus(-(w0 + jnp.tanh(wd) @ w_lora_w)) - 0.5
    decay = jnp.exp(-jnp.exp(log_w.astype(f32)))
    a = jax.nn.sigmoid(a0 + ad @ w_lora_a)
    g = jax.nn.sigmoid(gd) @ w_lora_g

    def heads(t):
        return t.reshape(bsz, s, N_RWKV_HEADS, RWKV_HEAD).astype(f32)

    kk = heads(k * k_k)
    kk = kk / jnp.maximum(jnp.sqrt(jnp.sum(kk * kk, axis=-1, keepdims=True)), 1e-12)
    k = k * (1.0 + (a - 1.0) * k_a)
    rh, kh, vh, ah, wh = heads(r), heads(k), heads(v), heads(a), heads(decay)
    y = wkv7_scan(rh, wh, kh, vh, -kk, kk * ah)
    mu = jnp.mean(y, axis=-1, keepdims=True)
    var = jnp.mean(jnp.square(y - mu), axis=-1, keepdims=True)
    y = (y - mu) * lax.rsqrt(var + LN_X_EPS)
    y = y.reshape(bsz, s, D_RWKV) * ln_x_w.astype(f32) + ln_x_b.astype(f32)
    bonus = jnp.sum(rh * kh * r_k.astype(f32), axis=-1, keepdims=True) * vh
    y = y + bonus.reshape(bsz, s, D_RWKV)
    return y.astype(r.dtype) * g


def short_conv_branch(bg, cg, xc, conv_w):
    u = cg * xc
    y = lax.conv_general_dilated(
        u, conv_w.astype(u.dtype), window_strides=(1,),
        padding=[(CONV_WIDTH - 1, 0)],
        dimension_numbers=("NWC", "WIO", "NWC"),
        feature_group_count=D_CONV)
    return bg * y


def memory_cross_attention(h, mem_n, w_q, w_kv, w_xo):
    bsz, s, _ = h.shape
    n_mem = mem_n.shape[1]
    q = (h @ w_q).reshape(bsz, s, N_XHEADS, XHEAD_DIM)
    k, v = jnp.split(mem_n @ w_kv, 2, axis=-1)
    k = k.reshape(bsz, n_mem, N_XHEADS, XHEAD_DIM)
    v = v.reshape(bsz, n_mem, N_XHEADS, XHEAD_DIM)
    scores = jnp.einsum("bshd,bmhd->bhsm", q, k).astype(jnp.float32) / math.sqrt(XHEAD_DIM)
    probs = jax.nn.softmax(scores, axis=-1).astype(v.dtype)
    o = jnp.einsum("bhsm,bmhd->bshd", probs, v).reshape(bsz, s, D_MODEL)
    return o @ w_xo


def setup_inputs(seed: int = 0) -> dict:
    key = jax.random.key(seed)
    ks = jax.random.split(key, 32)
    it = iter(range(32))
    f32 = jnp.float32
    L = DEPTH

    def nrm(shape, scale):
        return scale * jax.random.normal(ks[next(it)], shape, f32)

    def gain(shape):
        return 1.0 + 0.05 * jax.random.normal(ks[next(it)], shape, f32)

    return {
        "x": nrm((BATCH, SEQ, D_MODEL), 1.0),
        "mem": nrm((BATCH, N_MEM, D_MODEL), 1.0),
        "norm_mix": gain((L, D_MODEL)),
        "w_in": nrm((L, D_MODEL, D_IN), D_MODEL ** -0.5),
        "b_gate": nrm((L, 2 * D_MODEL), 0.01),
        "mu_shift": jax.random.uniform(ks[next(it)], (L, RWKV_COLS), f32),
        "w0": jax.random.uniform(ks[next(it)], (L, D_RWKV), f32, -6.0, -1.0),
        "w_lora_w": nrm((L, DECAY_LORA, D_RWKV), 0.1 * DECAY_LORA ** -0.5),
        "a0": nrm((L, D_RWKV), 0.1),
        "w_lora_a": nrm((L, AAA_LORA, D_RWKV), AAA_LORA ** -0.5),
        "w_lora_g": nrm((L, GATE_LORA, D_RWKV), GATE_LORA ** -0.5),
        "k_k": 0.85 + nrm((L, D_RWKV), 0.05),
        "k_a": gain((L, D_RWKV)),
        "r_k": nrm((L, N_RWKV_HEADS, RWKV_HEAD), 0.1),
        "ln_x_w": gain((L, D_RWKV)),
        "ln_x_b": nrm((L, D_RWKV), 0.01),
        "conv_w": nrm((L, CONV_WIDTH, 1, D_CONV), CONV_WIDTH ** -0.5),
        "w_proj_a": nrm((L, D_RWKV, D_MODEL), D_RWKV ** -0.5),
        "w_proj_b": nrm((L, D_CONV, D_MODEL), D_CONV ** -0.5),
        "w_out_mix": nrm((L, D_MODEL, D_MODEL), D_MODEL ** -0.5),
        "norm_xattn": gain((L, D_MODEL)),
        "norm_mem": gain((L, D_MODEL)),
        "w_q": nrm((L, D_MODEL, D_MODEL), D_MODEL ** -0.5),
        "w_kv": nrm((L, D_MODEL, 2 * D_MODEL), D_MODEL ** -0.5),
        "w_xo": nrm((L, D_MODEL, D_MODEL), D_MODEL ** -0.5),
        "norm_mlp": gain((L, D_MODEL)),
        "w_up": nrm((L, D_MODEL, D_FF), D_MODEL ** -0.5),
        "w_down": nrm((L, D_FF, D_MODEL), D_FF ** -0.5),
        "norm_final": gain((D_MODEL,)),
    }


def reference(x, mem, norm_mix, w_in, b_gate, mu_shift, w0, w_lora_w, a0, w_lora_a,
              w_lora_g, k_k, k_a, r_k, ln_x_w, ln_x_b, conv_w, w_proj_a, w_proj_b,
              w_out_mix, norm_xattn, norm_mem, w_q, w_kv, w_xo, norm_mlp, w_up,
              w_down, norm_final):
    offsets = np.cumsum(SPLITS)[:-1].tolist()
    for i in range(DEPTH):
        h = rms_norm(x, norm_mix[i])
        p = h @ w_in[i]
        p_rw = p[..., :RWKV_COLS]
        p_rw = p_rw + (token_shift(p_rw) - p_rw) * mu_shift[i]
        p = jnp.concatenate([p_rw, p[..., RWKV_COLS:]], axis=-1)
        r, k, v, wd, ad, gd, cb, cc, cx, ga, gb = jnp.split(p, offsets, axis=-1)
        out_a = rwkv7_branch(r, k, v, wd, ad, gd, w0[i], w_lora_w[i], a0[i], w_lora_a[i],
                             w_lora_g[i], k_k[i], k_a[i], r_k[i], ln_x_w[i], ln_x_b[i])
        out_b = short_conv_branch(cb, cc, cx, conv_w[i])
        bg_a, bg_b = jnp.split(b_gate[i], 2)
        merged = (jax.nn.sigmoid(ga + bg_a) * (out_a @ w_proj_a[i])
                  + jax.nn.sigmoid(gb + bg_b) * (out_b @ w_proj_b[i]))
        x = x + merged @ w_out_mix[i]
        mem_n = rms_norm(mem, norm_mem[i])
        x = x + memory_cross_attention(rms_norm(x, norm_xattn[i]), mem_n,
                                       w_q[i], w_kv[i], w_xo[i])
        hm = rms_norm(x, norm_mlp[i])
        x = x + jnp.square(jax.nn.relu(hm @ w_up[i])) @ w_down[i]
    return rms_norm(x, norm_final)
```

```python
import numpy as np
import concourse.bass as bass
import concourse.mybir as mybir
from concourse.bass_utils import run_bass_kernel_spmd

F32 = mybir.dt.float32
BF16 = mybir.dt.bfloat16
U32 = mybir.dt.uint32
ALU = mybir.AluOpType
ACT = mybir.ActivationFunctionType

ENGS = ("pe", "act", "dve", "pool", "sp")

D = 1024
KC = 8
TB = 512
RMS_EPS = 1e-6
LN_X_EPS = 64e-5
N_MEM = 256
DFF = 4096

C_GMIX, C_GXA, C_GMLP, C_GFIN, C_GMEM = 0, 8, 16, 24, 32
C_MU1 = 40
C_MUGD = 44
C_W0, C_A0, C_KK, C_KA, C_RK, C_LNW, C_LNB = 46, 47, 48, 49, 50, 51, 52
C_BGA, C_BGB = 53, 61
C_CONV = 69
NCV = 81


class Buf:
    __slots__ = ("name", "w", "r")

    def __init__(self, name=""):
        self.name = name
        self.w = None
        self.r = []


class Sched:
    def __init__(self, nc):
        self.nc = nc
        self.ops = {e: [] for e in ENGS}
        self.dma_sems = []
        self.ctx = []
        self.final_events = []
        self.all_bufs = []
        self.seq = 0
        self.rec_of = {}
        self.dma_evt = {}
        self.segs = {e: [0] for e in ENGS}

    def sb(self, name, shape, dt):
        self.uid = getattr(self, "uid", 0) + 1
        name = "%s_%d" % (name, self.uid)
        cm = self.nc.sbuf_tensor(name, list(shape), dt)
        t = cm.__enter__()
        self.ctx.append(cm)
        return t

    def ps(self, name, shape, dt=F32):
        self.uid = getattr(self, "uid", 0) + 1
        name = "%s_%d" % (name, self.uid)
        cm = self.nc.psum_tensor(name, list(shape), dt)
        t = cm.__enter__()
        self.ctx.append(cm)
        return t

    def mark(self):
        return len(self.ctx)

    def release(self, mark):
        while len(self.ctx) > mark:
            self.ctx.pop().__exit__(None, None, None)

    def buf(self, name=""):
        b = Buf(name)
        self.all_bufs.append(b)
        return b

    def _deps(self, reads, writes):
        waits = []
        for b in reads:
            if b.w is not None:
                waits.append(b.w)
        for b in writes:
            if b.w is not None:
                waits.append(b.w)
            waits.extend(b.r)
        return waits

    def op(self, eng, fn, reads=(), writes=(), cost=0.5):
        waits = self._deps(reads, writes)
        idx = len(self.ops[eng])
        self.seq += 1
        rec = {"fn": fn, "waits": waits, "sig": False, "dma": None, "seq": self.seq, "cost": cost, "eng": eng}
        self.ops[eng].append(rec)
        ev = ("e", eng, idx)
        self.rec_of[ev] = rec
        for b in reads:
            b.r.append(ev)
        for b in writes:
            b.w = ev
            b.r = []
        return ev

    def dma(self, eng, fn, reads=(), writes=(), sem=None, inc=16, cost=3.0):
        waits = self._deps(reads, writes)
        if sem is None:
            self.dma_sems.append({"h": None, "total": 0})
            sem = len(self.dma_sems) - 1
        s = self.dma_sems[sem]
        s["total"] += inc
        self.seq += 1
        rec = {"fn": fn, "waits": waits, "sig": False, "dma": sem, "inc": inc, "seq": self.seq, "cost": cost,
               "eng": eng}
        self.ops[eng].append(rec)
        ev = ("d", sem, s["total"])
        self.dma_evt[ev] = rec
        for b in reads:
            b.r.append(ev)
        for b in writes:
            b.w = ev
            b.r = []
        return ev

    def new_sem(self):
        self.dma_sems.append({"h": None, "total": 0})
        return len(self.dma_sems) - 1

    def barrier(self, keep=()):
        keep_ids = set(id(b) for b in keep)
        keep_evs = set(b.w for b in keep if b.w is not None)
        evs = []
        for e in ENGS:
            n = len(self.ops[e])
            for i in range(n - 1, -1, -1):
                if self.ops[e][i]["dma"] is None and self.ops[e][i]["fn"] is not None:
                    evs.append(("e", e, i))
                    break
        keep_sems = set(ev[1] for ev in keep_evs if ev[0] == "d")
        for i, s in enumerate(self.dma_sems):
            if s["total"] > 0 and i not in keep_sems:
                evs.append(("d", i, s["total"]))
        for e in ENGS:
            self.seq += 1
            self.ops[e].append({"fn": None, "waits": list(evs), "sig": False, "dma": None, "seq": self.seq,
                                "cost": 0.0, "eng": e})
            self.segs[e].append(len(self.ops[e]))
        for b in self.all_bufs:
            if id(b) in keep_ids:
                continue
            b.w = None
            b.r = []

    LAT_X = 1.4
    LAT_S = 0.6

    def schedule(self):
        import heapq
        ops = self.ops
        nseg = len(self.segs[ENGS[0]])
        new_ops = {e: [] for e in ENGS}
        for si in range(nseg):
            seg = []
            for e in ENGS:
                lo = self.segs[e][si]
                hi = self.segs[e][si + 1] if si + 1 < nseg else len(ops[e])
                seg.extend(ops[e][lo:hi])
            inseg = set(id(r) for r in seg)
            barrier_recs = [r for r in seg if r["fn"] is None]
            body = [r for r in seg if r["fn"] is not None]
            succ = {id(r): [] for r in body}
            ndep = {}
            for r in body:
                deps = []
                for w in r["waits"]:
                    p = self.rec_of.get(w) if w[0] == "e" else self.dma_evt.get(w)
                    if p is not None and id(p) in inseg and p["fn"] is not None:
                        deps.append(p)
                r["_deps"] = deps
                ndep[id(r)] = len(deps)
                for p in deps:
                    succ[id(p)].append(r)
            prio = {}
            if getattr(self, "cp_prio", True):
                for r in sorted(body, key=lambda r_: -r_["seq"]):
                    best_ = 0.0
                    for q in succ[id(r)]:
                        lat = self.LAT_S if (q["eng"] == r["eng"] and r["dma"] is None) else self.LAT_X
                        if r["eng"] == "pe" and q["eng"] == "pe":
                            lat = 0.0
                        v_ = lat + prio[id(q)]
                        if v_ > best_:
                            best_ = v_
                    prio[id(r)] = r["cost"] + best_
                for r in body:
                    r["_key"] = -prio[id(r)]
            else:
                for r in body:
                    r["_key"] = r["seq"]
            free = {e: 0.0 for e in ENGS}
            pend = {e: [] for e in ENGS}
            avail = {e: [] for e in ENGS}
            fin = {}
            for r in body:
                rt0 = 0.0
                for w in r["waits"]:
                    if w[0] == "d":
                        p = self.dma_evt.get(w)
                        if p is not None and id(p) not in inseg and p.get("inc") == 1:
                            rt0 = 90.0
                r["_rt0"] = rt0
                if ndep[id(r)] == 0:
                    heapq.heappush(pend[r["eng"]], (rt0, r["seq"], id(r), r))
            order = {e: [] for e in ENGS}
            left = len(body)
            while left:
                best = None
                for e in ENGS:
                    pe_, av = pend[e], avail[e]
                    while pe_ and pe_[0][0] <= free[e]:
                        t_, sq, _i, r = heapq.heappop(pe_)
                        heapq.heappush(av, (r["_key"], _i, r))
                    if av:
                        cand = (free[e], av[0][0], e, True)
                    elif pe_:
                        cand = (pe_[0][0], pe_[0][1], e, False)
                    else:
                        continue
                    if best is None or cand < best:
                        best = cand
                start, _sq, e, from_av = best
                if from_av:
                    _sq, _i, r = heapq.heappop(avail[e])
                else:
                    _t, _sq, _i, r = heapq.heappop(pend[e])
                order[e].append(r)
                r["_start"] = start
                r["_engprev"] = order[e][-2] if len(order[e]) > 1 else None
                left -= 1
                if r["dma"] is not None:
                    issue = 1.0 if e == "pool" else 0.08
                    free[e] = start + issue
                    done = start + r["cost"]
                else:
                    free[e] = start + r["cost"]
                    done = free[e]
                fin[id(r)] = done
                for q in succ[id(r)]:
                    ndep[id(q)] -= 1
                    if ndep[id(q)] == 0:
                        rt = q["_rt0"]
                        for p in q["_deps"]:
                            lat = self.LAT_S if (p["eng"] == q["eng"] and p["dma"] is None) else self.LAT_X
                            if p["eng"] == "pe" and q["eng"] == "pe":
                                lat = 0.0
                            rt = max(rt, fin[id(p)] + lat)
                        q["_ready"] = rt
                        heapq.heappush(pend[q["eng"]], (rt, q["seq"], id(q), q))
            self.sim_span = getattr(self, "sim_span", []) + [max(fin.values()) if fin else 0.0]
            self.sim_fin = getattr(self, "sim_fin", {})
            self.sim_fin.update(fin)
            for e in ENGS:
                new_ops[e].extend(order[e])
                new_ops[e].extend([r for r in barrier_recs if r["eng"] == e])
        for e in ENGS:
            assert len(new_ops[e]) == len(ops[e])
            ops[e][:] = new_ops[e]

    def finish(self, final_wait_eng="sp"):
        nc = self.nc
        ops = self.ops
        rec_of = self.rec_of
        if getattr(self, "do_schedule", True):
            self.schedule()
        for w in self.final_events:
            if w[0] == "e":
                rec_of[w]["sig"] = True
        for e in ENGS:
            for rec in ops[e]:
                for w in rec["waits"]:
                    if w[0] == "e":
                        rec_of[w]["sig"] = True
        for e in ENGS:
            c = 0
            for rec in ops[e]:
                if rec["sig"]:
                    c += 1
                rec["count"] = c
        sem_cms = []
        esem = {}
        for e in ENGS:
            cm = nc.semaphore("s_" + e)
            esem[e] = cm.__enter__()
            sem_cms.append(cm)
        for i, s in enumerate(self.dma_sems):
            cm = nc.semaphore("d_%d" % i)
            s["h"] = cm.__enter__()
            sem_cms.append(cm)
        dma_sems = self.dma_sems
        final_events = self.final_events
        nwaits = {e: 0 for e in ENGS}

        def emit(e, eng):
            known = {}
            for rec in ops[e]:
                need = {}
                for w in rec["waits"]:
                    if w[0] == "e":
                        if w[1] == e and e == "pe":
                            continue
                        key = ("e", w[1])
                        val = rec_of[w]["count"]
                    else:
                        key = ("d", w[1])
                        val = w[2]
                    if known.get(key, 0) >= val:
                        continue
                    if need.get(key, 0) < val:
                        need[key] = val
                for key, val in need.items():
                    h = esem[key[1]] if key[0] == "e" else dma_sems[key[1]]["h"]
                    eng.wait_ge(h, val)
                    known[key] = val
                    nwaits[e] += 1
                if rec["fn"] is None:
                    continue
                ins = rec["fn"](eng)
                if rec["dma"] is not None:
                    ins.then_inc(dma_sems[rec["dma"]]["h"], rec.get("inc", 16))
                elif rec["sig"]:
                    ins.then_inc(esem[e], 1)
            if e == final_wait_eng:
                for w in final_events:
                    if w[0] == "e":
                        eng.wait_ge(esem[w[1]], rec_of[w]["count"])
                    else:
                        eng.wait_ge(dma_sems[w[1]]["h"], w[2])

        with nc.Block() as block:
            @block.sync
            def _(eng):
                emit("sp", eng)

            @block.tensor
            def _(eng):
                emit("pe", eng)

            @block.scalar
            def _(eng):
                emit("act", eng)

            @block.vector
            def _(eng):
                emit("dve", eng)

            @block.gpsimd
            def _(eng):
                emit("pool", eng)
        self.stats = {e: (len(ops[e]), nwaits[e]) for e in ENGS}
        for cm in reversed(sem_cms):
            cm.__exit__(None, None, None)
        self.release(0)


class T:
    def __init__(self, S, name, shape, dt, nb=1, psum=False):
        self.t = S.ps(name, shape, dt) if psum else S.sb(name, shape, dt)
        self.bs = [S.buf(name + str(i)) for i in range(nb)]
        self.b = self.bs[0]


class K:
    def __init__(self, S):
        self.S = S

    @staticmethod
    def _c(eng, ap):
        n = 1
        for d in ap.shape[1:]:
            n *= d
        if eng == "pe":
            return 0.05 + 0.00055 * n
        if eng == "act":
            return 0.2 + 0.0009 * n
        if eng == "dve":
            return 0.12 + 0.0011 * n
        return 0.15 + 0.002 * n

    def tt(self, eng, out, in0, in1, op, R, W):
        return self.S.op(eng, lambda e: e.tensor_tensor(out, in0, in1, op), R, W, cost=self._c(eng, out))

    def ts(self, eng, out, in0, s1, s2, op0, op1, R, W):
        c = self._c(eng, out) * (5.0 if eng == "pool" else 1.0)
        if s2 is None:
            return self.S.op(eng, lambda e: e.tensor_scalar(out, in0, s1, None, op0), R, W, cost=c)
        return self.S.op(eng, lambda e: e.tensor_scalar(out, in0, s1, s2, op0, op1), R, W, cost=c)

    def stt(self, out, in0, sc, in1, op0, op1, R, W):
        return self.S.op("dve", lambda e: e.scalar_tensor_tensor(out, in0, sc, in1, op0, op1), R, W,
                         cost=self._c("dve", out))

    def act(self, out, in_, func, R, W, bias=None, scale=None, eng="act"):
        kw = {}
        if bias is not None:
            kw["bias"] = bias
        if scale is not None:
            kw["scale"] = scale
        return self.S.op(eng, lambda e: e.activation(out, in_, func, **kw), R, W, cost=self._c("act", out))

    def cp(self, eng, out, in_, R, W):
        if eng == "act":
            return self.S.op("act", lambda e: e.activation(out, in_, ACT.Copy), R, W, cost=self._c("act", out))
        return self.S.op(eng, lambda e: e.tensor_copy(out, in_), R, W, cost=self._c(eng, out))

    def recip(self, out, in_, R, W):
        return self.S.op("dve", lambda e: e.reciprocal(out, in_), R, W, cost=6 * self._c("dve", out))

    def mm(self, out, lhsT, rhs, R, W, start=True, stop=True, tp=None):
        c = self._c("pe", rhs)
        if tp is None:
            return self.S.op("pe", lambda e: e.matmul(out, lhsT, rhs, start=start, stop=stop), R, W, cost=c)
        return self.S.op("pe", lambda e: e.matmul(out, lhsT, rhs, start=start, stop=stop,
                                                   tile_position=tp), R, W, cost=c)

    def tr(self, out, in_, ident, R, W):
        return self.S.op("pe", lambda e: e.transpose(out, in_, ident), R, W, cost=0.12)

    def memset(self, eng, out, val, W):
        return self.S.op(eng, lambda e: e.memset(out, val), (), W, cost=self._c(eng, out))

    def dma(self, eng, out, in_, R, W, sem=None):
        nb_ = 1
        for d in out.shape:
            nb_ *= d
        return self.S.dma(eng, lambda e: e.dma_start(out=out, in_=in_), R, W, sem=sem, cost=2.5 + nb_ * 2e-5)

    def tsem(self, t):
        if not hasattr(t, "sem"):
            t.sem = self.S.new_sem()
        return t.sem


class Banks:
    def __init__(self, S, n):
        self.banks = [T(S, "pb%d" % i, [128, 512], F32, psum=True) for i in range(n)]
        self.i = 0

    def next(self):
        n = len(self.banks)
        for k in range(n):
            b = self.banks[(self.i + k) % n]
            if b.b.w is None or b.b.w[0] != "e" or b.b.w[1] != "pe":
                self.i = (self.i + k + 1) % n
                return b
        raise RuntimeError("no free PSUM bank (a bank is held across a pipeline yield)")


def v3(ap, c):
    return ap.rearrange("p (c t) -> p c t", c=c)


def build_consts(S, Kk, cv):
    c = {}
    ones_bf = T(S, "ones_bf", [128, 128], BF16)
    Kk.memset("pool", ones_bf.t[:], 1.0, [ones_bf.b])
    c["ones_bf"] = ones_bf
    ones32 = T(S, "ones32", [128, 64], F32)
    Kk.memset("pool", ones32.t[:], 1.0, [ones32.b])
    c["ones32"] = ones32
    for name, val in (("bones", 1.0), ("bmean", 1.0 / 64.0)):
        t = T(S, name, [128, 128], BF16)
        Kk.memset("pool", t.t[:], 0.0, [t.b])
        Kk.memset("pool", t.t[0:64, 0:64], val, [t.b])
        Kk.memset("pool", t.t[64:128, 64:128], val, [t.b])
        c[name] = t
    ident = T(S, "ident64", [128, 64], BF16)
    Kk.memset("pool", ident.t[:], 1.0, [ident.b])
    for h in range(2):
        sl = ident.t[h * 64:(h + 1) * 64, :]
        S.op("pool", lambda e, sl=sl: e.affine_select(sl, sl, [[-1, 64]], ALU.is_equal, 0.0,
                                                       base=0, channel_multiplier=1),
             [ident.b], [ident.b])
    c["ident64"] = ident
    m4 = T(S, "mask4", [128, 4, 128], BF16)
    mn4 = T(S, "maskn4", [128, 4, 128], BF16)
    Kk.memset("pool", m4.t[:], 0.0, [m4.b])
    Kk.memset("pool", mn4.t[:], 0.0, [mn4.b])
    for blk in range(2):
        ps = slice(blk * 64, (blk + 1) * 64)
        for q in range(4):
            sl = m4.t[ps, q, blk * 64:(blk + 1) * 64]
            Kk.memset("pool", sl, 1.0, [m4.b])
            cmp = ALU.is_gt if q % 2 == 0 else ALU.is_ge
            S.op("pool", lambda e, sl=sl, cmp=cmp: e.affine_select(sl, sl, [[1, 64]], cmp, 0.0, base=0,
                                                                    channel_multiplier=-1),
                 [m4.b], [m4.b])
            sl2 = mn4.t[ps, q, blk * 64:(blk + 1) * 64]
            Kk.memset("pool", sl2, 1.0, [mn4.b])
            S.op("pool", lambda e, sl2=sl2: e.affine_select(sl2, sl2, [[-1, 64]], ALU.is_gt, 0.0, base=0,
                                                             channel_multiplier=1),
                 [mn4.b], [mn4.b])
    c["mask4"] = m4
    c["maskn4"] = mn4
    omka = T(S, "omka", [128, 1], F32)
    Kk.ts("pool", omka.t[:], cv.t[:, C_KA:C_KA + 1], -1.0, 1.0, ALU.mult, ALU.add, [cv.b], [omka.b])
    c["omka"] = omka
    return c


def phase1(S, Kk, PB, cst, cv, dr, SEQ, cut=None, depth_off=6):
    nblk = SEQ // TB
    xT1, w1, lwa, ag_in = dr["xT1"], dr["w1"], dr["lwa"], dr["ag_in"]
    ones_bf, ones32, bones, bmean = cst["ones_bf"], cst["ones32"], cst["bones"], cst["bmean"]
    ident, m4, mn4, omka = cst["ident64"], cst["mask4"], cst["maskn4"], cst["omka"]
    MUL, ADD, SUB, MAX = ALU.mult, ALU.add, ALU.subtract, ALU.max

    def col(i):
        return cv.t[:, i:i + 1]

    w1b = T(S, "w1b", [128, KC, 512], BF16)
    Kk.dma("pool", w1b.t[:], w1.rearrange("(kc p) n -> p kc n", p=128), [], [w1b.b])
    lwab = T(S, "lwab", [128, 128], BF16)
    Kk.dma("pool", lwab.t[:], lwa, [], [lwab.b])

    def f32t(name):
        return T(S, name, [128, TB], F32)

    def bft(name):
        return T(S, name, [128, TB], BF16)

    xf = [T(S, "xf%d" % i, [128, KC, TB], F32) for i in range(2)]
    xn = T(S, "xn", [128, KC, TB], BF16, nb=KC)
    xsq = [bft("xsq%d" % i) for i in range(3)]
    P = T(S, "P", [128, 4, TB + 1], F32)
    Kk.memset("pool", P.t[:, :, TB:TB + 1], 0.0, [P.b])
    dsc = [f32t("dsc%d" % i) for i in range(2)]
    RING = dr.get("ring", ("pm", "a_t", "Em", "Eend", "kk", "beta"))
    pm_l = [T(S, "pm%d" % i, [128, 4, TB], F32, nb=4) for i in range(2 if "pm" in RING else 1)]
    scr = {}
    for nme in ("sqt", "rstd", "sg", "cs", "Em", "Eex", "Eend", "a_t", "kk", "sq2", "beta"):
        scr[nme] = [f32t(nme + str(i)) for i in range(2 if nme in RING else 1)]
    Ep2 = [f32t("Ep%d" % i) for i in range(4)]
    bonus2 = [f32t("bonus%d" % i) for i in range(4)]
    la, kk2, rk, BT, KT, BH, KH, vb = [bft(n) for n in ("la", "kk2", "rk", "BT", "KT", "BH", "KH", "vb")]
    AR2 = [T(S, "AR%d" % i, [128, 4, 2, 128], BF16) for i in range(2)]
    PTr = [T(S, "ptr%d" % i, [128, 1024], BF16, psum=True) for i in range(2)]
    TM2 = [T(S, "TM%d" % i, [128, 2, 4, 4, 64], BF16) for i in range(2)]
    AM2 = [T(S, "AM%d" % i, [128, 8, 5, 128], BF16, nb=8) for i in range(2)]
    Y02 = [T(S, "Y0_%d" % i, [128, 8, 128], BF16) for i in range(2)]
    Xl = [T(S, "Xl%d" % i, [128, 8, 128], BF16) for i in range(2)]
    Zl = [T(S, "Zl%d" % i, [128, 8, 128], BF16) for i in range(2)]
    Yl = [T(S, "Yl%d" % i, [128, 8, 128], BF16) for i in range(2)]
    PTs = T(S, "PTs", [128, 4, 2, 128], BF16)
    Kk.memset("pool", PTs.t[:], 0.0, [PTs.b])
    Qs = T(S, "Qs", [128, 4, 2, 64], F32)
    GT = bft("GT")
    YI = f32t("YI")
    Tbf = [T(S, "Tbf%d" % i, [128, 9, 64], BF16, nb=9) for i in range(2)]
    T32 = [T(S, "T32_%d" % i, [128, 64], F32) for i in range(2)]
    tmpst = T(S, "tmpst", [128, 64], F32)
    Kk.memset("pool", Tbf[0].t[:, 0, :], 0.0, [Tbf[0].bs[0]])
    Kk.memset("pool", T32[0].t[:], 0.0, [T32[0].b])
    y32, yc, sd = [f32t(n) for n in ("y32", "yc", "sd")]
    yb, yc2 = bft("yb"), bft("yc2")
    yout = [bft("yout%d" % i) for i in range(2)]
    xsq_i = [0]
    nce = T(S, "nce", [128, 8, 1], F32)
    C0 = 0.6065306597126334

    def block(n):
        par = n % 2
        X = xf[n % 2]
        Ep, bonus, AR, TM, AM, Y0 = Ep2[n % 4], bonus2[n % 4], AR2[par], TM2[par], AM2[par], Y02[par]
        pm = pm_l[n % len(pm_l)]
        sqt, rstd, sg, cs, Em, Eex, Eend, a_t, kk, sq2, beta = [scr[k_][n % len(scr[k_])] for k_ in (
            "sqt", "rstd", "sg", "cs", "Em", "Eex", "Eend", "a_t", "kk", "sq2", "beta")]
        Kk.dma("sp", X.t[:], xT1[n], [], [X.b], sem=Kk.tsem(X))
        yield
        st = PB.next()
        for kc in range(KC):
            Kk.act(xn.t[:, kc, :], X.t[:, kc, :], ACT.Copy, [X.b, cv.b], [xn.bs[kc]], scale=col(C_GMIX + kc))
            q_ = xsq[xsq_i[0] % 3]
            xsq_i[0] += 1
            Kk.act(q_.t[:], X.t[:, kc, :], ACT.Square, [X.b], [q_.b])
            Kk.mm(st.t[:], ones_bf.t[:], q_.t[:], [ones_bf.b, q_.b], [st.b], start=(kc == 0), stop=(kc == KC - 1))
        pp = [PB.next() for _ in range(4)]
        for ct in range(4):
            for kc in range(KC):
                Kk.mm(pp[ct].t[:], w1b.t[:, kc, ct * 128:(ct + 1) * 128], xn.t[:, kc, :],
                      [w1b.b, xn.bs[kc]], [pp[ct].b], start=(kc == 0), stop=(kc == KC - 1))
        Kk.act(sqt.t[:], st.t[:], ACT.Ln, [], [st.b, sqt.b], bias=RMS_EPS, scale=1.0 / D)
        tick = S.buf("tick%d" % n)
        Kk.act(rstd.t[:], sqt.t[:], ACT.Exp, [sqt.b], [rstd.b, tick], scale=-0.5)
        Kk.cp("pool", P.t[:, :, 0:1], P.t[:, :, TB:TB + 1], [], [P.b])
        for ct in range(4):
            Kk.tt("dve", P.t[:, ct, 1:TB + 1], pp[ct].t[:], rstd.t[:], MUL, [rstd.b], [pp[ct].b, P.b])
        yield
        for ct in range(4):
            d_ = dsc[ct % 2]
            Kk.tt("pool", d_.t[:], P.t[:, ct, 0:TB], P.t[:, ct, 1:TB + 1], SUB, [P.b], [d_.b])
            Kk.stt(pm.t[:, ct, :], d_.t[:], col(C_MU1 + ct), P.t[:, ct, 1:TB + 1], MUL, ADD,
                   [d_.b, P.b, cv.b], [pm.bs[ct]])
        rP, kP, vP, wP = (pm.t[:, i, :] for i in range(4))
        rB, kB, vB, wB = pm.bs
        wj = dr.get("wjobs", [])
        per = (len(wj) + max(1, nblk - n) - 1) // max(1, nblk - n) if wj else 0
        for _ in range(min(per, len(wj))):
            wj.pop(0)(tick)
        yield
        Kk.act(la.t[0:64, :], pm.t[0:64, 3, :], ACT.Tanh, [wB], [la.b])
        Kk.cp("pool", la.t[64:128, :], pm.t[64:128, 3, :], [wB], [la.b])
        zw, za = PB.next(), PB.next()
        Kk.mm(zw.t[:], lwab.t[0:64, :], la.t[0:64, :], [lwab.b, la.b], [zw.b])
        Kk.mm(za.t[:], lwab.t[64:128, :], la.t[64:128, :], [lwab.b, la.b], [za.b], tp=(64, 0))
        Kk.act(sg.t[:], zw.t[:], ACT.Sigmoid, [cv.b], [zw.b, sg.b], bias=col(C_W0))
        Kk.act(a_t.t[:], za.t[:], ACT.Sigmoid, [cv.b], [za.b, a_t.b], bias=col(C_A0))
        ld = sg
        yield
        for c in range(8):
            sl = slice(c * 64, (c + 1) * 64)
            S.op("dve", lambda e, sl=sl: e.tensor_tensor_scan(cs.t[:, sl], ones32.t[:, :], ld.t[:, sl], 0.0,
                                                               MUL, ADD),
                 [ones32.b, ld.b], [cs.b], cost=0.25)
        Kk.act(Ep.t[:], cs.t[:], ACT.Exp, [cs.b], [Ep.b], scale=-C0)
        Kk.act(Em.t[:], cs.t[:], ACT.Exp, [cs.b], [Em.b], scale=C0)
        Kk.tt("pool", Eex.t[:], cs.t[:], ld.t[:], SUB, [cs.b, ld.b], [Eex.b])
        Kk.act(Eex.t[:], Eex.t[:], ACT.Exp, [], [Eex.b], scale=-C0)
        Kk.act(nce.t[:], v3(cs.t[:], 8)[:, :, 63:64], ACT.Copy, [cs.b], [nce.b], scale=-C0)
        for c in range(8):
            sl = slice(c * 64, (c + 1) * 64)
            Kk.act(Eend.t[:, sl], cs.t[:, sl], ACT.Exp, [cs.b, nce.b], [Eend.b], scale=C0, bias=nce.t[:, c, :])
        yield
        Kk.act(kk.t[:], kP, ACT.Copy, [kB, cv.b], [kk.b], scale=col(C_KK))
        Kk.act(kk2.t[:], kP, ACT.Square, [kB, cv.b], [kk2.b], scale=col(C_KK))
        ssb = PB.next()
        Kk.mm(ssb.t[:], bones.t[:], kk2.t[:], [bones.b, kk2.b], [ssb.b])
        Kk.ts("dve", sq2.t[:], ssb.t[:], 1e-19, None, MAX, None, [], [ssb.b, sq2.b])
        Kk.act(sq2.t[:], sq2.t[:], ACT.Ln, [], [sq2.b])
        Kk.act(sq2.t[:], sq2.t[:], ACT.Exp, [], [sq2.b], scale=-0.5)
        Kk.tt("dve", kk.t[:], kk.t[:], sq2.t[:], MUL, [sq2.b], [kk.b])
        kkn = kk
        Kk.tt("pool", beta.t[:], kkn.t[:], a_t.t[:], MUL, [kkn.b, a_t.b], [beta.b])
        Kk.act(a_t.t[:], a_t.t[:], ACT.Identity, [cv.b, omka.b], [a_t.b], scale=col(C_KA), bias=omka.t[:, 0:1])
        Kk.tt("pool", a_t.t[:], kP, a_t.t[:], MUL, [kB], [a_t.b])
        kmod = a_t
        yield
        Kk.stt(AR.t[:, :, 0, :], v3(kkn.t[:], 4), -1.0, v3(Eex.t[:], 4), MUL, MUL, [kkn.b, Eex.b], [AR.b])
        Kk.tt("pool", AR.t[:, :, 1, :], v3(rP, 4), v3(Ep.t[:], 4), MUL, [rB, Ep.b], [AR.b])
        Kk.tt("pool", BT.t[:], beta.t[:], Em.t[:], MUL, [beta.b, Em.b], [BT.b])
        Kk.tt("dve", KT.t[:], kmod.t[:], Em.t[:], MUL, [kmod.b, Em.b], [KT.b])
        Kk.tt("pool", BH.t[:], beta.t[:], Eend.t[:], MUL, [beta.b, Eend.b], [BH.b])
        Kk.tt("dve", KH.t[:], kmod.t[:], Eend.t[:], MUL, [kmod.b, Eend.b], [KH.b])
        Kk.cp("act", vb.t[:], vP, [vB], [vb.b])
        Kk.act(sqt.t[:], rP, ACT.Copy, [rB, cv.b], [sqt.b], scale=col(C_RK))
        Kk.tt("pool", rk.t[:], sqt.t[:], kmod.t[:], MUL, [sqt.b, kmod.b], [rk.b])
        bsb = PB.next()
        Kk.mm(bsb.t[:], bones.t[:], rk.t[:], [bones.b, rk.b], [bsb.b])
        Kk.tt("dve", bonus.t[:], bsb.t[:], vP, MUL, [vB], [bsb.b, bonus.b])
        yield
        srcs = [(lambda cp: AR.t[:, cp, 0, :], AR.b), (lambda cp: BH.t[:, cp * 128:(cp + 1) * 128], BH.b),
                (lambda cp: KH.t[:, cp * 128:(cp + 1) * 128], KH.b),
                (lambda cp: vb.t[:, cp * 128:(cp + 1) * 128], vb.b)]
        for h in range(2):
            hs = slice(h * 64, (h + 1) * 64)
            for q, (sf, sbuf) in enumerate(srcs):
                for cp in range(4):
                    o0 = (q * 4 + cp) * 64
                    Kk.tr(PTr[h].t[:, o0:o0 + 64], sf(cp)[hs, :], ident.t[hs, :], [sbuf, ident.b], [PTr[h].b])
        Kk.cp("act", TM.t[:, 0, :, :, :].rearrange("p a b c -> p (a b c)"), PTr[0].t[:], [], [PTr[0].b, TM.b])
        Kk.cp("dve", TM.t[:, 1, :, :, :].rearrange("p a b c -> p (a b c)"), PTr[1].t[:], [], [PTr[1].b, TM.b])
        yield
        for h in range(2):
            hs = slice(h * 64, (h + 1) * 64)
            for cp in range(4):
                p = h * 4 + cp
                ts_ = slice(cp * 128, (cp + 1) * 128)
                bk = PB.next()
                arf = AR.t[hs, cp, :, :].rearrange("p a b -> p (a b)")
                Kk.mm(bk.t[:, 0:256], BT.t[hs, ts_], arf, [BT.b, AR.b], [bk.b], tp=(h * 64, 0))
                Kk.mm(bk.t[:, 256:512], KT.t[hs, ts_], arf, [KT.b, AR.b], [bk.b], tp=(h * 64, 0))
                Kk.tt("dve", AM.t[:, p, 0:4, :], v3(bk.t[:], 4), m4.t[:], MUL, [m4.b], [bk.b, AM.bs[p]])
            bkn = PB.next()
            for cp in range(4):
                ts_ = slice(cp * 128, (cp + 1) * 128)
                Kk.mm(bkn.t[:, ts_], AR.t[hs, cp, 0, :], BT.t[hs, ts_], [AR.b, BT.b], [bkn.b], tp=(h * 64, 0))
            Kk.tt("dve", AM.t[:, h * 4:(h + 1) * 4, 4, :], v3(bkn.t[:], 4), mn4.t[:], MUL, [mn4.b],
                  [bkn.b] + AM.bs[h * 4:(h + 1) * 4])
        yield
        bv = PB.next()
        for p in range(8):
            Kk.mm(bv.t[:, p * 64:(p + 1) * 64], AM.t[:, p, 2, :], TM.t[:, p // 4, 3, p % 4, :], [AM.bs[p], TM.b],
                  [bv.b])
        for h in range(2):
            wa = slice(h * 64, (h + 1) * 64)
            uv = slice((1 - h) * 64, (2 - h) * 64)
            Kk.cp("act", Y0.t[:, h * 4:(h + 1) * 4, wa], TM.t[:, h, 0, :, :], [TM.b], [Y0.b])
            Kk.cp("dve", Y0.t[:, h * 4:(h + 1) * 4, uv], v3(bv.t[:], 8)[:, h * 4:(h + 1) * 4, :], [], [bv.b, Y0.b])
        yield
        Yc = Y0
        for k in range(6):
            if k == 0:
                Xk = lambda p: AM.t[:, p, 0, :]
                Zk = lambda p: AM.t[:, p, 4, :]
                XkB = lambda p: [AM.bs[p]]
                ZkB = XkB
            else:
                Xt, Zt = Xl[k % 2], Zl[k % 2]
                Xk = lambda p, Xt=Xt: Xt.t[:, p, :]
                Zk = lambda p, Zt=Zt: Zt.t[:, p, :]
                XkB = lambda p, Xt=Xt: [Xt.b]
                ZkB = lambda p, Zt=Zt: [Zt.b]
            Yn = Yl[k % 2]
            if k < 5:
                Xn_, Zn_ = Xl[(k + 1) % 2], Zl[(k + 1) % 2]
                bx = [PB.next(), PB.next()]
                for p in range(8):
                    bb = bx[p // 4]
                    Kk.mm(bb.t[:, (p % 4) * 128:(p % 4 + 1) * 128], Zk(p), Xk(p), ZkB(p) + XkB(p), [bb.b])
                for i in range(2):
                    Kk.cp("act", Xn_.t[:, i * 4:(i + 1) * 4, :], v3(bx[i].t[:], 4), [], [bx[i].b, Xn_.b])
            b01 = [PB.next(), PB.next()]
            for p in range(8):
                bb = b01[p // 4]
                Kk.mm(bb.t[:, (p % 4) * 128:(p % 4 + 1) * 128], Xk(p), Yc.t[:, p, :], XkB(p) + [Yc.b], [bb.b])
            for i in range(2):
                Kk.tt("dve", Yn.t[:, i * 4:(i + 1) * 4, :], v3(b01[i].t[:], 4), Yc.t[:, i * 4:(i + 1) * 4, :], ADD,
                      [Yc.b], [b01[i].b, Yn.b])
            if k < 4:
                bz = [PB.next(), PB.next()]
                for p in range(8):
                    bb = bz[p // 4]
                    Kk.mm(bb.t[:, (p % 4) * 128:(p % 4 + 1) * 128], Xk(p), Zk(p), ZkB(p) + XkB(p), [bb.b])
                for i in range(2):
                    Kk.cp("act" if i == 0 else "dve", Zn_.t[:, i * 4:(i + 1) * 4, :], v3(bz[i].t[:], 4),
                          [], [bz[i].b, Zn_.b])
            Yc = Yn
            yield
        XF = Yc
        bq = [PB.next(), PB.next()]
        for e_ in range(2):
            rows = slice(e_ * 64, (e_ + 1) * 64)
            bb = bq[e_]
            for cp in range(4):
                c0 = cp * 128
                for h in range(2):
                    p = h * 4 + cp
                    off = h * 64
                    wa = slice(h * 64, (h + 1) * 64)
                    uv = slice((1 - h) * 64, (2 - h) * 64)
                    Kk.mm(bb.t[off:off + 64, c0:c0 + 64], XF.t[rows, p, wa], TM.t[rows, h, 1, cp, :], [XF.b, TM.b],
                          [bb.b], tp=(e_ * 64, off))
                    Kk.mm(bb.t[off:off + 64, c0 + 64:c0 + 128], TM.t[rows, h, 1, cp, :], XF.t[rows, p, uv],
                          [XF.b, TM.b], [bb.b], start=True, stop=False, tp=(e_ * 64, off))
                    Kk.mm(bb.t[off:off + 64, c0 + 64:c0 + 128], TM.t[rows, h, 2, cp, :], TM.t[rows, h, 3, cp, :],
                          [TM.b], [bb.b], start=False, stop=True, tp=(e_ * 64, off))
        for e_ in range(2):
            for h in range(2):
                hs = slice(h * 64, (h + 1) * 64)
                Kk.cp("act", PTs.t[hs, :, e_, hs], v3(bq[e_].t[:], 4)[hs, :, 0:64], [], [bq[e_].b, PTs.b])
            Kk.cp("dve", Qs.t[:, :, e_, :], v3(bq[e_].t[:], 4)[:, :, 64:128], [], [bq[e_].b, Qs.b])
        yield
        bg, byi = PB.next(), PB.next()
        for h in range(2):
            off = h * 64
            wa = slice(h * 64, (h + 1) * 64)
            uv = slice((1 - h) * 64, (2 - h) * 64)
            for cp in range(4):
                p = h * 4 + cp
                ts_ = slice(cp * 128, (cp + 1) * 128)
                Kk.mm(bg.t[off:off + 64, ts_], XF.t[:, p, wa], AM.t[:, p, 1, :], [XF.b, AM.bs[p]], [bg.b],
                      tp=(0, off))
                Kk.mm(byi.t[off:off + 64, ts_], XF.t[:, p, uv], AM.t[:, p, 1, :], [XF.b, AM.bs[p]], [byi.b],
                      start=True, stop=False, tp=(0, off))
                Kk.mm(byi.t[off:off + 64, ts_], TM.t[:, h, 3, cp, :], AM.t[:, p, 3, :], [TM.b, AM.bs[p]], [byi.b],
                      start=False, stop=True, tp=(0, off))
        Kk.tt("dve", v3(GT.t[:], 4), v3(bg.t[:], 4), AR.t[:, :, 1, :], ADD, [AR.b], [bg.b, GT.b])
        Kk.cp("act", YI.t[:], byi.t[:], [], [byi.b, YI.b])
        yield
        TB_ = Tbf[par]
        TBn = Tbf[1 - par]
        for c in range(8):
            gc = n * 8 + c
            Tc, Tn = T32[gc % 2], T32[(gc + 1) % 2]
            bst = PB.next()
            Kk.mm(bst.t[:, 0:64], PTs.t[:, c // 2, c % 2, :], TB_.t[:, c, :], [PTs.b, TB_.bs[c]], [bst.b])
            Kk.stt(tmpst.t[:], Tc.t[:], Ep.t[:, c * 64 + 63:c * 64 + 64], Qs.t[:, c // 2, c % 2, :], MUL, ADD,
                   [Tc.b, Ep.b, Qs.b], [tmpst.b])
            Kk.tt("dve", TB_.t[:, c + 1, :], tmpst.t[:], bst.t[:, 0:64], ADD, [tmpst.b], [bst.b, TB_.bs[c + 1]])
            Kk.tt("dve", Tn.t[:], tmpst.t[:], bst.t[:, 0:64], ADD, [tmpst.b], [bst.b, Tn.b])
            if c == 7:
                Kk.tt("dve", TBn.t[:, 0, :], tmpst.t[:], bst.t[:, 0:64], ADD, [tmpst.b], [bst.b, TBn.bs[0]])
            if c % 2 == 1:
                yield
        byh = [PB.next(), PB.next()]
        for h in range(2):
            off = h * 64
            hs = slice(off, off + 64)
            for c in range(8):
                cs_ = slice(c * 64, (c + 1) * 64)
                Kk.mm(byh[h].t[hs, cs_], TB_.t[hs, c, :], GT.t[hs, cs_], [TB_.bs[c], GT.b], [byh[h].b],
                      tp=(off, off))
        for h in range(2):
            hs = slice(h * 64, (h + 1) * 64)
            Kk.tt("dve", y32.t[hs, :], byh[h].t[hs, :], YI.t[hs, :], ADD, [YI.b], [byh[h].b, y32.b])
        Kk.cp("act", yb.t[:], y32.t[:], [y32.b], [yb.b])
        yield
        bm = PB.next()
        Kk.mm(bm.t[:], bmean.t[:], yb.t[:], [bmean.b, yb.b], [bm.b])
        Kk.tt("dve", yc.t[:], y32.t[:], bm.t[:], SUB, [y32.b], [bm.b, yc.b])
        Kk.tt("pool", yc2.t[:], yc.t[:], yc.t[:], MUL, [yc.b], [yc2.b])
        bvv = PB.next()
        Kk.mm(bvv.t[:], bmean.t[:], yc2.t[:], [bmean.b, yc2.b], [bvv.b])
        Kk.act(sd.t[:], bvv.t[:], ACT.Ln, [], [bvv.b, sd.b], bias=LN_X_EPS)
        Kk.act(sd.t[:], sd.t[:], ACT.Exp, [], [sd.b], scale=-0.5)
        Kk.tt("dve", yc.t[:], yc.t[:], sd.t[:], MUL, [sd.b], [yc.b])
        Kk.act(yc.t[:], yc.t[:], ACT.Identity, [cv.b], [yc.b], bias=col(C_LNB), scale=col(C_LNW))
        yo_ = yout[n % 2]
        Kk.tt("dve", yo_.t[:], yc.t[:], bonus.t[:], ADD, [yc.b, bonus.b], [yo_.b])
        bpq = nblk // 4
        jq = n // bpq
        Kk.dma("sp", ag_in[jq, :, (n % bpq) * TB:(n % bpq + 1) * TB], yo_.t[:], [yo_.b], [dr["ag_in_b"][jq]],
               sem=Kk.tsem(yo_))
        if (n + 1) % bpq == 0 and dr["do_ag"]:
            S.dma("pool", lambda e, jq=jq: e.collective_compute("AllGather", ALU.bypass,
                                                               replica_groups=[[0, 1, 2, 3], [4, 5, 6, 7]],
                                                               ins=[dr["ag_in"][jq]], outs=[dr["ag_out"][jq]]),
                  [dr["ag_in_b"][jq]], [dr["ag_out_b"][jq]], inc=1)
        yield

    gens = [block(n) for n in range(nblk)]
    live = {}
    t = 0
    nxt = 0
    while nxt < nblk or live:
        if nxt < nblk and t == nxt * depth_off:
            live[nxt] = gens[nxt]
            nxt += 1
        for n in sorted(live):
            try:
                next(live[n])
            except StopIteration:
                del live[n]
        t += 1


W2_TILES = ([(0, 128), (128, 32)] + [(672 + t * 128, 128) for t in range(4)] + [(1184 + t * 128, 128) for t in range(4)]
            + [(160 + t * 128, 128) for t in range(4)] + [(1696 + t * 128, 128) for t in range(8)]
            + [(2720 + t * 128, 128) for t in range(8)])
CT8 = [(t * 128, 128) for t in range(8)]


def weight_plan():
    plan = [("w2", "w2", 0, D, W2_TILES), ("wpa", "wpa", 0, 512, CT8), ("wpb", "wpb", 0, 512, CT8),
            ("wmix", "wmix", 0, D, CT8), ("wq", "wq", 0, D, CT8), ("wk", "wkv", 0, D, CT8), ("wxo", "wxo", 0, D, CT8)]
    for half in range(2):
        plan.append(("wup%d" % half, "wup", 0, D, [(half * 2048 + t * 128, 128) for t in range(16)]))
        plan.append(("wdn%d" % half, "wdn", half * 2048, 2048, CT8))
    return plan


def convert_weights(nc, S, Kk, dr):
    jobs = []
    wb = {}
    for name, src, r0, nr, tiles in weight_plan():
        kcs = nr // 128
        t = nc.dram_tensor("wb_" + name, [len(tiles), 128, kcs, 128], BF16).ap()
        b = S.buf("wb_" + name)
        sem = S.new_sem()
        wb[name] = (t, b, kcs, tiles)
        for ti, (c0, w) in enumerate(tiles):
            def job(tick=None, t=t, b=b, sem=sem, src=src, r0=r0, nr=nr, c0=c0, w=w, ti=ti, kcs=kcs):
                Kk.dma("pool", t[ti, :, :, 0:w],
                       dr[src][r0:r0 + nr, c0:c0 + w].rearrange("(kc p) n -> p kc n", p=128),
                       [tick] if tick is not None else [], [b], sem=sem)
            jobs.append(job)
    t = nc.dram_tensor("wb_wv", [4, 128, KC, 256], BF16).ap()
    b = S.buf("wb_wv")
    sem = S.new_sem()
    wb["wv"] = (t, b, KC, None)
    for nd in range(4):
        def job(tick=None, t=t, b=b, sem=sem, nd=nd):
            Kk.dma("pool", t[nd], dr["wkv"][:, D + nd * 256:D + (nd + 1) * 256].rearrange("(kc p) n -> p kc n", p=128),
                   [tick] if tick is not None else [], [b], sem=sem)
        jobs.append(job)
    dr["wb"] = wb
    return jobs


def phase2(S, Kk, cst, cv, dr, SEQ):
    TOK2 = SEQ // 4
    nblk = TOK2 // TB
    ones_bf = cst["ones_bf"]
    MUL, ADD, SUB, MAX = ALU.mult, ALU.add, ALU.subtract, ALU.max
    PB = Banks(S, 7)
    HB = T(S, "hb", [128, 512], F32, psum=True)

    def col(i):
        return cv.t[:, i:i + 1]

    NSLAB = 8
    slabs = [T(S, "slab%d" % i, [128, 16, 128], BF16) for i in range(NSLAB)]
    for sl in slabs:
        sl.sem = S.new_sem()
    sl_i = [0]

    def next_slab():
        sl = slabs[sl_i[0] % NSLAB]
        sl_i[0] += 1
        return sl

    def linear(wname, rhs, evac, halo=None):
        wt, wbuf, kcs, col_tiles = dr["wb"][wname]
        for ti, (c0, w) in enumerate(col_tiles):
            sl = next_slab()
            Kk.dma("sp", sl.t[:, 0:kcs, 0:w], wt[ti, :, :, 0:w], [wbuf], [sl.b], sem=sl.sem)
            bank = PB.next()
            for kc in range(kcs):
                ap, b = rhs(kc)
                Kk.mm(bank.t[0:w, :], sl.t[:, kc, 0:w], ap, [sl.b, b], [bank.b], start=(kc == 0),
                      stop=(kc == kcs - 1))
            if halo is not None and halo(ti) is not None:
                hsl = halo(ti)
                for kc in range(kcs):
                    Kk.mm(HB.t[0:w, hsl], sl.t[:, kc, 0:w], xhn.t[:, kc, :], [sl.b, xhn.b], [HB.b],
                          start=(kc == 0), stop=(kc == kcs - 1))
            evac(ti, bank)

    yag = T(S, "yag", [128, 4, TOK2], BF16)
    idx = dr["idx_sb"]
    agv = dr["ag_out"].rearrange("j r t -> (j r) t")
    for g in range(4):
        S.dma("pool", lambda e, g=g: e.indirect_dma_start(out=yag.t[:, g, :], out_offset=None, in_=agv,
                                                          in_offset=bass.IndirectOffsetOnAxis(idx.t[:, g:g + 1], 0)),
              dr["ag_out_b"] + [idx.b], [yag.b])
    wlg0 = T(S, "wlg0", [128, 512], BF16)
    wlg1 = T(S, "wlg1", [32, 512], BF16)
    Kk.dma("pool", wlg0.t[:], dr["wlg"][0:128, :], [], [wlg0.b])
    Kk.dma("pool", wlg1.t[:], dr["wlg"][128:160, :], [], [wlg1.b])
    KTm = T(S, "KTm", [128, 8, N_MEM], BF16)
    Vm = T(S, "Vm", [128, 2, D], BF16)

    sqt = T(S, "sqt2", [128, TB], F32)
    rstd = T(S, "rstd2", [128, TB], F32)
    xsq = [T(S, "xsq2_%d" % i, [128, TB], BF16) for i in range(2)]
    xsq_i = [0]

    def rms_stats(src_fn, src_b, n_tok, rstd_ap, rstd_b, sqt_ap, sqt_b, bank_ap_fn):
        bank = PB.next()
        for kc in range(KC):
            t = xsq[xsq_i[0] % 2]
            xsq_i[0] += 1
            sb_ = src_b(kc) if callable(src_b) else src_b
            Kk.tt("pool", t.t[:, 0:n_tok], src_fn(kc), src_fn(kc), MUL, [sb_], [t.b])
            Kk.mm(bank.t[:, 0:n_tok], ones_bf.t[:], t.t[:, 0:n_tok], [ones_bf.b, t.b], [bank.b],
                  start=(kc == 0), stop=(kc == KC - 1))
        Kk.act(sqt_ap, bank.t[:, 0:n_tok], ACT.Ln, [], [bank.b, sqt_b], bias=RMS_EPS, scale=1.0 / D)
        Kk.act(rstd_ap, sqt_ap, ACT.Exp, [sqt_b], [rstd_b], scale=-0.5)

    mk = S.mark()
    mf = T(S, "mf", [128, KC, N_MEM], F32)
    mn = T(S, "mn", [128, KC, N_MEM], BF16)
    Kk.dma("sp", mf.t[:], dr["memT"], [], [mf.b])
    rms_stats(lambda kc: mf.t[:, kc, :], mf.b, N_MEM, rstd.t[:, 0:N_MEM], rstd.b, sqt.t[:, 0:N_MEM], sqt.b, None)
    for kc in range(KC):
        Kk.stt(mn.t[:, kc, :], mf.t[:, kc, :], col(C_GMEM + kc), rstd.t[:, 0:N_MEM], MUL, MUL,
               [mf.b, rstd.b, cv.b], [mn.b])
    def evac_k(ti, bank):
        Kk.cp("act", KTm.t[:, ti, :], bank.t[:, 0:N_MEM], [], [bank.b, KTm.b])

    wt, wbuf, kcs_, tiles_ = dr["wb"]["wk"]
    for dt_ in range(8):
        sl = next_slab()
        Kk.dma("sp", sl.t[:, 0:KC, :], wt[dt_], [wbuf], [sl.b], sem=sl.sem)
        bank = PB.next()
        for kc in range(KC):
            Kk.mm(bank.t[:, 0:N_MEM], sl.t[:, kc, :], mn.t[:, kc, :], [sl.b, mn.b], [bank.b], start=(kc == 0),
                  stop=(kc == KC - 1))
        evac_k(dt_, bank)
    for nd in range(4):
        sl = next_slab()
        slv = sl.t[:].rearrange("p a b -> p (a b)").rearrange("p (k n) -> p k n", k=KC)
        Kk.dma("sp", slv, dr["wb"]["wv"][0][nd], [dr["wb"]["wv"][1]], [sl.b], sem=sl.sem)
        for mt in range(2):
            bank = PB.next()
            for kc in range(KC):
                Kk.mm(bank.t[:, 0:256], mn.t[:, kc, mt * 128:(mt + 1) * 128], slv[:, kc, :], [sl.b, mn.b], [bank.b],
                      start=(kc == 0), stop=(kc == KC - 1))
            Kk.cp("act", Vm.t[:, mt, nd * 256:(nd + 1) * 256], bank.t[:, 0:256], [], [bank.b, Vm.b])

    X = [T(S, "X%d" % i, [128, KC, TB], F32, nb=KC) for i in range(2)]
    xn = T(S, "xn2", [128, KC, TB], BF16, nb=KC)
    xhf = T(S, "xhf", [128, KC, 2], F32)
    xhn = T(S, "xhn", [128, KC, 2], BF16)
    rstd_h = T(S, "rstd_h", [128, 2], F32)
    sqt_h = T(S, "sqt_h", [128, 2], F32)
    pgd = T(S, "pgd", [128, 2, TB + 1], F32)
    dgd = T(S, "dgd", [128, 2, TB], F32)
    sgd = T(S, "sgd", [128, 2, TB], BF16)
    cbt = T(S, "cbt", [128, 4, TB], F32, nb=4)
    cct = T(S, "cct", [128, 4, TB], F32, nb=4)
    cch = T(S, "cch", [128, 4, 2], F32)
    u = T(S, "u", [128, 4, TB + 2], F32, nb=4)
    acc = T(S, "acc", [128, TB], F32)
    outb = T(S, "outb", [128, 4, TB], BF16, nb=4)
    outa = T(S, "outa", [128, 4, TB], BF16, nb=4)
    H16 = T(S, "H16", [128, 16, TB], BF16, nb=16)
    ma = T(S, "ma", [128, 8, TB], BF16, nb=8)
    mg = T(S, "mg", [128, 8, TB], BF16, nb=8)
    tmpm = T(S, "tmpm", [128, TB], F32)
    es = [T(S, "es%d" % i, [128, TB], BF16) for i in range(4)]
    rinv = T(S, "rinv", [128, TB], F32)
    rl = [T(S, "rl%d" % i, [128, TB], BF16) for i in range(2)]

    def normalize(Xb, gcol, dst, dstbs):
        rms_stats(lambda kc: Xb.t[:, kc, :], lambda kc: Xb.bs[kc], TB, rstd.t[:], rstd.b, sqt.t[:], sqt.b, None)
        for kc in range(KC):
            Kk.stt(dst.t[:, kc, :], Xb.t[:, kc, :], col(gcol + kc), rstd.t[:], MUL, MUL, [Xb.bs[kc], rstd.b, cv.b],
                   [dstbs[kc]])

    tiles = [(0, 128), (128, 32)]
    tiles += [(672 + t * 128, 128) for t in range(4)]
    tiles += [(1184 + t * 128, 128) for t in range(4)]
    tiles += [(160 + t * 128, 128) for t in range(4)]
    tiles += [(1696 + t * 128, 128) for t in range(8)]
    tiles += [(2720 + t * 128, 128) for t in range(8)]

    for n in range(nblk):
        Xb = X[n % 2]
        Kk.dma("sp", Xb.t[:], dr["xT2"][n], [], Xb.bs, sem=Kk.tsem(Xb))
        first = (n == 0)
        if first:
            Kk.dma("sp", xhf.t[:], dr["xh"], [], [xhf.b])
            rms_stats(lambda kc: xhf.t[:, kc, :], xhf.b, 2, rstd_h.t[:], rstd_h.b, sqt_h.t[:], sqt_h.b, None)
            for kc in range(KC):
                Kk.stt(xhn.t[:, kc, :], xhf.t[:, kc, :], col(C_GMIX + kc), rstd_h.t[:], MUL, MUL,
                       [xhf.b, rstd_h.b, cv.b], [xhn.b])
        normalize(Xb, C_GMIX, xn, xn.bs)

        def evac_in(ti, bank):
            if ti < 2:
                w = 128 if ti == 0 else 32
                if first:
                    if True:
                        Kk.cp("act", pgd.t[0:w, ti, 0:1], HB.t[0:w, ti * 2 + 1:ti * 2 + 2], [], [HB.b, pgd.b])
                else:
                    Kk.cp("dve", pgd.t[0:w, ti, 0:1], pgd.t[0:w, ti, TB:TB + 1], [], [pgd.b])
                Kk.cp("act", pgd.t[0:w, ti, 1:TB + 1], bank.t[0:w, :], [], [bank.b, pgd.b])
                if ti == 1:
                    for t in range(2):
                        ww = 128 if t == 0 else 32
                        Kk.tt("dve", dgd.t[0:ww, t, :], pgd.t[0:ww, t, 0:TB], pgd.t[0:ww, t, 1:TB + 1], SUB,
                              [pgd.b], [dgd.b])
                        Kk.stt(dgd.t[0:ww, t, :], dgd.t[0:ww, t, :], cv.t[0:ww, C_MUGD + t:C_MUGD + t + 1],
                               pgd.t[0:ww, t, 1:TB + 1], MUL, ADD, [pgd.b, cv.b], [dgd.b])
                        Kk.act(sgd.t[0:ww, t, :], dgd.t[0:ww, t, :], ACT.Sigmoid, [dgd.b], [sgd.b])
                    for ot in range(4):
                        bk = PB.next()
                        Kk.mm(bk.t[:], wlg0.t[:, ot * 128:(ot + 1) * 128], sgd.t[:, 0, :], [wlg0.b, sgd.b], [bk.b],
                              start=True, stop=False)
                        Kk.mm(bk.t[:], wlg1.t[:, ot * 128:(ot + 1) * 128], sgd.t[0:32, 1, :], [wlg1.b, sgd.b],
                              [bk.b], start=False, stop=True)
                        Kk.tt("dve", outa.t[:, ot, :], bk.t[:], yag.t[:, ot, n * TB:(n + 1) * TB], MUL, [yag.b],
                              [bk.b, outa.bs[ot]])
            elif ti < 6:
                t = ti - 2
                Kk.cp("act", cct.t[:, t, :], bank.t[:], [], [bank.b, cct.bs[t]])
                if first:
                    Kk.cp("act", cch.t[:, t, :], HB.t[:, ti * 2:ti * 2 + 2], [], [HB.b, cch.b])
            elif ti < 10:
                t = ti - 6
                if first:
                    Kk.tt("dve", u.t[:, t, 0:2], HB.t[:, ti * 2:ti * 2 + 2], cch.t[:, t, :], MUL, [cch.b],
                          [HB.b, u.bs[t]])
                else:
                    Kk.cp("dve", u.t[:, t, 0:2], u.t[:, t, TB:TB + 2], [], [u.bs[t]])
                Kk.tt("dve", u.t[:, t, 2:TB + 2], bank.t[:], cct.t[:, t, :], MUL, [cct.bs[t]], [bank.b, u.bs[t]])
            elif ti < 14:
                t = ti - 10
                Kk.cp("act", cbt.t[:, t, :], bank.t[:], [], [bank.b, cbt.bs[t]])
                Kk.ts("dve", acc.t[:], u.t[:, t, 2:TB + 2], col(C_CONV + 2 * 4 + t), None, MUL, None,
                      [u.bs[t], cv.b], [acc.b])
                Kk.stt(acc.t[:], u.t[:, t, 1:TB + 1], col(C_CONV + 1 * 4 + t), acc.t[:], MUL, ADD, [u.bs[t], cv.b],
                       [acc.b])
                Kk.stt(acc.t[:], u.t[:, t, 0:TB], col(C_CONV + 0 * 4 + t), acc.t[:], MUL, ADD, [u.bs[t], cv.b],
                       [acc.b])
                Kk.tt("dve", outb.t[:, t, :], acc.t[:], cbt.t[:, t, :], MUL, [acc.b, cbt.bs[t]], [outb.bs[t]])
            elif ti < 22:
                t = ti - 14
                Kk.act(H16.t[:, t, :], bank.t[:], ACT.Sigmoid, [cv.b], [bank.b, H16.bs[t]], bias=col(C_BGA + t))
            else:
                t = ti - 22
                Kk.act(H16.t[:, 8 + t, :], bank.t[:], ACT.Sigmoid, [cv.b], [bank.b, H16.bs[8 + t]],
                       bias=col(C_BGB + t))

        def halo_fn(ti):
            if first and ti < 10:
                return slice(ti * 2, ti * 2 + 2)
            return None

        linear("w2", lambda kc: (xn.t[:, kc, :], xn.bs[kc]), evac_in, halo=halo_fn)

        ct8 = [(t * 128, 128) for t in range(8)]

        def evac_pa(ti, bank):
            Kk.tt("dve", ma.t[:, ti, :], bank.t[:], H16.t[:, ti, :], MUL, [H16.bs[ti]], [bank.b, ma.bs[ti]])

        linear("wpa", lambda kc: (outa.t[:, kc, :], outa.bs[kc]), evac_pa)

        def evac_pb(ti, bank):
            Kk.tt("dve", tmpm.t[:], bank.t[:], H16.t[:, 8 + ti, :], MUL, [H16.bs[8 + ti]], [bank.b, tmpm.b])
            Kk.tt("dve", mg.t[:, ti, :], tmpm.t[:], ma.t[:, ti, :], ADD, [tmpm.b, ma.bs[ti]], [mg.bs[ti]])

        linear("wpb", lambda kc: (outb.t[:, kc, :], outb.bs[kc]), evac_pb)

        def evac_res(ti, bank):
            Kk.tt("dve", Xb.t[:, ti, :], bank.t[:], Xb.t[:, ti, :], ADD, [], [bank.b, Xb.bs[ti]])

        linear("wmix", lambda kc: (mg.t[:, kc, :], mg.bs[kc]), evac_res)

        normalize(Xb, C_GXA, xn, xn.bs)

        def evac_q(ti, bank):
            Kk.act(H16.t[:, ti, :], bank.t[:], ACT.Copy, [], [bank.b, H16.bs[ti]], scale=1.0 / 16.0)

        linear("wq", lambda kc: (xn.t[:, kc, :], xn.bs[kc]), evac_q)
        for hd in range(4):
            ee = [es[(hd % 2) * 2], es[(hd % 2) * 2 + 1]]
            for mt in range(2):
                bk = PB.next()
                for j in range(2):
                    dc = 2 * hd + j
                    Kk.mm(bk.t[:], KTm.t[:, dc, mt * 128:(mt + 1) * 128], H16.t[:, dc, :], [KTm.b, H16.bs[dc]],
                          [bk.b], start=(j == 0), stop=(j == 1))
                Kk.act(ee[mt].t[:], bk.t[:], ACT.Exp, [], [bk.b, ee[mt].b])
            bk = PB.next()
            for mt in range(2):
                Kk.mm(bk.t[:], ones_bf.t[:], ee[mt].t[:], [ones_bf.b, ee[mt].b], [bk.b], start=(mt == 0),
                      stop=(mt == 1))
            Kk.act(rinv.t[:], bk.t[:], ACT.Ln, [], [bk.b, rinv.b])
            Kk.act(rinv.t[:], rinv.t[:], ACT.Exp, [], [rinv.b], scale=-1.0)
            for j in range(2):
                dc = 2 * hd + j
                bk = PB.next()
                for mt in range(2):
                    Kk.mm(bk.t[:], Vm.t[:, mt, dc * 128:(dc + 1) * 128], ee[mt].t[:], [Vm.b, ee[mt].b], [bk.b],
                          start=(mt == 0), stop=(mt == 1))
                Kk.tt("dve", H16.t[:, 8 + dc, :], bk.t[:], rinv.t[:], MUL, [rinv.b], [bk.b, H16.bs[8 + dc]])
        linear("wxo", lambda kc: (H16.t[:, 8 + kc, :], H16.bs[8 + kc]), evac_res)

        normalize(Xb, C_GMLP, xn, xn.bs)
        for half in range(2):
            def evac_up(ti, bank):
                r_ = rl[ti % 2]
                Kk.act(r_.t[:], bank.t[:], ACT.Relu, [], [bank.b, r_.b])
                Kk.tt("pool", H16.t[:, ti, :], r_.t[:], r_.t[:], MUL, [r_.b], [H16.bs[ti]])

            linear("wup%d" % half, lambda kc: (xn.t[:, kc, :], xn.bs[kc]), evac_up)
            linear("wdn%d" % half, lambda kc: (H16.t[:, kc, :], H16.bs[kc]), evac_res)

        rms_stats(lambda kc: Xb.t[:, kc, :], lambda kc: Xb.bs[kc], TB, rstd.t[:], rstd.b, sqt.t[:], sqt.b, None)
        for kc in range(KC):
            Kk.stt(Xb.t[:, kc, :], Xb.t[:, kc, :], col(C_GFIN + kc), rstd.t[:], MUL, MUL, [rstd.b, cv.b],
                   [Xb.bs[kc]])
        ev = Kk.dma("sp", dr["outT"][n], Xb.t[:], Xb.bs, [], sem=Kk.tsem(Xb))
        S.final_events.append(ev)


def build(SEQ, phases=("p1", "ag", "p2"), debug=False, cut=None):
    nc = bass.Bass("TRN2", target_bir_lowering=False)
    S = Sched(nc)
    Kk = K(S)
    TOK2 = SEQ // 4
    dr = {}

    def din(name, shape, dt=F32):
        dr[name] = nc.dram_tensor(name, list(shape), dt, kind="ExternalInput").ap()

    din("xT1", [SEQ // TB, 128, KC, TB])
    din("xT2", [TOK2 // TB, 128, KC, TB])
    din("xh", [128, KC, 2])
    din("memT", [128, KC, N_MEM])
    din("cvec", [128, NCV])
    din("w1", [D, 512])
    din("lwa", [128, 128])
    din("w2", [D, 3744])
    din("wlg", [160, 512])
    din("wpa", [512, D])
    din("wpb", [512, D])
    din("wmix", [D, D])
    din("wq", [D, D])
    din("wkv", [D, 2 * D])
    din("wxo", [D, D])
    din("wup", [D, DFF])
    din("wdn", [DFF, D])
    din("idx", [128, 4], U32)
    dr["ag_in"] = nc.dram_tensor("ag_in", [4, 128, TOK2], BF16).ap()
    dr["ag_out"] = nc.dram_tensor("ag_out", [4, 512, TOK2], BF16).ap()
    dr["ag_in_b"] = [S.buf("ag_in%d" % j) for j in range(4)]
    dr["do_ag"] = "ag" in phases
    dr["ag_out_b"] = [S.buf("ag_out%d" % j) for j in range(4)]
    dr["outT"] = nc.dram_tensor("outT", [TOK2 // TB, 128, KC, TB], F32, kind="ExternalOutput").ap()
    if debug:
        dr["dbg"] = nc.dram_tensor("dbg", [4, 512, TOK2], BF16, kind="ExternalOutput").ap()

    cv = T(S, "cv", [128, NCV], F32)
    Kk.dma("sp", cv.t[:], dr["cvec"], [], [cv.b])
    cst = build_consts(S, Kk, cv)
    idx_sb = T(S, "idx", [128, 4], U32)
    Kk.dma("sp", idx_sb.t[:], dr["idx"], [], [idx_sb.b])
    dr["idx_sb"] = idx_sb
    base_mark = S.mark()
    jobs = convert_weights(nc, S, Kk, dr) if "p2" in phases else []
    dr["wjobs"] = jobs
    if "p1" in phases:
        PB = Banks(S, 6)
        phase1(S, Kk, PB, cst, cv, dr, SEQ, cut)
    for j in jobs:
        j()
    del jobs[:]
    if debug:
        ev = Kk.dma("sp", dr["dbg"], dr["ag_out"], dr["ag_out_b"], [])
        S.final_events.append(ev)
    if "p2" in phases:
        S.barrier(keep=dr["ag_out_b"])
        S.release(base_mark)
        phase2(S, Kk, cst, cv, dr, SEQ)
    S.finish()
    return nc, S


def host_prep(inputs, SEQ):
    f = lambda a: np.ascontiguousarray(np.asarray(a, dtype=np.float32))
    x = f(inputs["x"])
    mem = f(inputs["mem"])
    TOK2 = SEQ // 4
    w_in = f(inputs["w_in"])[0]
    mu = f(inputs["mu_shift"])[0]
    maps = []
    for c in range(8):
        b, hp = c // 4, c % 4
        q = hp
        my = slice(hp * 128, (hp + 1) * 128)
        def blk(a):
            nb_ = a.shape[0] // TB
            return np.ascontiguousarray(a.reshape(nb_, TB, KC, 128).transpose(0, 3, 2, 1))
        lo = q * TOK2
        xT1 = blk(x[b])
        xT2 = blk(x[b, lo:lo + TOK2])
        xh = np.zeros((128, KC, 2), np.float32)
        if q > 0:
            xh[:] = x[b, lo - 2:lo].reshape(2, KC, 128).transpose(2, 1, 0)
        memT = np.ascontiguousarray(mem[b].reshape(N_MEM, KC, 128).transpose(2, 1, 0))
        cvec = np.zeros((128, NCV), np.float32)
        for base, key in ((C_GMIX, "norm_mix"), (C_GXA, "norm_xattn"), (C_GMLP, "norm_mlp"), (C_GMEM, "norm_mem")):
            cvec[:, base:base + 8] = f(inputs[key])[0].reshape(8, 128).T
        cvec[:, C_GFIN:C_GFIN + 8] = f(inputs["norm_final"]).reshape(8, 128).T
        cvec[:, C_MU1 + 0] = mu[0:512][my]
        cvec[:, C_MU1 + 1] = mu[512:1024][my]
        cvec[:, C_MU1 + 2] = mu[1024:1536][my]
        cvec[:, C_MU1 + 3] = mu[1536:1664]
        cvec[:, C_MUGD] = mu[1664:1792]
        cvec[0:32, C_MUGD + 1] = mu[1792:1824]
        cvec[:, C_W0] = f(inputs["w0"])[0][my]
        cvec[:, C_A0] = f(inputs["a0"])[0][my]
        cvec[:, C_KK] = f(inputs["k_k"])[0][my]
        cvec[:, C_KA] = f(inputs["k_a"])[0][my]
        cvec[:, C_RK] = f(inputs["r_k"])[0].reshape(512)[my]
        cvec[:, C_LNW] = f(inputs["ln_x_w"])[0][my]
        cvec[:, C_LNB] = f(inputs["ln_x_b"])[0][my]
        bg = f(inputs["b_gate"])[0]
        cvec[:, C_BGA:C_BGA + 8] = bg[0:1024].reshape(8, 128).T
        cvec[:, C_BGB:C_BGB + 8] = bg[1024:2048].reshape(8, 128).T
        cw = f(inputs["conv_w"])[0][:, 0, :]
        for j in range(3):
            cvec[:, C_CONV + j * 4:C_CONV + j * 4 + 4] = cw[j].reshape(4, 128).T
        w1 = np.concatenate([w_in[:, 0:512][:, my], w_in[:, 512:1024][:, my], w_in[:, 1024:1536][:, my],
                             w_in[:, 1536:1664]], axis=1)
        lwa = np.concatenate([f(inputs["w_lora_w"])[0][:, my], f(inputs["w_lora_a"])[0][:, my]], axis=0)
        idx = np.zeros((128, 4), np.uint32)
        for g in range(4):
            idx[:, g] = q * 512 + g * 128 + np.arange(128)
        maps.append({
            "xT1": xT1, "xT2": xT2, "xh": xh, "memT": memT, "cvec": cvec,
            "w1": np.ascontiguousarray(w1), "lwa": np.ascontiguousarray(lwa),
            "w2": np.ascontiguousarray(w_in[:, 1664:]), "wlg": f(inputs["w_lora_g"])[0],
            "wpa": f(inputs["w_proj_a"])[0], "wpb": f(inputs["w_proj_b"])[0], "wmix": f(inputs["w_out_mix"])[0],
            "wq": f(inputs["w_q"])[0], "wkv": f(inputs["w_kv"])[0], "wxo": f(inputs["w_xo"])[0],
            "wup": f(inputs["w_up"])[0], "wdn": f(inputs["w_down"])[0], "idx": idx,
        })
    return maps


def kernel(**inputs):
    SEQ = inputs["x"].shape[1]
    nc, S = build(SEQ)
    maps = host_prep(inputs, SEQ)
    res = run_bass_kernel_spmd(nc, maps, core_ids=list(range(8)))
    TOK2 = SEQ // 4
    out = np.zeros((2, SEQ, D), np.float32)
    for c in range(8):
        b, q = c // 4, c % 4
        o = np.asarray(res.results[c]["outT"])
        out[b, q * TOK2:(q + 1) * TOK2, :] = o.transpose(0, 3, 2, 1).reshape(TOK2, D)
    return out
```

```python
import numpy as np
import concourse.bass as bass
import concourse.mybir as mybir
from concourse.bass_utils import run_bass_kernel_spmd

F32 = mybir.dt.float32
BF16 = mybir.dt.bfloat16
U32 = mybir.dt.uint32
ALU = mybir.AluOpType
ACT = mybir.ActivationFunctionType

ENGS = ("pe", "act", "dve", "pool", "sp")

D = 1024
KC = 8
TB = 512
RMS_EPS = 1e-6
LN_X_EPS = 64e-5
N_MEM = 256
DFF = 4096

C_GMIX, C_GXA, C_GMLP, C_GFIN, C_GMEM = 0, 8, 16, 24, 32
C_MU1 = 40
C_MUGD = 44
C_W0, C_A0, C_KK, C_KA, C_RK, C_LNW, C_LNB = 46, 47, 48, 49, 50, 51, 52
C_BGA, C_BGB = 53, 61
C_CONV = 69
NCV = 81


class Buf:
    __slots__ = ("name", "w", "r")

    def __init__(self, name=""):
        self.name = name
        self.w = None
        self.r = []


class Sched:
    def __init__(self, nc):
        self.nc = nc
        self.ops = {e: [] for e in ENGS}
        self.dma_sems = []
        self.ctx = []
        self.final_events = []
        self.all_bufs = []
        self.seq = 0
        self.rec_of = {}
        self.dma_evt = {}
        self.segs = {e: [0] for e in ENGS}

    def sb(self, name, shape, dt):
        self.uid = getattr(self, "uid", 0) + 1
        name = "%s_%d" % (name, self.uid)
        cm = self.nc.sbuf_tensor(name, list(shape), dt)
        t = cm.__enter__()
        self.ctx.append(cm)
        return t

    def ps(self, name, shape, dt=F32):
        self.uid = getattr(self, "uid", 0) + 1
        name = "%s_%d" % (name, self.uid)
        cm = self.nc.psum_tensor(name, list(shape), dt)
        t = cm.__enter__()
        self.ctx.append(cm)
        return t

    def mark(self):
        return len(self.ctx)

    def release(self, mark):
        while len(self.ctx) > mark:
            self.ctx.pop().__exit__(None, None, None)

    def buf(self, name=""):
        b = Buf(name)
        self.all_bufs.append(b)
        return b

    def _deps(self, reads, writes):
        waits = []
        for b in reads:
            if b.w is not None:
                waits.append(b.w)
        for b in writes:
            if b.w is not None:
                waits.append(b.w)
            waits.extend(b.r)
        return waits

    def op(self, eng, fn, reads=(), writes=(), cost=0.5):
        waits = self._deps(reads, writes)
        idx = len(self.ops[eng])
        self.seq += 1
        rec = {"fn": fn, "waits": waits, "sig": False, "dma": None, "seq": self.seq, "cost": cost, "eng": eng}
        self.ops[eng].append(rec)
        ev = ("e", eng, idx)
        self.rec_of[ev] = rec
        for b in reads:
            b.r.append(ev)
        for b in writes:
            b.w = ev
            b.r = []
        return ev

    def dma(self, eng, fn, reads=(), writes=(), sem=None, inc=16, cost=3.0):
        waits = self._deps(reads, writes)
        if sem is None:
            self.dma_sems.append({"h": None, "total": 0})
            sem = len(self.dma_sems) - 1
        s = self.dma_sems[sem]
        s["total"] += inc
        self.seq += 1
        rec = {"fn": fn, "waits": waits, "sig": False, "dma": sem, "inc": inc, "seq": self.seq, "cost": cost,
               "eng": eng}
        self.ops[eng].append(rec)
        ev = ("d", sem, s["total"])
        self.dma_evt[ev] = rec
        for b in reads:
            b.r.append(ev)
        for b in writes:
            b.w = ev
            b.r = []
        return ev

    def new_sem(self):
        self.dma_sems.append({"h": None, "total": 0})
        return len(self.dma_sems) - 1

    def barrier(self, keep=()):
        keep_ids = set(id(b) for b in keep)
        keep_evs = set(b.w for b in keep if b.w is not None)
        evs = []
        for e in ENGS:
            n = len(self.ops[e])
            for i in range(n - 1, -1, -1):
                if self.ops[e][i]["dma"] is None and self.ops[e][i]["fn"] is not None:
                    evs.append(("e", e, i))
                    break
        keep_sems = set(ev[1] for ev in keep_evs if ev[0] == "d")
        for i, s in enumerate(self.dma_sems):
            if s["total"] > 0 and i not in keep_sems:
                evs.append(("d", i, s["total"]))
        for e in ENGS:
            self.seq += 1
            self.ops[e].append({"fn": None, "waits": list(evs), "sig": False, "dma": None, "seq": self.seq,
                                "cost": 0.0, "eng": e})
            self.segs[e].append(len(self.ops[e]))
        for b in self.all_bufs:
            if id(b) in keep_ids:
                continue
            b.w = None
            b.r = []

    LAT_X = 1.4
    LAT_S = 0.6

    def schedule(self):
        import heapq
        ops = self.ops
        nseg = len(self.segs[ENGS[0]])
        new_ops = {e: [] for e in ENGS}
        for si in range(nseg):
            seg = []
            for e in ENGS:
                lo = self.segs[e][si]
                hi = self.segs[e][si + 1] if si + 1 < nseg else len(ops[e])
                seg.extend(ops[e][lo:hi])
            inseg = set(id(r) for r in seg)
            barrier_recs = [r for r in seg if r["fn"] is None]
            body = [r for r in seg if r["fn"] is not None]
            succ = {id(r): [] for r in body}
            ndep = {}
            for r in body:
                deps = []
                for w in r["waits"]:
                    p = self.rec_of.get(w) if w[0] == "e" else self.dma_evt.get(w)
                    if p is not None and id(p) in inseg and p["fn"] is not None:
                        deps.append(p)
                r["_deps"] = deps
                ndep[id(r)] = len(deps)
                for p in deps:
                    succ[id(p)].append(r)
            prio = {}
            if getattr(self, "cp_prio", True):
                for r in sorted(body, key=lambda r_: -r_["seq"]):
                    best_ = 0.0
                    for q in succ[id(r)]:
                        lat = self.LAT_S if (q["eng"] == r["eng"] and r["dma"] is None) else self.LAT_X
                        if r["eng"] == "pe" and q["eng"] == "pe":
                            lat = 0.0
                        v_ = lat + prio[id(q)]
                        if v_ > best_:
                            best_ = v_
                    prio[id(r)] = r["cost"] + best_
                for r in body:
                    r["_key"] = -prio[id(r)]
            else:
                for r in body:
                    r["_key"] = r["seq"]
            free = {e: 0.0 for e in ENGS}
            pend = {e: [] for e in ENGS}
            avail = {e: [] for e in ENGS}
            fin = {}
            for r in body:
                rt0 = 0.0
                for w in r["waits"]:
                    if w[0] == "d":
                        p = self.dma_evt.get(w)
                        if p is not None and id(p) not in inseg and p.get("inc") == 1:
                            rt0 = 150.0
                r["_rt0"] = rt0
                if ndep[id(r)] == 0:
                    heapq.heappush(pend[r["eng"]], (rt0, r["seq"], id(r), r))
            order = {e: [] for e in ENGS}
            left = len(body)
            while left:
                best = None
                for e in ENGS:
                    pe_, av = pend[e], avail[e]
                    while pe_ and pe_[0][0] <= free[e]:
                        t_, sq, _i, r = heapq.heappop(pe_)
                        heapq.heappush(av, (r["_key"], _i, r))
                    if av:
                        cand = (free[e], av[0][0], e, True)
                    elif pe_:
                        cand = (pe_[0][0], pe_[0][1], e, False)
                    else:
                        continue
                    if best is None or cand < best:
                        best = cand
                start, _sq, e, from_av = best
                if from_av:
                    _sq, _i, r = heapq.heappop(avail[e])
                else:
                    _t, _sq, _i, r = heapq.heappop(pend[e])
                order[e].append(r)
                r["_start"] = start
                r["_engprev"] = order[e][-2] if len(order[e]) > 1 else None
                left -= 1
                if r["dma"] is not None:
                    issue = 1.0 if e == "pool" else 0.08
                    free[e] = start + issue
                    done = start + r["cost"]
                else:
                    free[e] = start + r["cost"]
                    done = free[e]
                fin[id(r)] = done
                for q in succ[id(r)]:
                    ndep[id(q)] -= 1
                    if ndep[id(q)] == 0:
                        rt = q["_rt0"]
                        for p in q["_deps"]:
                            lat = self.LAT_S if (p["eng"] == q["eng"] and p["dma"] is None) else self.LAT_X
                            if p["eng"] == "pe" and q["eng"] == "pe":
                                lat = 0.0
                            rt = max(rt, fin[id(p)] + lat)
                        q["_ready"] = rt
                        heapq.heappush(pend[q["eng"]], (rt, q["seq"], id(q), q))
            self.sim_span = getattr(self, "sim_span", []) + [max(fin.values()) if fin else 0.0]
            self.sim_fin = getattr(self, "sim_fin", {})
            self.sim_fin.update(fin)
            for e in ENGS:
                new_ops[e].extend(order[e])
                new_ops[e].extend([r for r in barrier_recs if r["eng"] == e])
        for e in ENGS:
            assert len(new_ops[e]) == len(ops[e])
            ops[e][:] = new_ops[e]

    def finish(self, final_wait_eng="sp"):
        nc = self.nc
        ops = self.ops
        rec_of = self.rec_of
        if getattr(self, "do_schedule", True):
            self.schedule()
        for w in self.final_events:
            if w[0] == "e":
                rec_of[w]["sig"] = True
        for e in ENGS:
            for rec in ops[e]:
                for w in rec["waits"]:
                    if w[0] == "e":
                        rec_of[w]["sig"] = True
        for e in ENGS:
            c = 0
            for rec in ops[e]:
                if rec["sig"]:
                    c += 1
                rec["count"] = c
        sem_cms = []
        esem = {}
        for e in ENGS:
            cm = nc.semaphore("s_" + e)
            esem[e] = cm.__enter__()
            sem_cms.append(cm)
        for i, s in enumerate(self.dma_sems):
            cm = nc.semaphore("d_%d" % i)
            s["h"] = cm.__enter__()
            sem_cms.append(cm)
        dma_sems = self.dma_sems
        final_events = self.final_events
        nwaits = {e: 0 for e in ENGS}

        def emit(e, eng):
            known = {}
            for rec in ops[e]:
                need = {}
                for w in rec["waits"]:
                    if w[0] == "e":
                        if w[1] == e and e == "pe":
                            continue
                        key = ("e", w[1])
                        val = rec_of[w]["count"]
                    else:
                        key = ("d", w[1])
                        val = w[2]
                    if known.get(key, 0) >= val:
                        continue
                    if need.get(key, 0) < val:
                        need[key] = val
                for key, val in need.items():
                    h = esem[key[1]] if key[0] == "e" else dma_sems[key[1]]["h"]
                    eng.wait_ge(h, val)
                    known[key] = val
                    nwaits[e] += 1
                if rec["fn"] is None:
                    continue
                ins = rec["fn"](eng)
                if rec["dma"] is not None:
                    ins.then_inc(dma_sems[rec["dma"]]["h"], rec.get("inc", 16))
                elif rec["sig"]:
                    ins.then_inc(esem[e], 1)
            if e == final_wait_eng:
                for w in final_events:
                    if w[0] == "e":
                        eng.wait_ge(esem[w[1]], rec_of[w]["count"])
                    else:
                        eng.wait_ge(dma_sems[w[1]]["h"], w[2])

        with nc.Block() as block:
            @block.sync
            def _(eng):
                emit("sp", eng)

            @block.tensor
            def _(eng):
                emit("pe", eng)

            @block.scalar
            def _(eng):
                emit("act", eng)

            @block.vector
            def _(eng):
                emit("dve", eng)

            @block.gpsimd
            def _(eng):
                emit("pool", eng)
        self.stats = {e: (len(ops[e]), nwaits[e]) for e in ENGS}
        for cm in reversed(sem_cms):
            cm.__exit__(None, None, None)
        self.release(0)


class T:
    def __init__(self, S, name, shape, dt, nb=1, psum=False):
        self.t = S.ps(name, shape, dt) if psum else S.sb(name, shape, dt)
        self.bs = [S.buf(name + str(i)) for i in range(nb)]
        self.b = self.bs[0]


class K:
    def __init__(self, S):
        self.S = S

    @staticmethod
    def _c(eng, ap):
        n = 1
        for d in ap.shape[1:]:
            n *= d
        if eng == "pe":
            return 0.05 + 0.00055 * n
        if eng == "act":
            return 0.2 + 0.0009 * n
        if eng == "dve":
            return 0.12 + 0.0011 * n
        return 0.15 + 0.002 * n

    def tt(self, eng, out, in0, in1, op, R, W):
        return self.S.op(eng, lambda e: e.tensor_tensor(out, in0, in1, op), R, W, cost=self._c(eng, out))

    def ts(self, eng, out, in0, s1, s2, op0, op1, R, W):
        c = self._c(eng, out) * (5.0 if eng == "pool" else 1.0)
        if s2 is None:
            return self.S.op(eng, lambda e: e.tensor_scalar(out, in0, s1, None, op0), R, W, cost=c)
        return self.S.op(eng, lambda e: e.tensor_scalar(out, in0, s1, s2, op0, op1), R, W, cost=c)

    def stt(self, out, in0, sc, in1, op0, op1, R, W):
        return self.S.op("dve", lambda e: e.scalar_tensor_tensor(out, in0, sc, in1, op0, op1), R, W,
                         cost=self._c("dve", out))

    def act(self, out, in_, func, R, W, bias=None, scale=None, eng="act"):
        kw = {}
        if bias is not None:
            kw["bias"] = bias
        if scale is not None:
            kw["scale"] = scale
        return self.S.op(eng, lambda e: e.activation(out, in_, func, **kw), R, W, cost=self._c("act", out))

    def cp(self, eng, out, in_, R, W):
        if eng == "act":
            return self.S.op("act", lambda e: e.activation(out, in_, ACT.Copy), R, W, cost=self._c("act", out))
        return self.S.op(eng, lambda e: e.tensor_copy(out, in_), R, W, cost=self._c(eng, out))

    def recip(self, out, in_, R, W):
        return self.S.op("dve", lambda e: e.reciprocal(out, in_), R, W, cost=6 * self._c("dve", out))

    def mm(self, out, lhsT, rhs, R, W, start=True, stop=True, tp=None):
        c = self._c("pe", rhs)
        if tp is None:
            return self.S.op("pe", lambda e: e.matmul(out, lhsT, rhs, start=start, stop=stop), R, W, cost=c)
        return self.S.op("pe", lambda e: e.matmul(out, lhsT, rhs, start=start, stop=stop,
                                                   tile_position=tp), R, W, cost=c)

    def tr(self, out, in_, ident, R, W):
        return self.S.op("pe", lambda e: e.transpose(out, in_, ident), R, W, cost=0.12)

    def memset(self, eng, out, val, W):
        return self.S.op(eng, lambda e: e.memset(out, val), (), W, cost=self._c(eng, out))

    def dma(self, eng, out, in_, R, W, sem=None):
        nb_ = 1
        for d in out.shape:
            nb_ *= d
        return self.S.dma(eng, lambda e: e.dma_start(out=out, in_=in_), R, W, sem=sem, cost=2.5 + nb_ * 2e-5)

    def tsem(self, t):
        if not hasattr(t, "sem"):
            t.sem = self.S.new_sem()
        return t.sem


class Banks:
    def __init__(self, S, n):
        self.banks = [T(S, "pb%d" % i, [128, 512], F32, psum=True) for i in range(n)]
        self.i = 0

    def next(self):
        n = len(self.banks)
        for k in range(n):
            b = self.banks[(self.i + k) % n]
            if b.b.w is None or b.b.w[0] != "e" or b.b.w[1] != "pe":
                self.i = (self.i + k + 1) % n
                return b
        raise RuntimeError("no free PSUM bank (a bank is held across a pipeline yield)")


def v3(ap, c):
    return ap.rearrange("p (c t) -> p c t", c=c)


def build_consts(S, Kk, cv):
    c = {}
    ones_bf = T(S, "ones_bf", [128, 128], BF16)
    Kk.memset("pool", ones_bf.t[:], 1.0, [ones_bf.b])
    c["ones_bf"] = ones_bf
    ones32 = T(S, "ones32", [128, 64], F32)
    Kk.memset("pool", ones32.t[:], 1.0, [ones32.b])
    c["ones32"] = ones32
    for name, val in (("bones", 1.0), ("bmean", 1.0 / 64.0)):
        t = T(S, name, [128, 128], BF16)
        Kk.memset("pool", t.t[:], 0.0, [t.b])
        Kk.memset("pool", t.t[0:64, 0:64], val, [t.b])
        Kk.memset("pool", t.t[64:128, 64:128], val, [t.b])
        c[name] = t
    ident = T(S, "ident64", [128, 64], BF16)
    Kk.memset("pool", ident.t[:], 1.0, [ident.b])
    for h in range(2):
        sl = ident.t[h * 64:(h + 1) * 64, :]
        S.op("pool", lambda e, sl=sl: e.affine_select(sl, sl, [[-1, 64]], ALU.is_equal, 0.0,
                                                       base=0, channel_multiplier=1),
             [ident.b], [ident.b])
    c["ident64"] = ident
    m4 = T(S, "mask4", [128, 4, 128], BF16)
    mn4 = T(S, "maskn4", [128, 4, 128], BF16)
    Kk.memset("pool", m4.t[:], 0.0, [m4.b])
    Kk.memset("pool", mn4.t[:], 0.0, [mn4.b])
    for blk in range(2):
        ps = slice(blk * 64, (blk + 1) * 64)
        for q in range(4):
            sl = m4.t[ps, q, blk * 64:(blk + 1) * 64]
            Kk.memset("pool", sl, 1.0, [m4.b])
            cmp = ALU.is_gt if q % 2 == 0 else ALU.is_ge
            S.op("pool", lambda e, sl=sl, cmp=cmp: e.affine_select(sl, sl, [[1, 64]], cmp, 0.0, base=0,
                                                                    channel_multiplier=-1),
                 [m4.b], [m4.b])
            sl2 = mn4.t[ps, q, blk * 64:(blk + 1) * 64]
            Kk.memset("pool", sl2, 1.0, [mn4.b])
            S.op("pool", lambda e, sl2=sl2: e.affine_select(sl2, sl2, [[-1, 64]], ALU.is_gt, 0.0, base=0,
                                                             channel_multiplier=1),
                 [mn4.b], [mn4.b])
    c["mask4"] = m4
    c["maskn4"] = mn4
    omka = T(S, "omka", [128, 1], F32)
    Kk.ts("pool", omka.t[:], cv.t[:, C_KA:C_KA + 1], -1.0, 1.0, ALU.mult, ALU.add, [cv.b], [omka.b])
    c["omka"] = omka
    return c


def phase1(S, Kk, PB, cst, cv, dr, SEQ, cut=None, depth_off=6):
    nblk = SEQ // TB
    xT1, w1, lwa, ag_in = dr["xT1"], dr["w1"], dr["lwa"], dr["ag_in"]
    ones_bf, ones32, bones, bmean = cst["ones_bf"], cst["ones32"], cst["bones"], cst["bmean"]
    ident, m4, mn4, omka = cst["ident64"], cst["mask4"], cst["maskn4"], cst["omka"]
    MUL, ADD, SUB, MAX = ALU.mult, ALU.add, ALU.subtract, ALU.max

    def col(i):
        return cv.t[:, i:i + 1]

    w1b = T(S, "w1b", [128, KC, 512], BF16)
    Kk.dma("pool", w1b.t[:], w1.rearrange("(kc p) n -> p kc n", p=128), [], [w1b.b])
    lwab = T(S, "lwab", [128, 128], BF16)
    Kk.dma("pool", lwab.t[:], lwa, [], [lwab.b])

    def f32t(name):
        return T(S, name, [128, TB], F32)

    def bft(name):
        return T(S, name, [128, TB], BF16)

    xf = [T(S, "xf%d" % i, [128, KC, TB], F32) for i in range(2)]
    xn = T(S, "xn", [128, KC, TB], BF16, nb=KC)
    xsq = [bft("xsq%d" % i) for i in range(3)]
    P = T(S, "P", [128, 4, TB + 1], F32)
    Kk.memset("pool", P.t[:, :, TB:TB + 1], 0.0, [P.b])
    dsc = [f32t("dsc%d" % i) for i in range(2)]
    RING = dr.get("ring", ("pm", "a_t", "Em", "Eend", "kk", "beta"))
    pm_l = [T(S, "pm%d" % i, [128, 4, TB], F32, nb=4) for i in range(2 if "pm" in RING else 1)]
    scr = {}
    for nme in ("sqt", "rstd", "sg", "cs", "Em", "Eex", "Eend", "a_t", "kk", "sq2", "beta"):
        scr[nme] = [f32t(nme + str(i)) for i in range(2 if nme in RING else 1)]
    Ep2 = [f32t("Ep%d" % i) for i in range(4)]
    bonus2 = [f32t("bonus%d" % i) for i in range(4)]
    la, kk2, rk, BT, KT, BH, KH, vb = [bft(n) for n in ("la", "kk2", "rk", "BT", "KT", "BH", "KH", "vb")]
    AR2 = [T(S, "AR%d" % i, [128, 4, 2, 128], BF16) for i in range(2)]
    PTr = [T(S, "ptr%d" % i, [128, 1024], BF16, psum=True) for i in range(2)]
    TM2 = [T(S, "TM%d" % i, [128, 2, 4, 4, 64], BF16) for i in range(2)]
    AM2 = [T(S, "AM%d" % i, [128, 8, 5, 128], BF16, nb=8) for i in range(2)]
    Y02 = [T(S, "Y0_%d" % i, [128, 8, 128], BF16) for i in range(2)]
    Xl = [T(S, "Xl%d" % i, [128, 8, 128], BF16) for i in range(2)]
    Zl = [T(S, "Zl%d" % i, [128, 8, 128], BF16) for i in range(2)]
    Yl = [T(S, "Yl%d" % i, [128, 8, 128], BF16) for i in range(2)]
    PTs = T(S, "PTs", [128, 4, 2, 128], BF16)
    Kk.memset("pool", PTs.t[:], 0.0, [PTs.b])
    Qs = T(S, "Qs", [128, 4, 2, 64], F32)
    GT = bft("GT")
    YI = f32t("YI")
    Tbf = [T(S, "Tbf%d" % i, [128, 9, 64], BF16, nb=9) for i in range(2)]
    T32 = [T(S, "T32_%d" % i, [128, 64], F32) for i in range(2)]
    tmpst = T(S, "tmpst", [128, 64], F32)
    Kk.memset("pool", Tbf[0].t[:, 0, :], 0.0, [Tbf[0].bs[0]])
    Kk.memset("pool", T32[0].t[:], 0.0, [T32[0].b])
    y32, yc, sd = [f32t(n) for n in ("y32", "yc", "sd")]
    yb, yc2 = bft("yb"), bft("yc2")
    yout = [bft("yout%d" % i) for i in range(2)]
    xsq_i = [0]
    nce = T(S, "nce", [128, 8, 1], F32)
    C0 = 0.6065306597126334

    def block(n):
        par = n % 2
        X = xf[n % 2]
        Ep, bonus, AR, TM, AM, Y0 = Ep2[n % 4], bonus2[n % 4], AR2[par], TM2[par], AM2[par], Y02[par]
        pm = pm_l[n % len(pm_l)]
        sqt, rstd, sg, cs, Em, Eex, Eend, a_t, kk, sq2, beta = [scr[k_][n % len(scr[k_])] for k_ in (
            "sqt", "rstd", "sg", "cs", "Em", "Eex", "Eend", "a_t", "kk", "sq2", "beta")]
        Kk.dma("sp", X.t[:], xT1[n], [], [X.b], sem=Kk.tsem(X))
        yield
        st = PB.next()
        for kc in range(KC):
            Kk.act(xn.t[:, kc, :], X.t[:, kc, :], ACT.Copy, [X.b, cv.b], [xn.bs[kc]], scale=col(C_GMIX + kc))
            q_ = xsq[xsq_i[0] % 3]
            xsq_i[0] += 1
            Kk.act(q_.t[:], X.t[:, kc, :], ACT.Square, [X.b], [q_.b])
            Kk.mm(st.t[:], ones_bf.t[:], q_.t[:], [ones_bf.b, q_.b], [st.b], start=(kc == 0), stop=(kc == KC - 1))
        pp = [PB.next() for _ in range(4)]
        for ct in range(4):
            for kc in range(KC):
                Kk.mm(pp[ct].t[:], w1b.t[:, kc, ct * 128:(ct + 1) * 128], xn.t[:, kc, :],
                      [w1b.b, xn.bs[kc]], [pp[ct].b], start=(kc == 0), stop=(kc == KC - 1))
        Kk.act(sqt.t[:], st.t[:], ACT.Ln, [], [st.b, sqt.b], bias=RMS_EPS, scale=1.0 / D)
        tick = S.buf("tick%d" % n)
        Kk.act(rstd.t[:], sqt.t[:], ACT.Exp, [sqt.b], [rstd.b, tick], scale=-0.5)
        Kk.cp("pool", P.t[:, :, 0:1], P.t[:, :, TB:TB + 1], [], [P.b])
        for ct in range(4):
            Kk.tt("dve", P.t[:, ct, 1:TB + 1], pp[ct].t[:], rstd.t[:], MUL, [rstd.b], [pp[ct].b, P.b])
        yield
        for ct in range(4):
            d_ = dsc[ct % 2]
            Kk.tt("pool", d_.t[:], P.t[:, ct, 0:TB], P.t[:, ct, 1:TB + 1], SUB, [P.b], [d_.b])
            Kk.stt(pm.t[:, ct, :], d_.t[:], col(C_MU1 + ct), P.t[:, ct, 1:TB + 1], MUL, ADD,
                   [d_.b, P.b, cv.b], [pm.bs[ct]])
        rP, kP, vP, wP = (pm.t[:, i, :] for i in range(4))
        rB, kB, vB, wB = pm.bs
        wj = dr.get("wjobs", [])
        per = (len(wj) + max(1, nblk - n) - 1) // max(1, nblk - n) if wj else 0
        for _ in range(min(per, len(wj))):
            wj.pop(0)(tick)
        yield
        Kk.act(la.t[0:64, :], pm.t[0:64, 3, :], ACT.Tanh, [wB], [la.b])
        Kk.cp("pool", la.t[64:128, :], pm.t[64:128, 3, :], [wB], [la.b])
        zw, za = PB.next(), PB.next()
        Kk.mm(zw.t[:], lwab.t[0:64, :], la.t[0:64, :], [lwab.b, la.b], [zw.b])
        Kk.mm(za.t[:], lwab.t[64:128, :], la.t[64:128, :], [lwab.b, la.b], [za.b], tp=(64, 0))
        Kk.act(sg.t[:], zw.t[:], ACT.Sigmoid, [cv.b], [zw.b, sg.b], bias=col(C_W0))
        Kk.act(a_t.t[:], za.t[:], ACT.Sigmoid, [cv.b], [za.b, a_t.b], bias=col(C_A0))
        ld = sg
        yield
        for c in range(8):
            sl = slice(c * 64, (c + 1) * 64)
            S.op("dve", lambda e, sl=sl: e.tensor_tensor_scan(cs.t[:, sl], ones32.t[:, :], ld.t[:, sl], 0.0,
                                                               MUL, ADD),
                 [ones32.b, ld.b], [cs.b], cost=0.25)
        Kk.act(Ep.t[:], cs.t[:], ACT.Exp, [cs.b], [Ep.b], scale=-C0)
        Kk.act(Em.t[:], cs.t[:], ACT.Exp, [cs.b], [Em.b], scale=C0)
        Kk.tt("pool", Eex.t[:], cs.t[:], ld.t[:], SUB, [cs.b, ld.b], [Eex.b])
        Kk.act(Eex.t[:], Eex.t[:], ACT.Exp, [], [Eex.b], scale=-C0)
        Kk.act(nce.t[:], v3(cs.t[:], 8)[:, :, 63:64], ACT.Copy, [cs.b], [nce.b], scale=-C0)
        for c in range(8):
            sl = slice(c * 64, (c + 1) * 64)
            Kk.act(Eend.t[:, sl], cs.t[:, sl], ACT.Exp, [cs.b, nce.b], [Eend.b], scale=C0, bias=nce.t[:, c, :])
        yield
        Kk.act(kk.t[:], kP, ACT.Copy, [kB, cv.b], [kk.b], scale=col(C_KK))
        Kk.act(kk2.t[:], kP, ACT.Square, [kB, cv.b], [kk2.b], scale=col(C_KK))
        ssb = PB.next()
        Kk.mm(ssb.t[:], bones.t[:], kk2.t[:], [bones.b, kk2.b], [ssb.b])
        Kk.ts("dve", sq2.t[:], ssb.t[:], 1e-19, None, MAX, None, [], [ssb.b, sq2.b])
        Kk.act(sq2.t[:], sq2.t[:], ACT.Ln, [], [sq2.b])
        Kk.act(sq2.t[:], sq2.t[:], ACT.Exp, [], [sq2.b], scale=-0.5)
        Kk.tt("dve", kk.t[:], kk.t[:], sq2.t[:], MUL, [sq2.b], [kk.b])
        kkn = kk
        Kk.tt("pool", beta.t[:], kkn.t[:], a_t.t[:], MUL, [kkn.b, a_t.b], [beta.b])
        Kk.act(a_t.t[:], a_t.t[:], ACT.Identity, [cv.b, omka.b], [a_t.b], scale=col(C_KA), bias=omka.t[:, 0:1])
        Kk.tt("pool", a_t.t[:], kP, a_t.t[:], MUL, [kB], [a_t.b])
        kmod = a_t
        yield
        Kk.stt(AR.t[:, :, 0, :], v3(kkn.t[:], 4), -1.0, v3(Eex.t[:], 4), MUL, MUL, [kkn.b, Eex.b], [AR.b])
        Kk.tt("pool", AR.t[:, :, 1, :], v3(rP, 4), v3(Ep.t[:], 4), MUL, [rB, Ep.b], [AR.b])
        Kk.tt("pool", BT.t[:], beta.t[:], Em.t[:], MUL, [beta.b, Em.b], [BT.b])
        Kk.tt("dve", KT.t[:], kmod.t[:], Em.t[:], MUL, [kmod.b, Em.b], [KT.b])
        Kk.tt("pool", BH.t[:], beta.t[:], Eend.t[:], MUL, [beta.b, Eend.b], [BH.b])
        Kk.tt("dve", KH.t[:], kmod.t[:], Eend.t[:], MUL, [kmod.b, Eend.b], [KH.b])
        Kk.cp("act", vb.t[:], vP, [vB], [vb.b])
        Kk.act(sqt.t[:], rP, ACT.Copy, [rB, cv.b], [sqt.b], scale=col(C_RK))
        Kk.tt("pool", rk.t[:], sqt.t[:], kmod.t[:], MUL, [sqt.b, kmod.b], [rk.b])
        bsb = PB.next()
        Kk.mm(bsb.t[:], bones.t[:], rk.t[:], [bones.b, rk.b], [bsb.b])
        Kk.tt("dve", bonus.t[:], bsb.t[:], vP, MUL, [vB], [bsb.b, bonus.b])
        yield
        srcs = [(lambda cp: AR.t[:, cp, 0, :], AR.b), (lambda cp: BH.t[:, cp * 128:(cp + 1) * 128], BH.b),
                (lambda cp: KH.t[:, cp * 128:(cp + 1) * 128], KH.b),
                (lambda cp: vb.t[:, cp * 128:(cp + 1) * 128], vb.b)]
        for h in range(2):
            hs = slice(h * 64, (h + 1) * 64)
            for q, (sf, sbuf) in enumerate(srcs):
                for cp in range(4):
                    o0 = (q * 4 + cp) * 64
                    Kk.tr(PTr[h].t[:, o0:o0 + 64], sf(cp)[hs, :], ident.t[hs, :], [sbuf, ident.b], [PTr[h].b])
        Kk.cp("act", TM.t[:, 0, :, :, :].rearrange("p a b c -> p (a b c)"), PTr[0].t[:], [], [PTr[0].b, TM.b])
        Kk.cp("dve", TM.t[:, 1, :, :, :].rearrange("p a b c -> p (a b c)"), PTr[1].t[:], [], [PTr[1].b, TM.b])
        yield
        for h in range(2):
            hs = slice(h * 64, (h + 1) * 64)
            for cp in range(4):
                p = h * 4 + cp
                ts_ = slice(cp * 128, (cp + 1) * 128)
                bk = PB.next()
                arf = AR.t[hs, cp, :, :].rearrange("p a b -> p (a b)")
                Kk.mm(bk.t[:, 0:256], BT.t[hs, ts_], arf, [BT.b, AR.b], [bk.b], tp=(h * 64, 0))
                Kk.mm(bk.t[:, 256:512], KT.t[hs, ts_], arf, [KT.b, AR.b], [bk.b], tp=(h * 64, 0))
                Kk.tt("dve", AM.t[:, p, 0:4, :], v3(bk.t[:], 4), m4.t[:], MUL, [m4.b], [bk.b, AM.bs[p]])
            bkn = PB.next()
            for cp in range(4):
                ts_ = slice(cp * 128, (cp + 1) * 128)
                Kk.mm(bkn.t[:, ts_], AR.t[hs, cp, 0, :], BT.t[hs, ts_], [AR.b, BT.b], [bkn.b], tp=(h * 64, 0))
            Kk.tt("dve", AM.t[:, h * 4:(h + 1) * 4, 4, :], v3(bkn.t[:], 4), mn4.t[:], MUL, [mn4.b],
                  [bkn.b] + AM.bs[h * 4:(h + 1) * 4])
        yield
        bv = PB.next()
        for p in range(8):
            Kk.mm(bv.t[:, p * 64:(p + 1) * 64], AM.t[:, p, 2, :], TM.t[:, p // 4, 3, p % 4, :], [AM.bs[p], TM.b],
                  [bv.b])
        for h in range(2):
            wa = slice(h * 64, (h + 1) * 64)
            uv = slice((1 - h) * 64, (2 - h) * 64)
            Kk.cp("act", Y0.t[:, h * 4:(h + 1) * 4, wa], TM.t[:, h, 0, :, :], [TM.b], [Y0.b])
            Kk.cp("dve", Y0.t[:, h * 4:(h + 1) * 4, uv], v3(bv.t[:], 8)[:, h * 4:(h + 1) * 4, :], [], [bv.b, Y0.b])
        yield
        Yc = Y0
        for k in range(6):
            if k == 0:
                Xk = lambda p: AM.t[:, p, 0, :]
                Zk = lambda p: AM.t[:, p, 4, :]
                XkB = lambda p: [AM.bs[p]]
                ZkB = XkB
            else:
                Xt, Zt = Xl[k % 2], Zl[k % 2]
                Xk = lambda p, Xt=Xt: Xt.t[:, p, :]
                Zk = lambda p, Zt=Zt: Zt.t[:, p, :]
                XkB = lambda p, Xt=Xt: [Xt.b]
                ZkB = lambda p, Zt=Zt: [Zt.b]
            Yn = Yl[k % 2]
            if k < 5:
                Xn_, Zn_ = Xl[(k + 1) % 2], Zl[(k + 1) % 2]
                bx = [PB.next(), PB.next()]
                for p in range(8):
                    bb = bx[p // 4]
                    Kk.mm(bb.t[:, (p % 4) * 128:(p % 4 + 1) * 128], Zk(p), Xk(p), ZkB(p) + XkB(p), [bb.b])
                for i in range(2):
                    Kk.cp("act", Xn_.t[:, i * 4:(i + 1) * 4, :], v3(bx[i].t[:], 4), [], [bx[i].b, Xn_.b])
            b01 = [PB.next(), PB.next()]
            for p in range(8):
                bb = b01[p // 4]
                Kk.mm(bb.t[:, (p % 4) * 128:(p % 4 + 1) * 128], Xk(p), Yc.t[:, p, :], XkB(p) + [Yc.b], [bb.b])
            for i in range(2):
                Kk.tt("dve", Yn.t[:, i * 4:(i + 1) * 4, :], v3(b01[i].t[:], 4), Yc.t[:, i * 4:(i + 1) * 4, :], ADD,
                      [Yc.b], [b01[i].b, Yn.b])
            if k < 4:
                bz = [PB.next(), PB.next()]
                for p in range(8):
                    bb = bz[p // 4]
                    Kk.mm(bb.t[:, (p % 4) * 128:(p % 4 + 1) * 128], Xk(p), Zk(p), ZkB(p) + XkB(p), [bb.b])
                for i in range(2):
                    Kk.cp("act" if i == 0 else "dve", Zn_.t[:, i * 4:(i + 1) * 4, :], v3(bz[i].t[:], 4),
                          [], [bz[i].b, Zn_.b])
            Yc = Yn
            yield
        XF = Yc
        bq = [PB.next(), PB.next()]
        for e_ in range(2):
            rows = slice(e_ * 64, (e_ + 1) * 64)
            bb = bq[e_]
            for cp in range(4):
                c0 = cp * 128
                for h in range(2):
                    p = h * 4 + cp
                    off = h * 64
                    wa = slice(h * 64, (h + 1) * 64)
                    uv = slice((1 - h) * 64, (2 - h) * 64)
                    Kk.mm(bb.t[off:off + 64, c0:c0 + 64], XF.t[rows, p, wa], TM.t[rows, h, 1, cp, :], [XF.b, TM.b],
                          [bb.b], tp=(e_ * 64, off))
                    Kk.mm(bb.t[off:off + 64, c0 + 64:c0 + 128], TM.t[rows, h, 1, cp, :], XF.t[rows, p, uv],
                          [XF.b, TM.b], [bb.b], start=True, stop=False, tp=(e_ * 64, off))
                    Kk.mm(bb.t[off:off + 64, c0 + 64:c0 + 128], TM.t[rows, h, 2, cp, :], TM.t[rows, h, 3, cp, :],
                          [TM.b], [bb.b], start=False, stop=True, tp=(e_ * 64, off))
        for e_ in range(2):
            for h in range(2):
                hs = slice(h * 64, (h + 1) * 64)
                Kk.cp("act", PTs.t[hs, :, e_, hs], v3(bq[e_].t[:], 4)[hs, :, 0:64], [], [bq[e_].b, PTs.b])
            Kk.cp("dve", Qs.t[:, :, e_, :], v3(bq[e_].t[:], 4)[:, :, 64:128], [], [bq[e_].b, Qs.b])
        yield
        bg, byi = PB.next(), PB.next()
        for h in range(2):
            off = h * 64
            wa = slice(h * 64, (h + 1) * 64)
            uv = slice((1 - h) * 64, (2 - h) * 64)
            for cp in range(4):
                p = h * 4 + cp
                ts_ = slice(cp * 128, (cp + 1) * 128)
                Kk.mm(bg.t[off:off + 64, ts_], XF.t[:, p, wa], AM.t[:, p, 1, :], [XF.b, AM.bs[p]], [bg.b],
                      tp=(0, off))
                Kk.mm(byi.t[off:off + 64, ts_], XF.t[:, p, uv], AM.t[:, p, 1, :], [XF.b, AM.bs[p]], [byi.b],
                      start=True, stop=False, tp=(0, off))
                Kk.mm(byi.t[off:off + 64, ts_], TM.t[:, h, 3, cp, :], AM.t[:, p, 3, :], [TM.b, AM.bs[p]], [byi.b],
                      start=False, stop=True, tp=(0, off))
        Kk.tt("dve", v3(GT.t[:], 4), v3(bg.t[:], 4), AR.t[:, :, 1, :], ADD, [AR.b], [bg.b, GT.b])
        Kk.cp("act", YI.t[:], byi.t[:], [], [byi.b, YI.b])
        yield
        TB_ = Tbf[par]
        TBn = Tbf[1 - par]
        for c in range(8):
            gc = n * 8 + c
            Tc, Tn = T32[gc % 2], T32[(gc + 1) % 2]
            bst = PB.next()
            Kk.mm(bst.t[:, 0:64], PTs.t[:, c // 2, c % 2, :], TB_.t[:, c, :], [PTs.b, TB_.bs[c]], [bst.b])
            Kk.stt(tmpst.t[:], Tc.t[:], Ep.t[:, c * 64 + 63:c * 64 + 64], Qs.t[:, c // 2, c % 2, :], MUL, ADD,
                   [Tc.b, Ep.b, Qs.b], [tmpst.b])
            Kk.tt("dve", TB_.t[:, c + 1, :], tmpst.t[:], bst.t[:, 0:64], ADD, [tmpst.b], [bst.b, TB_.bs[c + 1]])
            Kk.tt("dve", Tn.t[:], tmpst.t[:], bst.t[:, 0:64], ADD, [tmpst.b], [bst.b, Tn.b])
            if c == 7:
                Kk.tt("dve", TBn.t[:, 0, :], tmpst.t[:], bst.t[:, 0:64], ADD, [tmpst.b], [bst.b, TBn.bs[0]])
            if c % 2 == 1:
                yield
        byh = [PB.next(), PB.next()]
        for h in range(2):
            off = h * 64
            hs = slice(off, off + 64)
            for c in range(8):
                cs_ = slice(c * 64, (c + 1) * 64)
                Kk.mm(byh[h].t[hs, cs_], TB_.t[hs, c, :], GT.t[hs, cs_], [TB_.bs[c], GT.b], [byh[h].b],
                      tp=(off, off))
        for h in range(2):
            hs = slice(h * 64, (h + 1) * 64)
            Kk.tt("dve", y32.t[hs, :], byh[h].t[hs, :], YI.t[hs, :], ADD, [YI.b], [byh[h].b, y32.b])
        Kk.cp("act", yb.t[:], y32.t[:], [y32.b], [yb.b])
        yield
        bm = PB.next()
        Kk.mm(bm.t[:], bmean.t[:], yb.t[:], [bmean.b, yb.b], [bm.b])
        Kk.tt("dve", yc.t[:], y32.t[:], bm.t[:], SUB, [y32.b], [bm.b, yc.b])
        Kk.tt("pool", yc2.t[:], yc.t[:], yc.t[:], MUL, [yc.b], [yc2.b])
        bvv = PB.next()
        Kk.mm(bvv.t[:], bmean.t[:], yc2.t[:], [bmean.b, yc2.b], [bvv.b])
        Kk.act(sd.t[:], bvv.t[:], ACT.Ln, [], [bvv.b, sd.b], bias=LN_X_EPS)
        Kk.act(sd.t[:], sd.t[:], ACT.Exp, [], [sd.b], scale=-0.5)
        Kk.tt("dve", yc.t[:], yc.t[:], sd.t[:], MUL, [sd.b], [yc.b])
        Kk.act(yc.t[:], yc.t[:], ACT.Identity, [cv.b], [yc.b], bias=col(C_LNB), scale=col(C_LNW))
        yo_ = yout[n % 2]
        Kk.tt("dve", yo_.t[:], yc.t[:], bonus.t[:], ADD, [yc.b, bonus.b], [yo_.b])
        bpq = nblk // 4
        jq = n // bpq
        Kk.dma("sp", ag_in[jq, :, (n % bpq) * TB:(n % bpq + 1) * TB], yo_.t[:], [yo_.b], [dr["ag_in_b"][jq]],
               sem=Kk.tsem(yo_))
        if (n + 1) % bpq == 0 and dr["do_ag"]:
            S.dma("pool", lambda e, jq=jq: e.collective_compute("AllGather", ALU.bypass,
                                                               replica_groups=[[0, 1, 2, 3], [4, 5, 6, 7]],
                                                               ins=[dr["ag_in"][jq]], outs=[dr["ag_out"][jq]]),
                  [dr["ag_in_b"][jq]], [dr["ag_out_b"][jq]], inc=1)
        yield

    gens = [block(n) for n in range(nblk)]
    live = {}
    t = 0
    nxt = 0
    while nxt < nblk or live:
        if nxt < nblk and t == nxt * depth_off:
            live[nxt] = gens[nxt]
            nxt += 1
        for n in sorted(live):
            try:
                next(live[n])
            except StopIteration:
                del live[n]
        t += 1


W2_TILES = ([(0, 128), (128, 32)] + [(672 + t * 128, 128) for t in range(4)] + [(1184 + t * 128, 128) for t in range(4)]
            + [(160 + t * 128, 128) for t in range(4)] + [(1696 + t * 128, 128) for t in range(8)]
            + [(2720 + t * 128, 128) for t in range(8)])
CT8 = [(t * 128, 128) for t in range(8)]


def weight_plan():
    plan = [("w2", "w2", 0, D, W2_TILES), ("wpa", "wpa", 0, 512, CT8), ("wpb", "wpb", 0, 512, CT8),
            ("wmix", "wmix", 0, D, CT8), ("wq", "wq", 0, D, CT8), ("wk", "wkv", 0, D, CT8), ("wxo", "wxo", 0, D, CT8)]
    for half in range(2):
        plan.append(("wup%d" % half, "wup", 0, D, [(half * 2048 + t * 128, 128) for t in range(16)]))
        plan.append(("wdn%d" % half, "wdn", half * 2048, 2048, CT8))
    return plan


def convert_weights(nc, S, Kk, dr):
    jobs = []
    wb = {}
    for name, src, r0, nr, tiles in weight_plan():
        kcs = nr // 128
        t = nc.dram_tensor("wb_" + name, [len(tiles), 128, kcs, 128], BF16).ap()
        b = S.buf("wb_" + name)
        sem = S.new_sem()
        wb[name] = (t, b, kcs, tiles)
        for ti, (c0, w) in enumerate(tiles):
            def job(tick=None, t=t, b=b, sem=sem, src=src, r0=r0, nr=nr, c0=c0, w=w, ti=ti, kcs=kcs):
                Kk.dma("pool", t[ti, :, :, 0:w],
                       dr[src][r0:r0 + nr, c0:c0 + w].rearrange("(kc p) n -> p kc n", p=128),
                       [tick] if tick is not None else [], [b], sem=sem)
            jobs.append(job)
    t = nc.dram_tensor("wb_wv", [4, 128, KC, 256], BF16).ap()
    b = S.buf("wb_wv")
    sem = S.new_sem()
    wb["wv"] = (t, b, KC, None)
    for nd in range(4):
        def job(tick=None, t=t, b=b, sem=sem, nd=nd):
            Kk.dma("pool", t[nd], dr["wkv"][:, D + nd * 256:D + (nd + 1) * 256].rearrange("(kc p) n -> p kc n", p=128),
                   [tick] if tick is not None else [], [b], sem=sem)
        jobs.append(job)
    dr["wb"] = wb
    return jobs


def phase2(S, Kk, cst, cv, dr, SEQ):
    TOK2 = SEQ // 4
    nblk = TOK2 // TB
    ones_bf = cst["ones_bf"]
    MUL, ADD, SUB, MAX = ALU.mult, ALU.add, ALU.subtract, ALU.max
    PB = Banks(S, 7)
    HB = T(S, "hb", [128, 512], F32, psum=True)

    def col(i):
        return cv.t[:, i:i + 1]

    NSLAB = 8
    slabs = [T(S, "slab%d" % i, [128, 16, 128], BF16) for i in range(NSLAB)]
    for sl in slabs:
        sl.sem = S.new_sem()
    sl_i = [0]

    def next_slab():
        sl = slabs[sl_i[0] % NSLAB]
        sl_i[0] += 1
        return sl

    def linear(wname, rhs, evac, halo=None):
        wt, wbuf, kcs, col_tiles = dr["wb"][wname]
        for ti, (c0, w) in enumerate(col_tiles):
            sl = next_slab()
            Kk.dma("sp", sl.t[:, 0:kcs, 0:w], wt[ti, :, :, 0:w], [wbuf], [sl.b], sem=sl.sem)
            bank = PB.next()
            for kc in range(kcs):
                ap, b = rhs(kc)
                Kk.mm(bank.t[0:w, :], sl.t[:, kc, 0:w], ap, [sl.b, b], [bank.b], start=(kc == 0),
                      stop=(kc == kcs - 1))
            if halo is not None and halo(ti) is not None:
                hsl = halo(ti)
                for kc in range(kcs):
                    Kk.mm(HB.t[0:w, hsl], sl.t[:, kc, 0:w], xhn.t[:, kc, :], [sl.b, xhn.b], [HB.b],
                          start=(kc == 0), stop=(kc == kcs - 1))
            evac(ti, bank)

    yag = T(S, "yag", [128, 4, TOK2], BF16)
    idx = dr["idx_sb"]
    agv = dr["ag_out"].rearrange("j r t -> (j r) t")
    for g in range(4):
        S.dma("pool", lambda e, g=g: e.indirect_dma_start(out=yag.t[:, g, :], out_offset=None, in_=agv,
                                                          in_offset=bass.IndirectOffsetOnAxis(idx.t[:, g:g + 1], 0)),
              dr["ag_out_b"] + [idx.b], [yag.b])
    wlg0 = T(S, "wlg0", [128, 512], BF16)
    wlg1 = T(S, "wlg1", [32, 512], BF16)
    Kk.dma("pool", wlg0.t[:], dr["wlg"][0:128, :], [], [wlg0.b])
    Kk.dma("pool", wlg1.t[:], dr["wlg"][128:160, :], [], [wlg1.b])
    KTm = T(S, "KTm", [128, 8, N_MEM], BF16)
    Vm = T(S, "Vm", [128, 2, D], BF16)

    sqt = T(S, "sqt2", [128, TB], F32)
    rstd = T(S, "rstd2", [128, TB], F32)
    xsq = [T(S, "xsq2_%d" % i, [128, TB], BF16) for i in range(2)]
    xsq_i = [0]

    def rms_stats(src_fn, src_b, n_tok, rstd_ap, rstd_b, sqt_ap, sqt_b, bank_ap_fn):
        bank = PB.next()
        for kc in range(KC):
            t = xsq[xsq_i[0] % 2]
            xsq_i[0] += 1
            sb_ = src_b(kc) if callable(src_b) else src_b
            Kk.tt("pool", t.t[:, 0:n_tok], src_fn(kc), src_fn(kc), MUL, [sb_], [t.b])
            Kk.mm(bank.t[:, 0:n_tok], ones_bf.t[:], t.t[:, 0:n_tok], [ones_bf.b, t.b], [bank.b],
                  start=(kc == 0), stop=(kc == KC - 1))
        Kk.act(sqt_ap, bank.t[:, 0:n_tok], ACT.Ln, [], [bank.b, sqt_b], bias=RMS_EPS, scale=1.0 / D)
        Kk.act(rstd_ap, sqt_ap, ACT.Exp, [sqt_b], [rstd_b], scale=-0.5)

    mk = S.mark()
    mf = T(S, "mf", [128, KC, N_MEM], F32)
    mn = T(S, "mn", [128, KC, N_MEM], BF16)
    Kk.dma("sp", mf.t[:], dr["memT"], [], [mf.b])
    rms_stats(lambda kc: mf.t[:, kc, :], mf.b, N_MEM, rstd.t[:, 0:N_MEM], rstd.b, sqt.t[:, 0:N_MEM], sqt.b, None)
    for kc in range(KC):
        Kk.stt(mn.t[:, kc, :], mf.t[:, kc, :], col(C_GMEM + kc), rstd.t[:, 0:N_MEM], MUL, MUL,
               [mf.b, rstd.b, cv.b], [mn.b])
    def evac_k(ti, bank):
        Kk.cp("act", KTm.t[:, ti, :], bank.t[:, 0:N_MEM], [], [bank.b, KTm.b])

    wt, wbuf, kcs_, tiles_ = dr["wb"]["wk"]
    for dt_ in range(8):
        sl = next_slab()
        Kk.dma("sp", sl.t[:, 0:KC, :], wt[dt_], [wbuf], [sl.b], sem=sl.sem)
        bank = PB.next()
        for kc in range(KC):
            Kk.mm(bank.t[:, 0:N_MEM], sl.t[:, kc, :], mn.t[:, kc, :], [sl.b, mn.b], [bank.b], start=(kc == 0),
                  stop=(kc == KC - 1))
        evac_k(dt_, bank)
    for nd in range(4):
        sl = next_slab()
        slv = sl.t[:].rearrange("p a b -> p (a b)").rearrange("p (k n) -> p k n", k=KC)
        Kk.dma("sp", slv, dr["wb"]["wv"][0][nd], [dr["wb"]["wv"][1]], [sl.b], sem=sl.sem)
        for mt in range(2):
            bank = PB.next()
            for kc in range(KC):
                Kk.mm(bank.t[:, 0:256], mn.t[:, kc, mt * 128:(mt + 1) * 128], slv[:, kc, :], [sl.b, mn.b], [bank.b],
                      start=(kc == 0), stop=(kc == KC - 1))
            Kk.cp("act", Vm.t[:, mt, nd * 256:(nd + 1) * 256], bank.t[:, 0:256], [], [bank.b, Vm.b])

    X = [T(S, "X%d" % i, [128, KC, TB], F32, nb=KC) for i in range(2)]
    xn = T(S, "xn2", [128, KC, TB], BF16, nb=KC)
    xhf = T(S, "xhf", [128, KC, 2], F32)
    xhn = T(S, "xhn", [128, KC, 2], BF16)
    rstd_h = T(S, "rstd_h", [128, 2], F32)
    sqt_h = T(S, "sqt_h", [128, 2], F32)
    pgd = T(S, "pgd", [128, 2, TB + 1], F32)
    dgd = T(S, "dgd", [128, 2, TB], F32)
    sgd = T(S, "sgd", [128, 2, TB], BF16)
    cbt = T(S, "cbt", [128, 4, TB], F32, nb=4)
    cct = T(S, "cct", [128, 4, TB], F32, nb=4)
    cch = T(S, "cch", [128, 4, 2], F32)
    u = T(S, "u", [128, 4, TB + 2], F32, nb=4)
    acc = T(S, "acc", [128, TB], F32)
    outb = T(S, "outb", [128, 4, TB], BF16, nb=4)
    outa = T(S, "outa", [128, 4, TB], BF16, nb=4)
    H16 = T(S, "H16", [128, 16, TB], BF16, nb=16)
    ma = T(S, "ma", [128, 8, TB], BF16, nb=8)
    mg = T(S, "mg", [128, 8, TB], BF16, nb=8)
    tmpm = T(S, "tmpm", [128, TB], F32)
    es = [T(S, "es%d" % i, [128, TB], BF16) for i in range(4)]
    rinv = T(S, "rinv", [128, TB], F32)
    rl = [T(S, "rl%d" % i, [128, TB], BF16) for i in range(2)]

    def normalize(Xb, gcol, dst, dstbs):
        rms_stats(lambda kc: Xb.t[:, kc, :], lambda kc: Xb.bs[kc], TB, rstd.t[:], rstd.b, sqt.t[:], sqt.b, None)
        for kc in range(KC):
            Kk.stt(dst.t[:, kc, :], Xb.t[:, kc, :], col(gcol + kc), rstd.t[:], MUL, MUL, [Xb.bs[kc], rstd.b, cv.b],
                   [dstbs[kc]])

    tiles = [(0, 128), (128, 32)]
    tiles += [(672 + t * 128, 128) for t in range(4)]
    tiles += [(1184 + t * 128, 128) for t in range(4)]
    tiles += [(160 + t * 128, 128) for t in range(4)]
    tiles += [(1696 + t * 128, 128) for t in range(8)]
    tiles += [(2720 + t * 128, 128) for t in range(8)]

    for n in range(nblk):
        Xb = X[n % 2]
        Kk.dma("sp", Xb.t[:], dr["xT2"][n], [], Xb.bs, sem=Kk.tsem(Xb))
        first = (n == 0)
        if first:
            Kk.dma("sp", xhf.t[:], dr["xh"], [], [xhf.b])
            rms_stats(lambda kc: xhf.t[:, kc, :], xhf.b, 2, rstd_h.t[:], rstd_h.b, sqt_h.t[:], sqt_h.b, None)
            for kc in range(KC):
                Kk.stt(xhn.t[:, kc, :], xhf.t[:, kc, :], col(C_GMIX + kc), rstd_h.t[:], MUL, MUL,
                       [xhf.b, rstd_h.b, cv.b], [xhn.b])
        normalize(Xb, C_GMIX, xn, xn.bs)

        def evac_in(ti, bank):
            if ti < 2:
                w = 128 if ti == 0 else 32
                if first:
                    if True:
                        Kk.cp("act", pgd.t[0:w, ti, 0:1], HB.t[0:w, ti * 2 + 1:ti * 2 + 2], [], [HB.b, pgd.b])
                else:
                    Kk.cp("dve", pgd.t[0:w, ti, 0:1], pgd.t[0:w, ti, TB:TB + 1], [], [pgd.b])
                Kk.cp("act", pgd.t[0:w, ti, 1:TB + 1], bank.t[0:w, :], [], [bank.b, pgd.b])
                if ti == 1:
                    for t in range(2):
                        ww = 128 if t == 0 else 32
                        Kk.tt("dve", dgd.t[0:ww, t, :], pgd.t[0:ww, t, 0:TB], pgd.t[0:ww, t, 1:TB + 1], SUB,
                              [pgd.b], [dgd.b])
                        Kk.stt(dgd.t[0:ww, t, :], dgd.t[0:ww, t, :], cv.t[0:ww, C_MUGD + t:C_MUGD + t + 1],
                               pgd.t[0:ww, t, 1:TB + 1], MUL, ADD, [pgd.b, cv.b], [dgd.b])
                        Kk.act(sgd.t[0:ww, t, :], dgd.t[0:ww, t, :], ACT.Sigmoid, [dgd.b], [sgd.b])
                    for ot in range(4):
                        bk = PB.next()
                        Kk.mm(bk.t[:], wlg0.t[:, ot * 128:(ot + 1) * 128], sgd.t[:, 0, :], [wlg0.b, sgd.b], [bk.b],
                              start=True, stop=False)
                        Kk.mm(bk.t[:], wlg1.t[:, ot * 128:(ot + 1) * 128], sgd.t[0:32, 1, :], [wlg1.b, sgd.b],
                              [bk.b], start=False, stop=True)
                        Kk.tt("dve", outa.t[:, ot, :], bk.t[:], yag.t[:, ot, n * TB:(n + 1) * TB], MUL, [yag.b],
                              [bk.b, outa.bs[ot]])
            elif ti < 6:
                t = ti - 2
                Kk.cp("act", cct.t[:, t, :], bank.t[:], [], [bank.b, cct.bs[t]])
                if first:
                    Kk.cp("act", cch.t[:, t, :], HB.t[:, ti * 2:ti * 2 + 2], [], [HB.b, cch.b])
            elif ti < 10:
                t = ti - 6
                if first:
                    Kk.tt("dve", u.t[:, t, 0:2], HB.t[:, ti * 2:ti * 2 + 2], cch.t[:, t, :], MUL, [cch.b],
                          [HB.b, u.bs[t]])
                else:
                    Kk.cp("dve", u.t[:, t, 0:2], u.t[:, t, TB:TB + 2], [], [u.bs[t]])
                Kk.tt("dve", u.t[:, t, 2:TB + 2], bank.t[:], cct.t[:, t, :], MUL, [cct.bs[t]], [bank.b, u.bs[t]])
            elif ti < 14:
                t = ti - 10
                Kk.cp("act", cbt.t[:, t, :], bank.t[:], [], [bank.b, cbt.bs[t]])
                Kk.ts("dve", acc.t[:], u.t[:, t, 2:TB + 2], col(C_CONV + 2 * 4 + t), None, MUL, None,
                      [u.bs[t], cv.b], [acc.b])
                Kk.stt(acc.t[:], u.t[:, t, 1:TB + 1], col(C_CONV + 1 * 4 + t), acc.t[:], MUL, ADD, [u.bs[t], cv.b],
                       [acc.b])
                Kk.stt(acc.t[:], u.t[:, t, 0:TB], col(C_CONV + 0 * 4 + t), acc.t[:], MUL, ADD, [u.bs[t], cv.b],
                       [acc.b])
                Kk.tt("dve", outb.t[:, t, :], acc.t[:], cbt.t[:, t, :], MUL, [acc.b, cbt.bs[t]], [outb.bs[t]])
            elif ti < 22:
                t = ti - 14
                Kk.act(H16.t[:, t, :], bank.t[:], ACT.Sigmoid, [cv.b], [bank.b, H16.bs[t]], bias=col(C_BGA + t))
            else:
                t = ti - 22
                Kk.act(H16.t[:, 8 + t, :], bank.t[:], ACT.Sigmoid, [cv.b], [bank.b, H16.bs[8 + t]],
                       bias=col(C_BGB + t))

        def halo_fn(ti):
            if first and ti < 10:
                return slice(ti * 2, ti * 2 + 2)
            return None

        linear("w2", lambda kc: (xn.t[:, kc, :], xn.bs[kc]), evac_in, halo=halo_fn)

        ct8 = [(t * 128, 128) for t in range(8)]

        def evac_pa(ti, bank):
            Kk.tt("dve", ma.t[:, ti, :], bank.t[:], H16.t[:, ti, :], MUL, [H16.bs[ti]], [bank.b, ma.bs[ti]])

        linear("wpa", lambda kc: (outa.t[:, kc, :], outa.bs[kc]), evac_pa)

        def evac_pb(ti, bank):
            Kk.tt("dve", tmpm.t[:], bank.t[:], H16.t[:, 8 + ti, :], MUL, [H16.bs[8 + ti]], [bank.b, tmpm.b])
            Kk.tt("dve", mg.t[:, ti, :], tmpm.t[:], ma.t[:, ti, :], ADD, [tmpm.b, ma.bs[ti]], [mg.bs[ti]])

        linear("wpb", lambda kc: (outb.t[:, kc, :], outb.bs[kc]), evac_pb)

        def evac_res(ti, bank):
            Kk.tt("dve", Xb.t[:, ti, :], bank.t[:], Xb.t[:, ti, :], ADD, [], [bank.b, Xb.bs[ti]])

        linear("wmix", lambda kc: (mg.t[:, kc, :], mg.bs[kc]), evac_res)

        normalize(Xb, C_GXA, xn, xn.bs)

        def evac_q(ti, bank):
            Kk.act(H16.t[:, ti, :], bank.t[:], ACT.Copy, [], [bank.b, H16.bs[ti]], scale=1.0 / 16.0)

        linear("wq", lambda kc: (xn.t[:, kc, :], xn.bs[kc]), evac_q)
        for hd in range(4):
            ee = [es[(hd % 2) * 2], es[(hd % 2) * 2 + 1]]
            for mt in range(2):
                bk = PB.next()
                for j in range(2):
                    dc = 2 * hd + j
                    Kk.mm(bk.t[:], KTm.t[:, dc, mt * 128:(mt + 1) * 128], H16.t[:, dc, :], [KTm.b, H16.bs[dc]],
                          [bk.b], start=(j == 0), stop=(j == 1))
                Kk.act(ee[mt].t[:], bk.t[:], ACT.Exp, [], [bk.b, ee[mt].b])
            bk = PB.next()
            for mt in range(2):
                Kk.mm(bk.t[:], ones_bf.t[:], ee[mt].t[:], [ones_bf.b, ee[mt].b], [bk.b], start=(mt == 0),
                      stop=(mt == 1))
            Kk.act(rinv.t[:], bk.t[:], ACT.Ln, [], [bk.b, rinv.b])
            Kk.act(rinv.t[:], rinv.t[:], ACT.Exp, [], [rinv.b], scale=-1.0)
            for j in range(2):
                dc = 2 * hd + j
                bk = PB.next()
                for mt in range(2):
                    Kk.mm(bk.t[:], Vm.t[:, mt, dc * 128:(dc + 1) * 128], ee[mt].t[:], [Vm.b, ee[mt].b], [bk.b],
                          start=(mt == 0), stop=(mt == 1))
                Kk.tt("dve", H16.t[:, 8 + dc, :], bk.t[:], rinv.t[:], MUL, [rinv.b], [bk.b, H16.bs[8 + dc]])
        linear("wxo", lambda kc: (H16.t[:, 8 + kc, :], H16.bs[8 + kc]), evac_res)

        normalize(Xb, C_GMLP, xn, xn.bs)
        for half in range(2):
            def evac_up(ti, bank):
                r_ = rl[ti % 2]
                Kk.act(r_.t[:], bank.t[:], ACT.Relu, [], [bank.b, r_.b])
                Kk.tt("pool", H16.t[:, ti, :], r_.t[:], r_.t[:], MUL, [r_.b], [H16.bs[ti]])

            linear("wup%d" % half, lambda kc: (xn.t[:, kc, :], xn.bs[kc]), evac_up)
            linear("wdn%d" % half, lambda kc: (H16.t[:, kc, :], H16.bs[kc]), evac_res)

        rms_stats(lambda kc: Xb.t[:, kc, :], lambda kc: Xb.bs[kc], TB, rstd.t[:], rstd.b, sqt.t[:], sqt.b, None)
        for kc in range(KC):
            Kk.stt(Xb.t[:, kc, :], Xb.t[:, kc, :], col(C_GFIN + kc), rstd.t[:], MUL, MUL, [rstd.b, cv.b],
                   [Xb.bs[kc]])
        ev = Kk.dma("sp", dr["outT"][n], Xb.t[:], Xb.bs, [], sem=Kk.tsem(Xb))
        S.final_events.append(ev)


def build(SEQ, phases=("p1", "ag", "p2"), debug=False, cut=None):
    nc = bass.Bass("TRN2", target_bir_lowering=False)
    S = Sched(nc)
    Kk = K(S)
    TOK2 = SEQ // 4
    dr = {}

    def din(name, shape, dt=F32):
        dr[name] = nc.dram_tensor(name, list(shape), dt, kind="ExternalInput").ap()

    din("xT1", [SEQ // TB, 128, KC, TB])
    din("xT2", [TOK2 // TB, 128, KC, TB])
    din("xh", [128, KC, 2])
    din("memT", [128, KC, N_MEM])
    din("cvec", [128, NCV])
    din("w1", [D, 512])
    din("lwa", [128, 128])
    din("w2", [D, 3744])
    din("wlg", [160, 512])
    din("wpa", [512, D])
    din("wpb", [512, D])
    din("wmix", [D, D])
    din("wq", [D, D])
    din("wkv", [D, 2 * D])
    din("wxo", [D, D])
    din("wup", [D, DFF])
    din("wdn", [DFF, D])
    din("idx", [128, 4], U32)
    dr["ag_in"] = nc.dram_tensor("ag_in", [4, 128, TOK2], BF16).ap()
    dr["ag_out"] = nc.dram_tensor("ag_out", [4, 512, TOK2], BF16).ap()
    dr["ag_in_b"] = [S.buf("ag_in%d" % j) for j in range(4)]
    dr["do_ag"] = "ag" in phases
    dr["ag_out_b"] = [S.buf("ag_out%d" % j) for j in range(4)]
    dr["outT"] = nc.dram_tensor("outT", [TOK2 // TB, 128, KC, TB], F32, kind="ExternalOutput").ap()
    if debug:
        dr["dbg"] = nc.dram_tensor("dbg", [4, 512, TOK2], BF16, kind="ExternalOutput").ap()

    cv = T(S, "cv", [128, NCV], F32)
    Kk.dma("sp", cv.t[:], dr["cvec"], [], [cv.b])
    cst = build_consts(S, Kk, cv)
    idx_sb = T(S, "idx", [128, 4], U32)
    Kk.dma("sp", idx_sb.t[:], dr["idx"], [], [idx_sb.b])
    dr["idx_sb"] = idx_sb
    base_mark = S.mark()
    jobs = convert_weights(nc, S, Kk, dr) if "p2" in phases else []
    dr["wjobs"] = jobs
    if "p1" in phases:
        PB = Banks(S, 6)
        phase1(S, Kk, PB, cst, cv, dr, SEQ, cut)
    for j in jobs:
        j()
    del jobs[:]
    if debug:
        ev = Kk.dma("sp", dr["dbg"], dr["ag_out"], dr["ag_out_b"], [])
        S.final_events.append(ev)
    if "p2" in phases:
        S.barrier(keep=dr["ag_out_b"])
        S.release(base_mark)
        phase2(S, Kk, cst, cv, dr, SEQ)
    S.finish()
    return nc, S


def host_prep(inputs, SEQ):
    f = lambda a: np.ascontiguousarray(np.asarray(a, dtype=np.float32))
    x = f(inputs["x"])
    mem = f(inputs["mem"])
    TOK2 = SEQ // 4
    w_in = f(inputs["w_in"])[0]
    mu = f(inputs["mu_shift"])[0]
    maps = []
    for c in range(8):
        b, hp = c // 4, c % 4
        q = hp
        my = slice(hp * 128, (hp + 1) * 128)
        def blk(a):
            nb_ = a.shape[0] // TB
            return np.ascontiguousarray(a.reshape(nb_, TB, KC, 128).transpose(0, 3, 2, 1))
        lo = q * TOK2
        xT1 = blk(x[b])
        xT2 = blk(x[b, lo:lo + TOK2])
        xh = np.zeros((128, KC, 2), np.float32)
        if q > 0:
            xh[:] = x[b, lo - 2:lo].reshape(2, KC, 128).transpose(2, 1, 0)
        memT = np.ascontiguousarray(mem[b].reshape(N_MEM, KC, 128).transpose(2, 1, 0))
        cvec = np.zeros((128, NCV), np.float32)
        for base, key in ((C_GMIX, "norm_mix"), (C_GXA, "norm_xattn"), (C_GMLP, "norm_mlp"), (C_GMEM, "norm_mem")):
            cvec[:, base:base + 8] = f(inputs[key])[0].reshape(8, 128).T
        cvec[:, C_GFIN:C_GFIN + 8] = f(inputs["norm_final"]).reshape(8, 128).T
        cvec[:, C_MU1 + 0] = mu[0:512][my]
        cvec[:, C_MU1 + 1] = mu[512:1024][my]
        cvec[:, C_MU1 + 2] = mu[1024:1536][my]
        cvec[:, C_MU1 + 3] = mu[1536:1664]
        cvec[:, C_MUGD] = mu[1664:1792]
        cvec[0:32, C_MUGD + 1] = mu[1792:1824]
        cvec[:, C_W0] = f(inputs["w0"])[0][my]
        cvec[:, C_A0] = f(inputs["a0"])[0][my]
        cvec[:, C_KK] = f(inputs["k_k"])[0][my]
        cvec[:, C_KA] = f(inputs["k_a"])[0][my]
        cvec[:, C_RK] = f(inputs["r_k"])[0].reshape(512)[my]
        cvec[:, C_LNW] = f(inputs["ln_x_w"])[0][my]
        cvec[:, C_LNB] = f(inputs["ln_x_b"])[0][my]
        bg = f(inputs["b_gate"])[0]
        cvec[:, C_BGA:C_BGA + 8] = bg[0:1024].reshape(8, 128).T
        cvec[:, C_BGB:C_BGB + 8] = bg[1024:2048].reshape(8, 128).T
        cw = f(inputs["conv_w"])[0][:, 0, :]
        for j in range(3):
            cvec[:, C_CONV + j * 4:C_CONV + j * 4 + 4] = cw[j].reshape(4, 128).T
        w1 = np.concatenate([w_in[:, 0:512][:, my], w_in[:, 512:1024][:, my], w_in[:, 1024:1536][:, my],
                             w_in[:, 1536:1664]], axis=1)
        lwa = np.concatenate([f(inputs["w_lora_w"])[0][:, my], f(inputs["w_lora_a"])[0][:, my]], axis=0)
        idx = np.zeros((128, 4), np.uint32)
        for g in range(4):
            idx[:, g] = q * 512 + g * 128 + np.arange(128)
        maps.append({
            "xT1": xT1, "xT2": xT2, "xh": xh, "memT": memT, "cvec": cvec,
            "w1": np.ascontiguousarray(w1), "lwa": np.ascontiguousarray(lwa),
            "w2": np.ascontiguousarray(w_in[:, 1664:]), "wlg": f(inputs["w_lora_g"])[0],
            "wpa": f(inputs["w_proj_a"])[0], "wpb": f(inputs["w_proj_b"])[0], "wmix": f(inputs["w_out_mix"])[0],
            "wq": f(inputs["w_q"])[0], "wkv": f(inputs["w_kv"])[0], "wxo": f(inputs["w_xo"])[0],
            "wup": f(inputs["w_up"])[0], "wdn": f(inputs["w_down"])[0], "idx": idx,
        })
    return maps


def kernel(**inputs):
    SEQ = inputs["x"].shape[1]
    nc, S = build(SEQ)
    maps = host_prep(inputs, SEQ)
    res = run_bass_kernel_spmd(nc, maps, core_ids=list(range(8)))
    TOK2 = SEQ // 4
    out = np.zeros((2, SEQ, D), np.float32)
    for c in range(8):
        b, q = c // 4, c % 4
        o = np.asarray(res.results[c]["outT"])
        out[b, q * TOK2:(q + 1) * TOK2, :] = o.transpose(0, 3, 2, 1).reshape(TOK2, D)
    return out
```

```python
import numpy as np
import concourse.bass as bass
import concourse.mybir as mybir
from concourse.bass_utils import run_bass_kernel_spmd

F32 = mybir.dt.float32
BF16 = mybir.dt.bfloat16
U32 = mybir.dt.uint32
ALU = mybir.AluOpType
ACT = mybir.ActivationFunctionType

ENGS = ("pe", "act", "dve", "pool", "sp")

D = 1024
KC = 8
TB = 512
RMS_EPS = 1e-6
LN_X_EPS = 64e-5
N_MEM = 256
DFF = 4096

C_GMIX, C_GXA, C_GMLP, C_GFIN, C_GMEM = 0, 8, 16, 24, 32
C_MU1 = 40
C_MUGD = 44
C_W0, C_A0, C_KK, C_KA, C_RK, C_LNW, C_LNB = 46, 47, 48, 49, 50, 51, 52
C_BGA, C_BGB = 53, 61
C_CONV = 69
NCV = 81


class Buf:
    __slots__ = ("name", "w", "r")

    def __init__(self, name=""):
        self.name = name
        self.w = None
        self.r = []


class Sched:
    def __init__(self, nc):
        self.nc = nc
        self.ops = {e: [] for e in ENGS}
        self.dma_sems = []
        self.ctx = []
        self.final_events = []
        self.all_bufs = []
        self.seq = 0
        self.rec_of = {}
        self.dma_evt = {}
        self.segs = {e: [0] for e in ENGS}

    def sb(self, name, shape, dt):
        self.uid = getattr(self, "uid", 0) + 1
        name = "%s_%d" % (name, self.uid)
        cm = self.nc.sbuf_tensor(name, list(shape), dt)
        t = cm.__enter__()
        self.ctx.append(cm)
        return t

    def ps(self, name, shape, dt=F32):
        self.uid = getattr(self, "uid", 0) + 1
        name = "%s_%d" % (name, self.uid)
        cm = self.nc.psum_tensor(name, list(shape), dt)
        t = cm.__enter__()
        self.ctx.append(cm)
        return t

    def mark(self):
        return len(self.ctx)

    def release(self, mark):
        while len(self.ctx) > mark:
            self.ctx.pop().__exit__(None, None, None)

    def buf(self, name=""):
        b = Buf(name)
        self.all_bufs.append(b)
        return b

    def _deps(self, reads, writes):
        waits = []
        for b in reads:
            if b.w is not None:
                waits.append(b.w)
        for b in writes:
            if b.w is not None:
                waits.append(b.w)
            waits.extend(b.r)
        return waits

    def op(self, eng, fn, reads=(), writes=(), cost=0.5):
        waits = self._deps(reads, writes)
        idx = len(self.ops[eng])
        self.seq += 1
        rec = {"fn": fn, "waits": waits, "sig": False, "dma": None, "seq": self.seq, "cost": cost, "eng": eng}
        self.ops[eng].append(rec)
        ev = ("e", eng, idx)
        self.rec_of[ev] = rec
        for b in reads:
            b.r.append(ev)
        for b in writes:
            b.w = ev
            b.r = []
        return ev

    def dma(self, eng, fn, reads=(), writes=(), sem=None, inc=16, cost=3.0):
        waits = self._deps(reads, writes)
        if sem is None:
            self.dma_sems.append({"h": None, "total": 0})
            sem = len(self.dma_sems) - 1
        s = self.dma_sems[sem]
        s["total"] += inc
        self.seq += 1
        rec = {"fn": fn, "waits": waits, "sig": False, "dma": sem, "inc": inc, "seq": self.seq, "cost": cost,
               "eng": eng}
        self.ops[eng].append(rec)
        ev = ("d", sem, s["total"])
        self.dma_evt[ev] = rec
        for b in reads:
            b.r.append(ev)
        for b in writes:
            b.w = ev
            b.r = []
        return ev

    def new_sem(self):
        self.dma_sems.append({"h": None, "total": 0})
        return len(self.dma_sems) - 1

    def barrier(self, keep=()):
        keep_ids = set(id(b) for b in keep)
        keep_evs = set(b.w for b in keep if b.w is not None)
        evs = []
        for e in ENGS:
            n = len(self.ops[e])
            for i in range(n - 1, -1, -1):
                if self.ops[e][i]["dma"] is None and self.ops[e][i]["fn"] is not None:
                    evs.append(("e", e, i))
                    break
        keep_sems = set(ev[1] for ev in keep_evs if ev[0] == "d")
        for i, s in enumerate(self.dma_sems):
            if s["total"] > 0 and i not in keep_sems:
                evs.append(("d", i, s["total"]))
        for e in ENGS:
            self.seq += 1
            self.ops[e].append({"fn": None, "waits": list(evs), "sig": False, "dma": None, "seq": self.seq,
                                "cost": 0.0, "eng": e})
            self.segs[e].append(len(self.ops[e]))
        for b in self.all_bufs:
            if id(b) in keep_ids:
                continue
            b.w = None
            b.r = []

    LAT_X = 1.4
    LAT_S = 0.6

    def schedule(self):
        import heapq
        ops = self.ops
        nseg = len(self.segs[ENGS[0]])
        new_ops = {e: [] for e in ENGS}
        for si in range(nseg):
            seg = []
            for e in ENGS:
                lo = self.segs[e][si]
                hi = self.segs[e][si + 1] if si + 1 < nseg else len(ops[e])
                seg.extend(ops[e][lo:hi])
            inseg = set(id(r) for r in seg)
            barrier_recs = [r for r in seg if r["fn"] is None]
            body = [r for r in seg if r["fn"] is not None]
            succ = {id(r): [] for r in body}
            ndep = {}
            for r in body:
                deps = []
                for w in r["waits"]:
                    p = self.rec_of.get(w) if w[0] == "e" else self.dma_evt.get(w)
                    if p is not None and id(p) in inseg and p["fn"] is not None:
                        deps.append(p)
                r["_deps"] = deps
                ndep[id(r)] = len(deps)
                for p in deps:
                    succ[id(p)].append(r)
            prio = {}
            if getattr(self, "cp_prio", True):
                for r in sorted(body, key=lambda r_: -r_["seq"]):
                    best_ = 0.0
                    for q in succ[id(r)]:
                        lat = self.LAT_S if (q["eng"] == r["eng"] and r["dma"] is None) else self.LAT_X
                        if r["eng"] == "pe" and q["eng"] == "pe":
                            lat = 0.0
                        v_ = lat + prio[id(q)]
                        if v_ > best_:
                            best_ = v_
                    prio[id(r)] = r["cost"] + best_
                for r in body:
                    r["_key"] = -prio[id(r)]
            else:
                for r in body:
                    r["_key"] = r["seq"]
            free = {e: 0.0 for e in ENGS}
            pend = {e: [] for e in ENGS}
            avail = {e: [] for e in ENGS}
            fin = {}
            for r in body:
                rt0 = 0.0
                for w in r["waits"]:
                    if w[0] == "d":
                        p = self.dma_evt.get(w)
                        if p is not None and id(p) not in inseg and p.get("inc") == 1:
                            rt0 = 150.0
                r["_rt0"] = rt0
                if ndep[id(r)] == 0:
                    heapq.heappush(pend[r["eng"]], (rt0, r["seq"], r["seq"], r))
            order = {e: [] for e in ENGS}
            left = len(body)
            while left:
                best = None
                for e in ENGS:
                    pe_, av = pend[e], avail[e]
                    while pe_ and pe_[0][0] <= free[e]:
                        t_, sq, _i, r = heapq.heappop(pe_)
                        heapq.heappush(av, (r["_key"], r["seq"], r))
                    if av:
                        cand = (free[e], av[0][0], e, True)
                    elif pe_:
                        cand = (pe_[0][0], pe_[0][1], e, False)
                    else:
                        continue
                    if best is None or cand < best:
                        best = cand
                start, _sq, e, from_av = best
                if from_av:
                    _sq, _i, r = heapq.heappop(avail[e])
                else:
                    _t, _sq, _i, r = heapq.heappop(pend[e])
                order[e].append(r)
                r["_start"] = start
                r["_engprev"] = order[e][-2] if len(order[e]) > 1 else None
                left -= 1
                if r["dma"] is not None:
                    issue = 1.0 if e == "pool" else 0.08
                    free[e] = start + issue
                    done = start + r["cost"]
                else:
                    free[e] = start + r["cost"]
                    done = free[e]
                fin[id(r)] = done
                for q in succ[id(r)]:
                    ndep[id(q)] -= 1
                    if ndep[id(q)] == 0:
                        rt = q["_rt0"]
                        for p in q["_deps"]:
                            lat = self.LAT_S if (p["eng"] == q["eng"] and p["dma"] is None) else self.LAT_X
                            if p["eng"] == "pe" and q["eng"] == "pe":
                                lat = 0.0
                            rt = max(rt, fin[id(p)] + lat)
                        q["_ready"] = rt
                        heapq.heappush(pend[q["eng"]], (rt, q["seq"], q["seq"], q))
            self.sim_span = getattr(self, "sim_span", []) + [max(fin.values()) if fin else 0.0]
            self.sim_fin = getattr(self, "sim_fin", {})
            self.sim_fin.update(fin)
            for e in ENGS:
                new_ops[e].extend(order[e])
                new_ops[e].extend([r for r in barrier_recs if r["eng"] == e])
        for e in ENGS:
            assert len(new_ops[e]) == len(ops[e])
            ops[e][:] = new_ops[e]

    def finish(self, final_wait_eng="sp"):
        nc = self.nc
        ops = self.ops
        rec_of = self.rec_of
        if getattr(self, "do_schedule", True):
            self.schedule()
        for w in self.final_events:
            if w[0] == "e":
                rec_of[w]["sig"] = True
        for e in ENGS:
            for rec in ops[e]:
                for w in rec["waits"]:
                    if w[0] == "e":
                        rec_of[w]["sig"] = True
        for e in ENGS:
            c = 0
            for rec in ops[e]:
                if rec["sig"]:
                    c += 1
                rec["count"] = c
        sem_cms = []
        esem = {}
        for e in ENGS:
            cm = nc.semaphore("s_" + e)
            esem[e] = cm.__enter__()
            sem_cms.append(cm)
        for i, s in enumerate(self.dma_sems):
            cm = nc.semaphore("d_%d" % i)
            s["h"] = cm.__enter__()
            sem_cms.append(cm)
        dma_sems = self.dma_sems
        final_events = self.final_events
        nwaits = {e: 0 for e in ENGS}

        def emit(e, eng):
            known = {}
            for rec in ops[e]:
                need = {}
                for w in rec["waits"]:
                    if w[0] == "e":
                        if w[1] == e and e == "pe":
                            continue
                        key = ("e", w[1])
                        val = rec_of[w]["count"]
                    else:
                        key = ("d", w[1])
                        val = w[2]
                    if known.get(key, 0) >= val:
                        continue
                    if need.get(key, 0) < val:
                        need[key] = val
                for key, val in need.items():
                    h = esem[key[1]] if key[0] == "e" else dma_sems[key[1]]["h"]
                    eng.wait_ge(h, val)
                    known[key] = val
                    nwaits[e] += 1
                if rec["fn"] is None:
                    continue
                ins = rec["fn"](eng)
                if rec["dma"] is not None:
                    ins.then_inc(dma_sems[rec["dma"]]["h"], rec.get("inc", 16))
                elif rec["sig"]:
                    ins.then_inc(esem[e], 1)
            if e == final_wait_eng:
                for w in final_events:
                    if w[0] == "e":
                        eng.wait_ge(esem[w[1]], rec_of[w]["count"])
                    else:
                        eng.wait_ge(dma_sems[w[1]]["h"], w[2])

        with nc.Block() as block:
            @block.sync
            def _(eng):
                emit("sp", eng)

            @block.tensor
            def _(eng):
                emit("pe", eng)

            @block.scalar
            def _(eng):
                emit("act", eng)

            @block.vector
            def _(eng):
                emit("dve", eng)

            @block.gpsimd
            def _(eng):
                emit("pool", eng)
        self.stats = {e: (len(ops[e]), nwaits[e]) for e in ENGS}
        for cm in reversed(sem_cms):
            cm.__exit__(None, None, None)
        self.release(0)


class T:
    def __init__(self, S, name, shape, dt, nb=1, psum=False):
        self.t = S.ps(name, shape, dt) if psum else S.sb(name, shape, dt)
        self.bs = [S.buf(name + str(i)) for i in range(nb)]
        self.b = self.bs[0]


class K:
    def __init__(self, S):
        self.S = S

    @staticmethod
    def _c(eng, ap):
        n = 1
        for d in ap.shape[1:]:
            n *= d
        if eng == "pe":
            return 0.05 + 0.00055 * n
        if eng == "act":
            return 0.2 + 0.0009 * n
        if eng == "dve":
            return 0.12 + 0.0011 * n
        return 0.15 + 0.002 * n

    def tt(self, eng, out, in0, in1, op, R, W):
        return self.S.op(eng, lambda e: e.tensor_tensor(out, in0, in1, op), R, W, cost=self._c(eng, out))

    def ts(self, eng, out, in0, s1, s2, op0, op1, R, W):
        c = self._c(eng, out) * (5.0 if eng == "pool" else 1.0)
        if s2 is None:
            return self.S.op(eng, lambda e: e.tensor_scalar(out, in0, s1, None, op0), R, W, cost=c)
        return self.S.op(eng, lambda e: e.tensor_scalar(out, in0, s1, s2, op0, op1), R, W, cost=c)

    def stt(self, out, in0, sc, in1, op0, op1, R, W):
        return self.S.op("dve", lambda e: e.scalar_tensor_tensor(out, in0, sc, in1, op0, op1), R, W,
                         cost=self._c("dve", out))

    def act(self, out, in_, func, R, W, bias=None, scale=None, eng="act"):
        kw = {}
        if bias is not None:
            kw["bias"] = bias
        if scale is not None:
            kw["scale"] = scale
        return self.S.op(eng, lambda e: e.activation(out, in_, func, **kw), R, W, cost=self._c("act", out))

    def cp(self, eng, out, in_, R, W):
        if eng == "act":
            return self.S.op("act", lambda e: e.activation(out, in_, ACT.Copy), R, W, cost=self._c("act", out))
        return self.S.op(eng, lambda e: e.tensor_copy(out, in_), R, W, cost=self._c(eng, out))

    def recip(self, out, in_, R, W):
        return self.S.op("dve", lambda e: e.reciprocal(out, in_), R, W, cost=6 * self._c("dve", out))

    def mm(self, out, lhsT, rhs, R, W, start=True, stop=True, tp=None):
        c = self._c("pe", rhs)
        if tp is None:
            return self.S.op("pe", lambda e: e.matmul(out, lhsT, rhs, start=start, stop=stop), R, W, cost=c)
        return self.S.op("pe", lambda e: e.matmul(out, lhsT, rhs, start=start, stop=stop,
                                                   tile_position=tp), R, W, cost=c)

    def tr(self, out, in_, ident, R, W):
        return self.S.op("pe", lambda e: e.transpose(out, in_, ident), R, W, cost=0.12)

    def memset(self, eng, out, val, W):
        return self.S.op(eng, lambda e: e.memset(out, val), (), W, cost=self._c(eng, out))

    def dma(self, eng, out, in_, R, W, sem=None):
        nb_ = 1
        for d in out.shape:
            nb_ *= d
        return self.S.dma(eng, lambda e: e.dma_start(out=out, in_=in_), R, W, sem=sem, cost=2.5 + nb_ * 2e-5)

    def tsem(self, t):
        if not hasattr(t, "sem"):
            t.sem = self.S.new_sem()
        return t.sem


class Banks:
    def __init__(self, S, n):
        self.banks = [T(S, "pb%d" % i, [128, 512], F32, psum=True) for i in range(n)]
        self.i = 0

    def next(self):
        n = len(self.banks)
        for k in range(n):
            b = self.banks[(self.i + k) % n]
            if b.b.w is None or b.b.w[0] != "e" or b.b.w[1] != "pe":
                self.i = (self.i + k + 1) % n
                return b
        raise RuntimeError("no free PSUM bank (a bank is held across a pipeline yield)")


def v3(ap, c):
    return ap.rearrange("p (c t) -> p c t", c=c)


def build_consts(S, Kk, cv):
    c = {}
    ones_bf = T(S, "ones_bf", [128, 128], BF16)
    Kk.memset("pool", ones_bf.t[:], 1.0, [ones_bf.b])
    c["ones_bf"] = ones_bf
    ones32 = T(S, "ones32", [128, 64], F32)
    Kk.memset("pool", ones32.t[:], 1.0, [ones32.b])
    c["ones32"] = ones32
    for name, val in (("bones", 1.0), ("bmean", 1.0 / 64.0)):
        t = T(S, name, [128, 128], BF16)
        Kk.memset("pool", t.t[:], 0.0, [t.b])
        Kk.memset("pool", t.t[0:64, 0:64], val, [t.b])
        Kk.memset("pool", t.t[64:128, 64:128], val, [t.b])
        c[name] = t
    ident = T(S, "ident64", [128, 64], BF16)
    Kk.memset("pool", ident.t[:], 1.0, [ident.b])
    for h in range(2):
        sl = ident.t[h * 64:(h + 1) * 64, :]
        S.op("pool", lambda e, sl=sl: e.affine_select(sl, sl, [[-1, 64]], ALU.is_equal, 0.0,
                                                       base=0, channel_multiplier=1),
             [ident.b], [ident.b])
    c["ident64"] = ident
    ident128 = T(S, "ident128", [128, 128], BF16)
    Kk.memset("pool", ident128.t[:], 1.0, [ident128.b])
    S.op("pool", lambda e: e.affine_select(ident128.t[:], ident128.t[:], [[-1, 128]], ALU.is_equal, 0.0,
                                           base=0, channel_multiplier=1), [ident128.b], [ident128.b])
    c["ident128"] = ident128
    m4 = T(S, "mask4", [128, 4, 128], BF16)
    mn4 = T(S, "maskn4", [128, 4, 128], BF16)
    Kk.memset("pool", m4.t[:], 0.0, [m4.b])
    Kk.memset("pool", mn4.t[:], 0.0, [mn4.b])
    for blk in range(2):
        ps = slice(blk * 64, (blk + 1) * 64)
        for q in range(4):
            sl = m4.t[ps, q, blk * 64:(blk + 1) * 64]
            Kk.memset("pool", sl, 1.0, [m4.b])
            cmp = ALU.is_gt if q % 2 == 0 else ALU.is_ge
            S.op("pool", lambda e, sl=sl, cmp=cmp: e.affine_select(sl, sl, [[1, 64]], cmp, 0.0, base=0,
                                                                    channel_multiplier=-1),
                 [m4.b], [m4.b])
            sl2 = mn4.t[ps, q, blk * 64:(blk + 1) * 64]
            Kk.memset("pool", sl2, 1.0, [mn4.b])
            S.op("pool", lambda e, sl2=sl2: e.affine_select(sl2, sl2, [[-1, 64]], ALU.is_gt, 0.0, base=0,
                                                             channel_multiplier=1),
                 [mn4.b], [mn4.b])
    c["mask4"] = m4
    c["maskn4"] = mn4
    omka = T(S, "omka", [128, 1], F32)
    Kk.ts("pool", omka.t[:], cv.t[:, C_KA:C_KA + 1], -1.0, 1.0, ALU.mult, ALU.add, [cv.b], [omka.b])
    c["omka"] = omka
    return c


def phase1(S, Kk, PB, cst, cv, dr, SEQ, cut=None, depth_off=5):
    nblk = SEQ // TB
    xT1, w1, lwa, ag_in = dr["xT1"], dr["w1"], dr["lwa"], dr["ag_in"]
    ones_bf, ones32, bones, bmean = cst["ones_bf"], cst["ones32"], cst["bones"], cst["bmean"]
    ident, m4, mn4, omka = cst["ident64"], cst["mask4"], cst["maskn4"], cst["omka"]
    ident128 = cst["ident128"]
    MUL, ADD, SUB, MAX = ALU.mult, ALU.add, ALU.subtract, ALU.max

    def col(i):
        return cv.t[:, i:i + 1]

    w1b = T(S, "w1b", [128, KC, 512], BF16)
    Kk.dma("pool", w1b.t[:], w1.rearrange("(kc p) n -> p kc n", p=128), [], [w1b.b])
    lwab = T(S, "lwab", [128, 128], BF16)
    Kk.dma("pool", lwab.t[:], lwa, [], [lwab.b])

    def f32t(name):
        return T(S, name, [128, TB], F32)

    def bft(name):
        return T(S, name, [128, TB], BF16)

    xf = [T(S, "xf%d" % i, [128, KC, TB], F32) for i in range(2)]
    xn = T(S, "xn", [128, KC, TB], BF16, nb=KC)
    xsq = [bft("xsq%d" % i) for i in range(3)]
    P = T(S, "P", [128, 4, TB + 1], F32)
    Kk.memset("pool", P.t[:, :, TB:TB + 1], 0.0, [P.b])
    dsc = [f32t("dsc%d" % i) for i in range(2)]
    RING = dr.get("ring", ())
    pm_l = [T(S, "pm%d" % i, [128, 4, TB], F32, nb=4) for i in range(2 if "pm" in RING else 1)]
    scr = {}
    for nme in ("sqt", "rstd", "sg", "cs", "Em", "Eex", "Eend", "a_t", "kk", "sq2", "beta"):
        scr[nme] = [f32t(nme + str(i)) for i in range(2 if nme in RING else 1)]
    Ep2 = [f32t("Ep%d" % i) for i in range(4)]
    bonus2 = [f32t("bonus%d" % i) for i in range(4)]
    la, kk2, rk, BT, KT, BH, KH, vb = [bft(n) for n in ("la", "kk2", "rk", "BT", "KT", "BH", "KH", "vb")]
    AR2 = [T(S, "AR%d" % i, [128, 4, 2, 128], BF16) for i in range(3)]
    PTr = [T(S, "ptr%d" % i, [128, 1024], BF16, psum=True) for i in range(2)]
    TM2 = [T(S, "TM%d" % i, [128, 4, 4, 128], BF16) for i in range(3)]
    AM2 = [T(S, "AM%d" % i, [128, 8, 5, 128], BF16, nb=8) for i in range(2)]
    Y02 = [T(S, "Y0_%d" % i, [128, 8, 128], BF16) for i in range(2)]
    Xl_r = [[T(S, "Xl%d_%d" % (j, i), [128, 8, 128], BF16) for i in range(2)] for j in range(2)]
    Zl_r = [[T(S, "Zl%d_%d" % (j, i), [128, 8, 128], BF16) for i in range(2)] for j in range(2)]
    Yl_r = [[T(S, "Yl%d_%d" % (j, i), [128, 8, 128], BF16) for i in range(2)] for j in range(2)]
    PTs = T(S, "PTs", [128, 4, 2, 128], BF16)
    Kk.memset("pool", PTs.t[:], 0.0, [PTs.b])
    Qs = T(S, "Qs", [128, 4, 2, 64], F32)
    GT = bft("GT")
    YI = f32t("YI")
    Tbf = [T(S, "Tbf%d" % i, [128, 9, 64], BF16, nb=9) for i in range(2)]
    T32 = [T(S, "T32_%d" % i, [128, 64], F32) for i in range(2)]
    tmpst = T(S, "tmpst", [128, 64], F32)
    Kk.memset("pool", Tbf[0].t[:, 0, :], 0.0, [Tbf[0].bs[0]])
    Kk.memset("pool", T32[0].t[:], 0.0, [T32[0].b])
    y32, yc, sd = [f32t(n) for n in ("y32", "yc", "sd")]
    yb, yc2 = bft("yb"), bft("yc2")
    yout = [bft("yout%d" % i) for i in range(2)]
    xsq_i = [0]
    nce = T(S, "nce", [128, 8, 1], F32)
    C0 = 0.6065306597126334

    def block(n):
        par = n % 2
        X = xf[n % 2]
        Ep, bonus, AR, TM, AM, Y0 = Ep2[n % 4], bonus2[n % 4], AR2[n % 3], TM2[n % 3], AM2[par], Y02[par]
        Xl, Zl, Yl = Xl_r[par], Zl_r[par], Yl_r[par]
        pm = pm_l[n % len(pm_l)]
        sqt, rstd, sg, cs, Em, Eex, Eend, a_t, kk, sq2, beta = [scr[k_][n % len(scr[k_])] for k_ in (
            "sqt", "rstd", "sg", "cs", "Em", "Eex", "Eend", "a_t", "kk", "sq2", "beta")]
        Kk.dma("sp", X.t[:], xT1[n], [], [X.b], sem=Kk.tsem(X))
        yield
        st = PB.next()
        for kc in range(KC):
            Kk.act(xn.t[:, kc, :], X.t[:, kc, :], ACT.Copy, [X.b, cv.b], [xn.bs[kc]], scale=col(C_GMIX + kc))
            q_ = xsq[xsq_i[0] % 3]
            xsq_i[0] += 1
            Kk.act(q_.t[:], X.t[:, kc, :], ACT.Square, [X.b], [q_.b])
            Kk.mm(st.t[:], ones_bf.t[:], q_.t[:], [ones_bf.b, q_.b], [st.b], start=(kc == 0), stop=(kc == KC - 1))
        pp = [PB.next() for _ in range(4)]
        for ct in range(4):
            for kc in range(KC):
                Kk.mm(pp[ct].t[:], w1b.t[:, kc, ct * 128:(ct + 1) * 128], xn.t[:, kc, :],
                      [w1b.b, xn.bs[kc]], [pp[ct].b], start=(kc == 0), stop=(kc == KC - 1))
        Kk.act(sqt.t[:], st.t[:], ACT.Ln, [], [st.b, sqt.b], bias=RMS_EPS, scale=1.0 / D)
        tick = S.buf("tick%d" % n)
        Kk.act(rstd.t[:], sqt.t[:], ACT.Exp, [sqt.b], [rstd.b, tick], scale=-0.5)
        Kk.cp("pool", P.t[:, :, 0:1], P.t[:, :, TB:TB + 1], [], [P.b])
        for ct in range(4):
            Kk.tt("dve", P.t[:, ct, 1:TB + 1], pp[ct].t[:], rstd.t[:], MUL, [rstd.b], [pp[ct].b, P.b])
        yield
        for ct in range(4):
            d_ = dsc[ct % 2]
            Kk.tt("pool", d_.t[:], P.t[:, ct, 0:TB], P.t[:, ct, 1:TB + 1], SUB, [P.b], [d_.b])
            Kk.stt(pm.t[:, ct, :], d_.t[:], col(C_MU1 + ct), P.t[:, ct, 1:TB + 1], MUL, ADD,
                   [d_.b, P.b, cv.b], [pm.bs[ct]])
        rP, kP, vP, wP = (pm.t[:, i, :] for i in range(4))
        rB, kB, vB, wB = pm.bs
        wj = dr.get("wjobs", [])
        per = (len(wj) + max(1, nblk - n) - 1) // max(1, nblk - n) if wj else 0
        for _ in range(min(per, len(wj))):
            wj.pop(0)(tick)
        yield
        Kk.act(la.t[0:64, :], pm.t[0:64, 3, :], ACT.Tanh, [wB], [la.b])
        Kk.cp("pool", la.t[64:128, :], pm.t[64:128, 3, :], [wB], [la.b])
        zw, za = PB.next(), PB.next()
        Kk.mm(zw.t[:], lwab.t[0:64, :], la.t[0:64, :], [lwab.b, la.b], [zw.b])
        Kk.mm(za.t[:], lwab.t[64:128, :], la.t[64:128, :], [lwab.b, la.b], [za.b], tp=(64, 0))
        Kk.act(sg.t[:], zw.t[:], ACT.Sigmoid, [cv.b], [zw.b, sg.b], bias=col(C_W0))
        Kk.act(a_t.t[:], za.t[:], ACT.Sigmoid, [cv.b], [za.b, a_t.b], bias=col(C_A0))
        ld = sg
        yield
        for c in range(8):
            sl = slice(c * 64, (c + 1) * 64)
            S.op("dve", lambda e, sl=sl: e.tensor_tensor_scan(cs.t[:, sl], ones32.t[:, :], ld.t[:, sl], 0.0,
                                                               MUL, ADD),
                 [ones32.b, ld.b], [cs.b], cost=0.25)
        Kk.act(Ep.t[:], cs.t[:], ACT.Exp, [cs.b], [Ep.b], scale=-C0)
        Kk.act(Em.t[:], cs.t[:], ACT.Exp, [cs.b], [Em.b], scale=C0)
        Kk.tt("pool", Eex.t[:], cs.t[:], ld.t[:], SUB, [cs.b, ld.b], [Eex.b])
        Kk.act(Eex.t[:], Eex.t[:], ACT.Exp, [], [Eex.b], scale=-C0)
        Kk.act(nce.t[:], v3(cs.t[:], 8)[:, :, 63:64], ACT.Copy, [cs.b], [nce.b], scale=-C0)
        for c in range(8):
            sl = slice(c * 64, (c + 1) * 64)
            Kk.act(Eend.t[:, sl], cs.t[:, sl], ACT.Exp, [cs.b, nce.b], [Eend.b], scale=C0, bias=nce.t[:, c, :])
        yield
        Kk.act(kk.t[:], kP, ACT.Copy, [kB, cv.b], [kk.b], scale=col(C_KK))
        Kk.act(kk2.t[:], kP, ACT.Square, [kB, cv.b], [kk2.b], scale=col(C_KK))
        ssb = PB.next()
        Kk.mm(ssb.t[:], bones.t[:], kk2.t[:], [bones.b, kk2.b], [ssb.b])
        Kk.ts("dve", sq2.t[:], ssb.t[:], 1e-19, None, MAX, None, [], [ssb.b, sq2.b])
        Kk.act(sq2.t[:], sq2.t[:], ACT.Ln, [], [sq2.b])
        Kk.act(sq2.t[:], sq2.t[:], ACT.Exp, [], [sq2.b], scale=-0.5)
        Kk.tt("dve", kk.t[:], kk.t[:], sq2.t[:], MUL, [sq2.b], [kk.b])
        kkn = kk
        Kk.tt("pool", beta.t[:], kkn.t[:], a_t.t[:], MUL, [kkn.b, a_t.b], [beta.b])
        Kk.act(a_t.t[:], a_t.t[:], ACT.Identity, [cv.b, omka.b], [a_t.b], scale=col(C_KA), bias=omka.t[:, 0:1])
        Kk.tt("pool", a_t.t[:], kP, a_t.t[:], MUL, [kB], [a_t.b])
        kmod = a_t
        yield
        Kk.stt(AR.t[:, :, 0, :], v3(kkn.t[:], 4), -1.0, v3(Eex.t[:], 4), MUL, MUL, [kkn.b, Eex.b], [AR.b])
        Kk.tt("pool", AR.t[:, :, 1, :], v3(rP, 4), v3(Ep.t[:], 4), MUL, [rB, Ep.b], [AR.b])
        Kk.tt("pool", BT.t[:], beta.t[:], Em.t[:], MUL, [beta.b, Em.b], [BT.b])
        Kk.tt("dve", KT.t[:], kmod.t[:], Em.t[:], MUL, [kmod.b, Em.b], [KT.b])
        Kk.tt("pool", BH.t[:], beta.t[:], Eend.t[:], MUL, [beta.b, Eend.b], [BH.b])
        Kk.tt("dve", KH.t[:], kmod.t[:], Eend.t[:], MUL, [kmod.b, Eend.b], [KH.b])
        Kk.cp("act", vb.t[:], vP, [vB], [vb.b])
        Kk.act(sqt.t[:], rP, ACT.Copy, [rB, cv.b], [sqt.b], scale=col(C_RK))
        Kk.tt("pool", rk.t[:], sqt.t[:], kmod.t[:], MUL, [sqt.b, kmod.b], [rk.b])
        bsb = PB.next()
        Kk.mm(bsb.t[:], bones.t[:], rk.t[:], [bones.b, rk.b], [bsb.b])
        Kk.tt("dve", bonus.t[:], bsb.t[:], vP, MUL, [vB], [bsb.b, bonus.b])
        yield
        srcs = [(lambda cp: AR.t[:, cp, 0, :], AR.b), (lambda cp: BH.t[:, cp * 128:(cp + 1) * 128], BH.b),
                (lambda cp: KH.t[:, cp * 128:(cp + 1) * 128], KH.b),
                (lambda cp: vb.t[:, cp * 128:(cp + 1) * 128], vb.b)]
        for q, (sf, sbuf) in enumerate(srcs):
            for cp in range(4):
                o0 = ((q % 2) * 4 + cp) * 128
                Kk.tr(PTr[q // 2].t[:, o0:o0 + 128], sf(cp), ident128.t[:], [sbuf, ident128.b], [PTr[q // 2].b])
        Kk.cp("act", TM.t[:, 0:2, :, :].rearrange("p a b c -> p (a b c)"), PTr[0].t[:], [], [PTr[0].b, TM.b])
        Kk.cp("dve", TM.t[:, 2:4, :, :].rearrange("p a b c -> p (a b c)"), PTr[1].t[:], [], [PTr[1].b, TM.b])
        yield
        for h in range(2):
            hs = slice(h * 64, (h + 1) * 64)
            for cp in range(4):
                p = h * 4 + cp
                ts_ = slice(cp * 128, (cp + 1) * 128)
                bk = PB.next()
                arf = AR.t[hs, cp, :, :].rearrange("p a b -> p (a b)")
                Kk.mm(bk.t[:, 0:256], BT.t[hs, ts_], arf, [BT.b, AR.b], [bk.b], tp=(h * 64, 0))
                Kk.mm(bk.t[:, 256:512], KT.t[hs, ts_], arf, [KT.b, AR.b], [bk.b], tp=(h * 64, 0))
                Kk.tt("dve", AM.t[:, p, 0:4, :], v3(bk.t[:], 4), m4.t[:], MUL, [m4.b], [bk.b, AM.bs[p]])
            bkn = PB.next()
            for cp in range(4):
                ts_ = slice(cp * 128, (cp + 1) * 128)
                Kk.mm(bkn.t[:, ts_], AR.t[hs, cp, 0, :], BT.t[hs, ts_], [AR.b, BT.b], [bkn.b], tp=(h * 64, 0))
            Kk.tt("dve", AM.t[:, h * 4:(h + 1) * 4, 4, :], v3(bkn.t[:], 4), mn4.t[:], MUL, [mn4.b],
                  [bkn.b] + AM.bs[h * 4:(h + 1) * 4])
        yield
        bv = PB.next()
        for p in range(8):
            Kk.mm(bv.t[:, p * 64:(p + 1) * 64], AM.t[:, p, 2, :], TM.t[:, 3, p % 4, (p // 4) * 64:(p // 4 + 1) * 64], [AM.bs[p], TM.b],
                  [bv.b])
        for h in range(2):
            wa = slice(h * 64, (h + 1) * 64)
            uv = slice((1 - h) * 64, (2 - h) * 64)
            Kk.cp("act", Y0.t[:, h * 4:(h + 1) * 4, wa], TM.t[:, 0, :, h * 64:(h + 1) * 64], [TM.b], [Y0.b])
            Kk.cp("dve", Y0.t[:, h * 4:(h + 1) * 4, uv], v3(bv.t[:], 8)[:, h * 4:(h + 1) * 4, :], [], [bv.b, Y0.b])
        yield
        Yc = Y0
        for k in range(6):
            if k == 0:
                Xk = lambda p: AM.t[:, p, 0, :]
                Zk = lambda p: AM.t[:, p, 4, :]
                XkB = lambda p: [AM.bs[p]]
                ZkB = XkB
            else:
                Xt, Zt = Xl[k % 2], Zl[k % 2]
                Xk = lambda p, Xt=Xt: Xt.t[:, p, :]
                Zk = lambda p, Zt=Zt: Zt.t[:, p, :]
                XkB = lambda p, Xt=Xt: [Xt.b]
                ZkB = lambda p, Zt=Zt: [Zt.b]
            Yn = Yl[k % 2]
            if k < 5:
                Xn_, Zn_ = Xl[(k + 1) % 2], Zl[(k + 1) % 2]
                bx = [PB.next(), PB.next()]
                for p in range(8):
                    bb = bx[p // 4]
                    Kk.mm(bb.t[:, (p % 4) * 128:(p % 4 + 1) * 128], Zk(p), Xk(p), ZkB(p) + XkB(p), [bb.b])
                for i in range(2):
                    Kk.cp("act", Xn_.t[:, i * 4:(i + 1) * 4, :], v3(bx[i].t[:], 4), [], [bx[i].b, Xn_.b])
            b01 = [PB.next(), PB.next()]
            for p in range(8):
                bb = b01[p // 4]
                Kk.mm(bb.t[:, (p % 4) * 128:(p % 4 + 1) * 128], Xk(p), Yc.t[:, p, :], XkB(p) + [Yc.b], [bb.b])
            for i in range(2):
                Kk.tt("dve", Yn.t[:, i * 4:(i + 1) * 4, :], v3(b01[i].t[:], 4), Yc.t[:, i * 4:(i + 1) * 4, :], ADD,
                      [Yc.b], [b01[i].b, Yn.b])
            if k < 4:
                bz = [PB.next(), PB.next()]
                for p in range(8):
                    bb = bz[p // 4]
                    Kk.mm(bb.t[:, (p % 4) * 128:(p % 4 + 1) * 128], Xk(p), Zk(p), ZkB(p) + XkB(p), [bb.b])
                for i in range(2):
                    Kk.cp("act" if i == 0 else "dve", Zn_.t[:, i * 4:(i + 1) * 4, :], v3(bz[i].t[:], 4),
                          [], [bz[i].b, Zn_.b])
            Yc = Yn
            yield
        XF = Yc
        bq = [PB.next(), PB.next()]
        for e_ in range(2):
            rows = slice(e_ * 64, (e_ + 1) * 64)
            bb = bq[e_]
            for cp in range(4):
                c0 = cp * 128
                for h in range(2):
                    p = h * 4 + cp
                    off = h * 64
                    wa = slice(h * 64, (h + 1) * 64)
                    uv = slice((1 - h) * 64, (2 - h) * 64)
                    Kk.mm(bb.t[off:off + 64, c0:c0 + 64], XF.t[rows, p, wa], TM.t[rows, 1, cp, h * 64:(h + 1) * 64], [XF.b, TM.b],
                          [bb.b], tp=(e_ * 64, off))
                    Kk.mm(bb.t[off:off + 64, c0 + 64:c0 + 128], TM.t[rows, 1, cp, h * 64:(h + 1) * 64], XF.t[rows, p, uv],
                          [XF.b, TM.b], [bb.b], start=True, stop=False, tp=(e_ * 64, off))
                    Kk.mm(bb.t[off:off + 64, c0 + 64:c0 + 128], TM.t[rows, 2, cp, h * 64:(h + 1) * 64], TM.t[rows, 3, cp, h * 64:(h + 1) * 64],
                          [TM.b], [bb.b], start=False, stop=True, tp=(e_ * 64, off))
        for e_ in range(2):
            for h in range(2):
                hs = slice(h * 64, (h + 1) * 64)
                Kk.cp("act", PTs.t[hs, :, e_, hs], v3(bq[e_].t[:], 4)[hs, :, 0:64], [], [bq[e_].b, PTs.b])
            Kk.cp("dve", Qs.t[:, :, e_, :], v3(bq[e_].t[:], 4)[:, :, 64:128], [], [bq[e_].b, Qs.b])
        yield
        bg, byi = PB.next(), PB.next()
        for h in range(2):
            off = h * 64
            wa = slice(h * 64, (h + 1) * 64)
            uv = slice((1 - h) * 64, (2 - h) * 64)
            for cp in range(4):
                p = h * 4 + cp
                ts_ = slice(cp * 128, (cp + 1) * 128)
                Kk.mm(bg.t[off:off + 64, ts_], XF.t[:, p, wa], AM.t[:, p, 1, :], [XF.b, AM.bs[p]], [bg.b],
                      tp=(0, off))
                Kk.mm(byi.t[off:off + 64, ts_], XF.t[:, p, uv], AM.t[:, p, 1, :], [XF.b, AM.bs[p]], [byi.b],
                      start=True, stop=False, tp=(0, off))
                Kk.mm(byi.t[off:off + 64, ts_], TM.t[:, 3, cp, h * 64:(h + 1) * 64], AM.t[:, p, 3, :], [TM.b, AM.bs[p]], [byi.b],
                      start=False, stop=True, tp=(0, off))
        Kk.tt("dve", v3(GT.t[:], 4), v3(bg.t[:], 4), AR.t[:, :, 1, :], ADD, [AR.b], [bg.b, GT.b])
        Kk.cp("act", YI.t[:], byi.t[:], [], [byi.b, YI.b])
        yield
        TB_ = Tbf[par]
        TBn = Tbf[1 - par]
        for c in range(8):
            gc = n * 8 + c
            Tc, Tn = T32[gc % 2], T32[(gc + 1) % 2]
            bst = PB.next()
            Kk.mm(bst.t[:, 0:64], PTs.t[:, c // 2, c % 2, :], TB_.t[:, c, :], [PTs.b, TB_.bs[c]], [bst.b])
            Kk.stt(tmpst.t[:], Tc.t[:], Ep.t[:, c * 64 + 63:c * 64 + 64], Qs.t[:, c // 2, c % 2, :], MUL, ADD,
                   [Tc.b, Ep.b, Qs.b], [tmpst.b])
            Kk.tt("dve", TB_.t[:, c + 1, :], tmpst.t[:], bst.t[:, 0:64], ADD, [tmpst.b], [bst.b, TB_.bs[c + 1]])
            Kk.tt("dve", Tn.t[:], tmpst.t[:], bst.t[:, 0:64], ADD, [tmpst.b], [bst.b, Tn.b])
            if c == 7:
                Kk.tt("dve", TBn.t[:, 0, :], tmpst.t[:], bst.t[:, 0:64], ADD, [tmpst.b], [bst.b, TBn.bs[0]])
            if c % 2 == 1:
                yield
        byh = [PB.next(), PB.next()]
        for h in range(2):
            off = h * 64
            hs = slice(off, off + 64)
            for c in range(8):
                cs_ = slice(c * 64, (c + 1) * 64)
                Kk.mm(byh[h].t[hs, cs_], TB_.t[hs, c, :], GT.t[hs, cs_], [TB_.bs[c], GT.b], [byh[h].b],
                      tp=(off, off))
        for h in range(2):
            hs = slice(h * 64, (h + 1) * 64)
            Kk.tt("dve", y32.t[hs, :], byh[h].t[hs, :], YI.t[hs, :], ADD, [YI.b], [byh[h].b, y32.b])
        Kk.cp("act", yb.t[:], y32.t[:], [y32.b], [yb.b])
        yield
        bm = PB.next()
        Kk.mm(bm.t[:], bmean.t[:], yb.t[:], [bmean.b, yb.b], [bm.b])
        Kk.tt("dve", yc.t[:], y32.t[:], bm.t[:], SUB, [y32.b], [bm.b, yc.b])
        Kk.tt("pool", yc2.t[:], yc.t[:], yc.t[:], MUL, [yc.b], [yc2.b])
        bvv = PB.next()
        Kk.mm(bvv.t[:], bmean.t[:], yc2.t[:], [bmean.b, yc2.b], [bvv.b])
        Kk.act(sd.t[:], bvv.t[:], ACT.Ln, [], [bvv.b, sd.b], bias=LN_X_EPS)
        Kk.act(sd.t[:], sd.t[:], ACT.Exp, [], [sd.b], scale=-0.5)
        Kk.tt("dve", yc.t[:], yc.t[:], sd.t[:], MUL, [sd.b], [yc.b])
        Kk.act(yc.t[:], yc.t[:], ACT.Identity, [cv.b], [yc.b], bias=col(C_LNB), scale=col(C_LNW))
        yo_ = yout[n % 2]
        Kk.tt("dve", yo_.t[:], yc.t[:], bonus.t[:], ADD, [yc.b, bonus.b], [yo_.b])
        bpq = nblk // 4
        jq = n // bpq
        Kk.dma("sp", ag_in[jq, :, (n % bpq) * TB:(n % bpq + 1) * TB], yo_.t[:], [yo_.b], [dr["ag_in_b"][jq]],
               sem=Kk.tsem(yo_))
        if (n + 1) % bpq == 0 and dr["do_ag"]:
            S.dma("pool", lambda e, jq=jq: e.collective_compute("AllGather", ALU.bypass,
                                                               replica_groups=[[0, 1, 2, 3], [4, 5, 6, 7]],
                                                               ins=[dr["ag_in"][jq]], outs=[dr["ag_out"][jq]]),
                  [dr["ag_in_b"][jq]], [dr["ag_out_b"][jq]], inc=1)
        yield

    gens = [block(n) for n in range(nblk)]
    live = {}
    t = 0
    nxt = 0
    while nxt < nblk or live:
        if nxt < nblk and t == nxt * depth_off:
            live[nxt] = gens[nxt]
            nxt += 1
        for n in sorted(live):
            try:
                next(live[n])
            except StopIteration:
                del live[n]
        t += 1


W2_TILES = ([(0, 128), (128, 32)] + [(672 + t * 128, 128) for t in range(4)] + [(1184 + t * 128, 128) for t in range(4)]
            + [(160 + t * 128, 128) for t in range(4)] + [(1696 + t * 128, 128) for t in range(8)]
            + [(2720 + t * 128, 128) for t in range(8)])
CT8 = [(t * 128, 128) for t in range(8)]


def weight_plan():
    plan = [("w2", "w2", 0, D, W2_TILES), ("wpa", "wpa", 0, 512, CT8), ("wpb", "wpb", 0, 512, CT8),
            ("wmix", "wmix", 0, D, CT8), ("wq", "wq", 0, D, CT8), ("wk", "wkv", 0, D, CT8), ("wxo", "wxo", 0, D, CT8)]
    for half in range(2):
        plan.append(("wup%d" % half, "wup", 0, D, [(half * 2048 + t * 128, 128) for t in range(16)]))
        plan.append(("wdn%d" % half, "wdn", half * 2048, 2048, CT8))
    return plan


def convert_weights(nc, S, Kk, dr):
    jobs = []
    wb = {}
    for name, src, r0, nr, tiles in weight_plan():
        kcs = nr // 128
        t = nc.dram_tensor("wb_" + name, [len(tiles), 128, kcs, 128], BF16).ap()
        b = S.buf("wb_" + name)
        sem = S.new_sem()
        wb[name] = (t, b, kcs, tiles)
        for ti, (c0, w) in enumerate(tiles):
            def job(tick=None, t=t, b=b, sem=sem, src=src, r0=r0, nr=nr, c0=c0, w=w, ti=ti, kcs=kcs):
                Kk.dma("pool", t[ti, :, :, 0:w],
                       dr[src][r0:r0 + nr, c0:c0 + w].rearrange("(kc p) n -> p kc n", p=128),
                       [tick] if tick is not None else [], [b], sem=sem)
            jobs.append(job)
    t = nc.dram_tensor("wb_wv", [4, 128, KC, 256], BF16).ap()
    b = S.buf("wb_wv")
    sem = S.new_sem()
    wb["wv"] = (t, b, KC, None)
    for nd in range(4):
        def job(tick=None, t=t, b=b, sem=sem, nd=nd):
            Kk.dma("pool", t[nd], dr["wkv"][:, D + nd * 256:D + (nd + 1) * 256].rearrange("(kc p) n -> p kc n", p=128),
                   [tick] if tick is not None else [], [b], sem=sem)
        jobs.append(job)
    dr["wb"] = wb
    return jobs


def phase2(S, Kk, cst, cv, dr, SEQ):
    TOK2 = SEQ // 4
    nblk = TOK2 // TB
    ones_bf = cst["ones_bf"]
    MUL, ADD, SUB, MAX = ALU.mult, ALU.add, ALU.subtract, ALU.max
    PB = Banks(S, 7)
    HB = T(S, "hb", [128, 512], F32, psum=True)

    def col(i):
        return cv.t[:, i:i + 1]

    NSLAB = 8
    slabs = [T(S, "slab%d" % i, [128, 16, 128], BF16) for i in range(NSLAB)]
    for sl in slabs:
        sl.sem = S.new_sem()
    sl_i = [0]

    def next_slab():
        sl = slabs[sl_i[0] % NSLAB]
        sl_i[0] += 1
        return sl

    def linear(wname, rhs, evac, halo=None):
        wt, wbuf, kcs, col_tiles = dr["wb"][wname]
        for ti, (c0, w) in enumerate(col_tiles):
            sl = next_slab()
            Kk.dma("sp", sl.t[:, 0:kcs, 0:w], wt[ti, :, :, 0:w], [wbuf], [sl.b], sem=sl.sem)
            bank = PB.next()
            for kc in range(kcs):
                ap, b = rhs(kc)
                Kk.mm(bank.t[0:w, :], sl.t[:, kc, 0:w], ap, [sl.b, b], [bank.b], start=(kc == 0),
                      stop=(kc == kcs - 1))
            if halo is not None and halo(ti) is not None:
                hsl = halo(ti)
                for kc in range(kcs):
                    Kk.mm(HB.t[0:w, hsl], sl.t[:, kc, 0:w], xhn.t[:, kc, :], [sl.b, xhn.b], [HB.b],
                          start=(kc == 0), stop=(kc == kcs - 1))
            evac(ti, bank)

    yag = T(S, "yag", [128, 4, TOK2], BF16)
    idx = dr["idx_sb"]
    agv = dr["ag_out"].rearrange("j r t -> (j r) t")
    for g in range(4):
        S.dma("pool", lambda e, g=g: e.indirect_dma_start(out=yag.t[:, g, :], out_offset=None, in_=agv,
                                                          in_offset=bass.IndirectOffsetOnAxis(idx.t[:, g:g + 1], 0)),
              dr["ag_out_b"] + [idx.b], [yag.b])
    wlg0 = T(S, "wlg0", [128, 512], BF16)
    wlg1 = T(S, "wlg1", [32, 512], BF16)
    Kk.dma("pool", wlg0.t[:], dr["wlg"][0:128, :], [], [wlg0.b])
    Kk.dma("pool", wlg1.t[:], dr["wlg"][128:160, :], [], [wlg1.b])
    KTm = T(S, "KTm", [128, 8, N_MEM], BF16)
    Vm = T(S, "Vm", [128, 2, D], BF16)

    sqt = T(S, "sqt2", [128, TB], F32)
    rstd = T(S, "rstd2", [128, TB], F32)
    xsq = [T(S, "xsq2_%d" % i, [128, TB], BF16) for i in range(2)]
    xsq_i = [0]

    def rms_stats(src_fn, src_b, n_tok, rstd_ap, rstd_b, sqt_ap, sqt_b, bank_ap_fn):
        bank = PB.next()
        for kc in range(KC):
            t = xsq[xsq_i[0] % 2]
            xsq_i[0] += 1
            sb_ = src_b(kc) if callable(src_b) else src_b
            Kk.tt("pool", t.t[:, 0:n_tok], src_fn(kc), src_fn(kc), MUL, [sb_], [t.b])
            Kk.mm(bank.t[:, 0:n_tok], ones_bf.t[:], t.t[:, 0:n_tok], [ones_bf.b, t.b], [bank.b],
                  start=(kc == 0), stop=(kc == KC - 1))
        Kk.act(sqt_ap, bank.t[:, 0:n_tok], ACT.Ln, [], [bank.b, sqt_b], bias=RMS_EPS, scale=1.0 / D)
        Kk.act(rstd_ap, sqt_ap, ACT.Exp, [sqt_b], [rstd_b], scale=-0.5)

    mk = S.mark()
    mf = T(S, "mf", [128, KC, N_MEM], F32)
    mn = T(S, "mn", [128, KC, N_MEM], BF16)
    Kk.dma("sp", mf.t[:], dr["memT"], [], [mf.b])
    rms_stats(lambda kc: mf.t[:, kc, :], mf.b, N_MEM, rstd.t[:, 0:N_MEM], rstd.b, sqt.t[:, 0:N_MEM], sqt.b, None)
    for kc in range(KC):
        Kk.stt(mn.t[:, kc, :], mf.t[:, kc, :], col(C_GMEM + kc), rstd.t[:, 0:N_MEM], MUL, MUL,
               [mf.b, rstd.b, cv.b], [mn.b])
    def evac_k(ti, bank):
        Kk.cp("act", KTm.t[:, ti, :], bank.t[:, 0:N_MEM], [], [bank.b, KTm.b])

    wt, wbuf, kcs_, tiles_ = dr["wb"]["wk"]
    for dt_ in range(8):
        sl = next_slab()
        Kk.dma("sp", sl.t[:, 0:KC, :], wt[dt_], [wbuf], [sl.b], sem=sl.sem)
        bank = PB.next()
        for kc in range(KC):
            Kk.mm(bank.t[:, 0:N_MEM], sl.t[:, kc, :], mn.t[:, kc, :], [sl.b, mn.b], [bank.b], start=(kc == 0),
                  stop=(kc == KC - 1))
        evac_k(dt_, bank)
    for nd in range(4):
        sl = next_slab()
        slv = sl.t[:].rearrange("p a b -> p (a b)").rearrange("p (k n) -> p k n", k=KC)
        Kk.dma("sp", slv, dr["wb"]["wv"][0][nd], [dr["wb"]["wv"][1]], [sl.b], sem=sl.sem)
        for mt in range(2):
            bank = PB.next()
            for kc in range(KC):
                Kk.mm(bank.t[:, 0:256], mn.t[:, kc, mt * 128:(mt + 1) * 128], slv[:, kc, :], [sl.b, mn.b], [bank.b],
                      start=(kc == 0), stop=(kc == KC - 1))
            Kk.cp("act", Vm.t[:, mt, nd * 256:(nd + 1) * 256], bank.t[:, 0:256], [], [bank.b, Vm.b])

    X = [T(S, "X%d" % i, [128, KC, TB], F32, nb=KC) for i in range(2)]
    xn = T(S, "xn2", [128, KC, TB], BF16, nb=KC)
    xhf = T(S, "xhf", [128, KC, 2], F32)
    xhn = T(S, "xhn", [128, KC, 2], BF16)
    rstd_h = T(S, "rstd_h", [128, 2], F32)
    sqt_h = T(S, "sqt_h", [128, 2], F32)
    pgd = T(S, "pgd", [128, 2, TB + 1], F32)
    dgd = T(S, "dgd", [128, 2, TB], F32)
    sgd = T(S, "sgd", [128, 2, TB], BF16)
    cbt = T(S, "cbt", [128, 4, TB], F32, nb=4)
    cct = T(S, "cct", [128, 4, TB], F32, nb=4)
    cch = T(S, "cch", [128, 4, 2], F32)
    u = T(S, "u", [128, 4, TB + 2], F32, nb=4)
    acc = T(S, "acc", [128, TB], F32)
    outb = T(S, "outb", [128, 4, TB], BF16, nb=4)
    outa = T(S, "outa", [128, 4, TB], BF16, nb=4)
    H16 = T(S, "H16", [128, 16, TB], BF16, nb=16)
    ma = T(S, "ma", [128, 8, TB], BF16, nb=8)
    mg = T(S, "mg", [128, 8, TB], BF16, nb=8)
    tmpm = T(S, "tmpm", [128, TB], F32)
    es = [T(S, "es%d" % i, [128, TB], BF16) for i in range(4)]
    rinv = T(S, "rinv", [128, TB], F32)
    rl = [T(S, "rl%d" % i, [128, TB], BF16) for i in range(2)]

    def normalize(Xb, gcol, dst, dstbs):
        rms_stats(lambda kc: Xb.t[:, kc, :], lambda kc: Xb.bs[kc], TB, rstd.t[:], rstd.b, sqt.t[:], sqt.b, None)
        for kc in range(KC):
            Kk.stt(dst.t[:, kc, :], Xb.t[:, kc, :], col(gcol + kc), rstd.t[:], MUL, MUL, [Xb.bs[kc], rstd.b, cv.b],
                   [dstbs[kc]])

    tiles = [(0, 128), (128, 32)]
    tiles += [(672 + t * 128, 128) for t in range(4)]
    tiles += [(1184 + t * 128, 128) for t in range(4)]
    tiles += [(160 + t * 128, 128) for t in range(4)]
    tiles += [(1696 + t * 128, 128) for t in range(8)]
    tiles += [(2720 + t * 128, 128) for t in range(8)]

    for n in range(nblk):
        Xb = X[n % 2]
        Kk.dma("sp", Xb.t[:], dr["xT2"][n], [], Xb.bs, sem=Kk.tsem(Xb))
        first = (n == 0)
        if first:
            Kk.dma("sp", xhf.t[:], dr["xh"], [], [xhf.b])
            rms_stats(lambda kc: xhf.t[:, kc, :], xhf.b, 2, rstd_h.t[:], rstd_h.b, sqt_h.t[:], sqt_h.b, None)
            for kc in range(KC):
                Kk.stt(xhn.t[:, kc, :], xhf.t[:, kc, :], col(C_GMIX + kc), rstd_h.t[:], MUL, MUL,
                       [xhf.b, rstd_h.b, cv.b], [xhn.b])
        normalize(Xb, C_GMIX, xn, xn.bs)

        def evac_in(ti, bank):
            if ti < 2:
                w = 128 if ti == 0 else 32
                if first:
                    if True:
                        Kk.cp("act", pgd.t[0:w, ti, 0:1], HB.t[0:w, ti * 2 + 1:ti * 2 + 2], [], [HB.b, pgd.b])
                else:
                    Kk.cp("dve", pgd.t[0:w, ti, 0:1], pgd.t[0:w, ti, TB:TB + 1], [], [pgd.b])
                Kk.cp("act", pgd.t[0:w, ti, 1:TB + 1], bank.t[0:w, :], [], [bank.b, pgd.b])
                if ti == 1:
                    for t in range(2):
                        ww = 128 if t == 0 else 32
                        Kk.tt("dve", dgd.t[0:ww, t, :], pgd.t[0:ww, t, 0:TB], pgd.t[0:ww, t, 1:TB + 1], SUB,
                              [pgd.b], [dgd.b])
                        Kk.stt(dgd.t[0:ww, t, :], dgd.t[0:ww, t, :], cv.t[0:ww, C_MUGD + t:C_MUGD + t + 1],
                               pgd.t[0:ww, t, 1:TB + 1], MUL, ADD, [pgd.b, cv.b], [dgd.b])
                        Kk.act(sgd.t[0:ww, t, :], dgd.t[0:ww, t, :], ACT.Sigmoid, [dgd.b], [sgd.b])
                    for ot in range(4):
                        bk = PB.next()
                        Kk.mm(bk.t[:], wlg0.t[:, ot * 128:(ot + 1) * 128], sgd.t[:, 0, :], [wlg0.b, sgd.b], [bk.b],
                              start=True, stop=False)
                        Kk.mm(bk.t[:], wlg1.t[:, ot * 128:(ot + 1) * 128], sgd.t[0:32, 1, :], [wlg1.b, sgd.b],
                              [bk.b], start=False, stop=True)
                        Kk.tt("dve", outa.t[:, ot, :], bk.t[:], yag.t[:, ot, n * TB:(n + 1) * TB], MUL, [yag.b],
                              [bk.b, outa.bs[ot]])
            elif ti < 6:
                t = ti - 2
                Kk.cp("act", cct.t[:, t, :], bank.t[:], [], [bank.b, cct.bs[t]])
                if first:
                    Kk.cp("act", cch.t[:, t, :], HB.t[:, ti * 2:ti * 2 + 2], [], [HB.b, cch.b])
            elif ti < 10:
                t = ti - 6
                if first:
                    Kk.tt("dve", u.t[:, t, 0:2], HB.t[:, ti * 2:ti * 2 + 2], cch.t[:, t, :], MUL, [cch.b],
                          [HB.b, u.bs[t]])
                else:
                    Kk.cp("dve", u.t[:, t, 0:2], u.t[:, t, TB:TB + 2], [], [u.bs[t]])
                Kk.tt("dve", u.t[:, t, 2:TB + 2], bank.t[:], cct.t[:, t, :], MUL, [cct.bs[t]], [bank.b, u.bs[t]])
            elif ti < 14:
                t = ti - 10
                Kk.cp("act", cbt.t[:, t, :], bank.t[:], [], [bank.b, cbt.bs[t]])
                Kk.ts("dve", acc.t[:], u.t[:, t, 2:TB + 2], col(C_CONV + 2 * 4 + t), None, MUL, None,
                      [u.bs[t], cv.b], [acc.b])
                Kk.stt(acc.t[:], u.t[:, t, 1:TB + 1], col(C_CONV + 1 * 4 + t), acc.t[:], MUL, ADD, [u.bs[t], cv.b],
                       [acc.b])
                Kk.stt(acc.t[:], u.t[:, t, 0:TB], col(C_CONV + 0 * 4 + t), acc.t[:], MUL, ADD, [u.bs[t], cv.b],
                       [acc.b])
                Kk.tt("dve", outb.t[:, t, :], acc.t[:], cbt.t[:, t, :], MUL, [acc.b, cbt.bs[t]], [outb.bs[t]])
            elif ti < 22:
                t = ti - 14
                Kk.act(H16.t[:, t, :], bank.t[:], ACT.Sigmoid, [cv.b], [bank.b, H16.bs[t]], bias=col(C_BGA + t))
            else:
                t = ti - 22
                Kk.act(H16.t[:, 8 + t, :], bank.t[:], ACT.Sigmoid, [cv.b], [bank.b, H16.bs[8 + t]],
                       bias=col(C_BGB + t))

        def halo_fn(ti):
            if first and ti < 10:
                return slice(ti * 2, ti * 2 + 2)
            return None

        linear("w2", lambda kc: (xn.t[:, kc, :], xn.bs[kc]), evac_in, halo=halo_fn)

        ct8 = [(t * 128, 128) for t in range(8)]

        def evac_pa(ti, bank):
            Kk.tt("dve", ma.t[:, ti, :], bank.t[:], H16.t[:, ti, :], MUL, [H16.bs[ti]], [bank.b, ma.bs[ti]])

        linear("wpa", lambda kc: (outa.t[:, kc, :], outa.bs[kc]), evac_pa)

        def evac_pb(ti, bank):
            Kk.tt("dve", tmpm.t[:], bank.t[:], H16.t[:, 8 + ti, :], MUL, [H16.bs[8 + ti]], [bank.b, tmpm.b])
            Kk.tt("dve", mg.t[:, ti, :], tmpm.t[:], ma.t[:, ti, :], ADD, [tmpm.b, ma.bs[ti]], [mg.bs[ti]])

        linear("wpb", lambda kc: (outb.t[:, kc, :], outb.bs[kc]), evac_pb)

        def evac_res(ti, bank):
            Kk.tt("dve", Xb.t[:, ti, :], bank.t[:], Xb.t[:, ti, :], ADD, [], [bank.b, Xb.bs[ti]])

        linear("wmix", lambda kc: (mg.t[:, kc, :], mg.bs[kc]), evac_res)

        normalize(Xb, C_GXA, xn, xn.bs)

        def evac_q(ti, bank):
            Kk.act(H16.t[:, ti, :], bank.t[:], ACT.Copy, [], [bank.b, H16.bs[ti]], scale=1.0 / 16.0)

        linear("wq", lambda kc: (xn.t[:, kc, :], xn.bs[kc]), evac_q)
        for hd in range(4):
            ee = [es[(hd % 2) * 2], es[(hd % 2) * 2 + 1]]
            for mt in range(2):
                bk = PB.next()
                for j in range(2):
                    dc = 2 * hd + j
                    Kk.mm(bk.t[:], KTm.t[:, dc, mt * 128:(mt + 1) * 128], H16.t[:, dc, :], [KTm.b, H16.bs[dc]],
                          [bk.b], start=(j == 0), stop=(j == 1))
                Kk.act(ee[mt].t[:], bk.t[:], ACT.Exp, [], [bk.b, ee[mt].b])
            bk = PB.next()
            for mt in range(2):
                Kk.mm(bk.t[:], ones_bf.t[:], ee[mt].t[:], [ones_bf.b, ee[mt].b], [bk.b], start=(mt == 0),
                      stop=(mt == 1))
            Kk.act(rinv.t[:], bk.t[:], ACT.Ln, [], [bk.b, rinv.b])
            Kk.act(rinv.t[:], rinv.t[:], ACT.Exp, [], [rinv.b], scale=-1.0)
            for j in range(2):
                dc = 2 * hd + j
                bk = PB.next()
                for mt in range(2):
                    Kk.mm(bk.t[:], Vm.t[:, mt, dc * 128:(dc + 1) * 128], ee[mt].t[:], [Vm.b, ee[mt].b], [bk.b],
                          start=(mt == 0), stop=(mt == 1))
                Kk.tt("dve", H16.t[:, 8 + dc, :], bk.t[:], rinv.t[:], MUL, [rinv.b], [bk.b, H16.bs[8 + dc]])
        linear("wxo", lambda kc: (H16.t[:, 8 + kc, :], H16.bs[8 + kc]), evac_res)

        normalize(Xb, C_GMLP, xn, xn.bs)
        for half in range(2):
            def evac_up(ti, bank):
                r_ = rl[ti % 2]
                Kk.act(r_.t[:], bank.t[:], ACT.Relu, [], [bank.b, r_.b])
                Kk.tt("pool", H16.t[:, ti, :], r_.t[:], r_.t[:], MUL, [r_.b], [H16.bs[ti]])

            linear("wup%d" % half, lambda kc: (xn.t[:, kc, :], xn.bs[kc]), evac_up)
            linear("wdn%d" % half, lambda kc: (H16.t[:, kc, :], H16.bs[kc]), evac_res)

        rms_stats(lambda kc: Xb.t[:, kc, :], lambda kc: Xb.bs[kc], TB, rstd.t[:], rstd.b, sqt.t[:], sqt.b, None)
        for kc in range(KC):
            Kk.stt(Xb.t[:, kc, :], Xb.t[:, kc, :], col(C_GFIN + kc), rstd.t[:], MUL, MUL, [rstd.b, cv.b],
                   [Xb.bs[kc]])
        ev = Kk.dma("sp", dr["outT"][n], Xb.t[:], Xb.bs, [], sem=Kk.tsem(Xb))
        S.final_events.append(ev)


def build(SEQ, phases=("p1", "ag", "p2"), debug=False, cut=None):
    nc = bass.Bass("TRN2", target_bir_lowering=False)
    S = Sched(nc)
    Kk = K(S)
    TOK2 = SEQ // 4
    dr = {}

    def din(name, shape, dt=F32):
        dr[name] = nc.dram_tensor(name, list(shape), dt, kind="ExternalInput").ap()

    din("xT1", [SEQ // TB, 128, KC, TB])
    din("xT2", [TOK2 // TB, 128, KC, TB])
    din("xh", [128, KC, 2])
    din("memT", [128, KC, N_MEM])
    din("cvec", [128, NCV])
    din("w1", [D, 512])
    din("lwa", [128, 128])
    din("w2", [D, 3744])
    din("wlg", [160, 512])
    din("wpa", [512, D])
    din("wpb", [512, D])
    din("wmix", [D, D])
    din("wq", [D, D])
    din("wkv", [D, 2 * D])
    din("wxo", [D, D])
    din("wup", [D, DFF])
    din("wdn", [DFF, D])
    din("idx", [128, 4], U32)
    dr["ag_in"] = nc.dram_tensor("ag_in", [4, 128, TOK2], BF16).ap()
    dr["ag_out"] = nc.dram_tensor("ag_out", [4, 512, TOK2], BF16).ap()
    dr["ag_in_b"] = [S.buf("ag_in%d" % j) for j in range(4)]
    dr["do_ag"] = "ag" in phases
    dr["ag_out_b"] = [S.buf("ag_out%d" % j) for j in range(4)]
    dr["outT"] = nc.dram_tensor("outT", [TOK2 // TB, 128, KC, TB], F32, kind="ExternalOutput").ap()
    if debug:
        dr["dbg"] = nc.dram_tensor("dbg", [4, 512, TOK2], BF16, kind="ExternalOutput").ap()

    cv = T(S, "cv", [128, NCV], F32)
    Kk.dma("sp", cv.t[:], dr["cvec"], [], [cv.b])
    cst = build_consts(S, Kk, cv)
    idx_sb = T(S, "idx", [128, 4], U32)
    Kk.dma("sp", idx_sb.t[:], dr["idx"], [], [idx_sb.b])
    dr["idx_sb"] = idx_sb
    base_mark = S.mark()
    jobs = convert_weights(nc, S, Kk, dr) if "p2" in phases else []
    dr["wjobs"] = jobs
    if "p1" in phases:
        PB = Banks(S, 6)
        phase1(S, Kk, PB, cst, cv, dr, SEQ, cut)
    for j in jobs:
        j()
    del jobs[:]
    if debug:
        ev = Kk.dma("sp", dr["dbg"], dr["ag_out"], dr["ag_out_b"], [])
        S.final_events.append(ev)
    if "p2" in phases:
        S.barrier(keep=dr["ag_out_b"])
        S.release(base_mark)
        phase2(S, Kk, cst, cv, dr, SEQ)
    S.finish()
    return nc, S


def host_prep(inputs, SEQ):
    f = lambda a: np.ascontiguousarray(np.asarray(a, dtype=np.float32))
    x = f(inputs["x"])
    mem = f(inputs["mem"])
    TOK2 = SEQ // 4
    w_in = f(inputs["w_in"])[0]
    mu = f(inputs["mu_shift"])[0]
    maps = []
    for c in range(8):
        b, hp = c // 4, c % 4
        q = hp
        my = slice(hp * 128, (hp + 1) * 128)
        def blk(a):
            nb_ = a.shape[0] // TB
            return np.ascontiguousarray(a.reshape(nb_, TB, KC, 128).transpose(0, 3, 2, 1))
        lo = q * TOK2
        xT1 = blk(x[b])
        xT2 = blk(x[b, lo:lo + TOK2])
        xh = np.zeros((128, KC, 2), np.float32)
        if q > 0:
            xh[:] = x[b, lo - 2:lo].reshape(2, KC, 128).transpose(2, 1, 0)
        memT = np.ascontiguousarray(mem[b].reshape(N_MEM, KC, 128).transpose(2, 1, 0))
        cvec = np.zeros((128, NCV), np.float32)
        for base, key in ((C_GMIX, "norm_mix"), (C_GXA, "norm_xattn"), (C_GMLP, "norm_mlp"), (C_GMEM, "norm_mem")):
            cvec[:, base:base + 8] = f(inputs[key])[0].reshape(8, 128).T
        cvec[:, C_GFIN:C_GFIN + 8] = f(inputs["norm_final"]).reshape(8, 128).T
        cvec[:, C_MU1 + 0] = mu[0:512][my]
        cvec[:, C_MU1 + 1] = mu[512:1024][my]
        cvec[:, C_MU1 + 2] = mu[1024:1536][my]
        cvec[:, C_MU1 + 3] = mu[1536:1664]
        cvec[:, C_MUGD] = mu[1664:1792]
        cvec[0:32, C_MUGD + 1] = mu[1792:1824]
        cvec[:, C_W0] = f(inputs["w0"])[0][my]
        cvec[:, C_A0] = f(inputs["a0"])[0][my]
        cvec[:, C_KK] = f(inputs["k_k"])[0][my]
        cvec[:, C_KA] = f(inputs["k_a"])[0][my]
        cvec[:, C_RK] = f(inputs["r_k"])[0].reshape(512)[my]
        cvec[:, C_LNW] = f(inputs["ln_x_w"])[0][my]
        cvec[:, C_LNB] = f(inputs["ln_x_b"])[0][my]
        bg = f(inputs["b_gate"])[0]
        cvec[:, C_BGA:C_BGA + 8] = bg[0:1024].reshape(8, 128).T
        cvec[:, C_BGB:C_BGB + 8] = bg[1024:2048].reshape(8, 128).T
        cw = f(inputs["conv_w"])[0][:, 0, :]
        for j in range(3):
            cvec[:, C_CONV + j * 4:C_CONV + j * 4 + 4] = cw[j].reshape(4, 128).T
        w1 = np.concatenate([w_in[:, 0:512][:, my], w_in[:, 512:1024][:, my], w_in[:, 1024:1536][:, my],
                             w_in[:, 1536:1664]], axis=1)
        lwa = np.concatenate([f(inputs["w_lora_w"])[0][:, my], f(inputs["w_lora_a"])[0][:, my]], axis=0)
        idx = np.zeros((128, 4), np.uint32)
        for g in range(4):
            idx[:, g] = q * 512 + g * 128 + np.arange(128)
        maps.append({
            "xT1": xT1, "xT2": xT2, "xh": xh, "memT": memT, "cvec": cvec,
            "w1": np.ascontiguousarray(w1), "lwa": np.ascontiguousarray(lwa),
            "w2": np.ascontiguousarray(w_in[:, 1664:]), "wlg": f(inputs["w_lora_g"])[0],
            "wpa": f(inputs["w_proj_a"])[0], "wpb": f(inputs["w_proj_b"])[0], "wmix": f(inputs["w_out_mix"])[0],
            "wq": f(inputs["w_q"])[0], "wkv": f(inputs["w_kv"])[0], "wxo": f(inputs["w_xo"])[0],
            "wup": f(inputs["w_up"])[0], "wdn": f(inputs["w_down"])[0], "idx": idx,
        })
    return maps


def kernel(**inputs):
    SEQ = inputs["x"].shape[1]
    nc, S = build(SEQ)
    maps = host_prep(inputs, SEQ)
    res = run_bass_kernel_spmd(nc, maps, core_ids=list(range(8)))
    TOK2 = SEQ // 4
    out = np.zeros((2, SEQ, D), np.float32)
    for c in range(8):
        b, q = c // 4, c % 4
        o = np.asarray(res.results[c]["outT"])
        out[b, q * TOK2:(q + 1) * TOK2, :] = o.transpose(0, 3, 2, 1).reshape(TOK2, D)
    return out
```
